# Optimizing a Trainium2 kernel written in Bass

```python
import jax, jax.numpy as jnp
from jax import lax
import numpy as np

D_MODEL = 1024
BATCH = 4
SEQ = 8192
DEPTH = 2

GROUP_WIDTH = D_MODEL // 4
MIX_WIDTH = 4 * GROUP_WIDTH
FOX_HEADS = 4
FOX_HEAD_DIM = GROUP_WIDTH // FOX_HEADS
Q_BLOCK = 128
GLA_HEADS = 4
GLA_DV = GROUP_WIDTH // GLA_HEADS
GLA_DK = GLA_DV // 2
GLA_GATE_RANK = 16
GLA_GATE_TAU = 16.0
GLA_CHUNK = 64
MLA_HEADS = 4
MLA_NOPE_DIM = 64
MLA_ROPE_DIM = 32
MLA_V_DIM = GROUP_WIDTH // MLA_HEADS
MLA_Q_LORA = 256
MLA_KV_LORA = 128
ROPE_THETA = 10000.0
SSM_D_INNER = GROUP_WIDTH
SSM_HEAD_DIM = 64
SSM_HEADS = SSM_D_INNER // SSM_HEAD_DIM
SSM_GROUPS = 2
SSM_STATE = 128
SSM_CONV = 4
SSM_CHUNK = 128
SSM_CONV_DIM = SSM_D_INNER + 2 * SSM_GROUPS * SSM_STATE
FFN_HIDDEN = -(-8 * D_MODEL // (3 * 256)) * 256
EPS = 1e-6

IN_SPLITS = (
    FOX_HEADS * FOX_HEAD_DIM, FOX_HEADS * FOX_HEAD_DIM, FOX_HEADS * FOX_HEAD_DIM, FOX_HEADS,
    GLA_HEADS * GLA_DK, GLA_HEADS * GLA_DK, GLA_HEADS * GLA_DV, GLA_HEADS * GLA_DV, GLA_GATE_RANK,
    MLA_Q_LORA, MLA_KV_LORA, MLA_ROPE_DIM,
    SSM_D_INNER, SSM_CONV_DIM, SSM_HEADS,
)
IN_COLS = sum(IN_SPLITS)

kernel_name = "hybrid_fox_gla_mla_ssd_block"


def split_cols(t, sizes):
    out, off = [], 0
    for s in sizes:
        out.append(t[..., off:off + s])
        off += s
    return out


def rmsnorm(x, g):
    xf = x.astype(jnp.float32)
    y = xf * lax.rsqrt(jnp.mean(xf * xf, axis=-1, keepdims=True) + EPS)
    return (y * g.astype(jnp.float32)).astype(x.dtype)


def to_heads(t, h):
    b, s, _ = t.shape
    return t.reshape(b, s, h, -1).transpose(0, 2, 1, 3)


def from_heads(t):
    b, h, s, d = t.shape
    return t.transpose(0, 2, 1, 3).reshape(b, s, h * d)


def rope(t, positions):
    half = t.shape[-1] // 2
    inv = ROPE_THETA ** (-jnp.arange(half, dtype=jnp.float32) / half)
    ang = positions.astype(jnp.float32)[..., None] * inv
    ang = ang.reshape(ang.shape[:2] + (1,) * (t.ndim - 3) + (half,))
    cos, sin = jnp.cos(ang), jnp.sin(ang)
    t1, t2 = t[..., :half].astype(jnp.float32), t[..., half:].astype(jnp.float32)
    return jnp.concatenate([t1 * cos - t2 * sin, t1 * sin + t2 * cos], axis=-1).astype(t.dtype)


def causal_block_attention(q, k, v, logf_cum=None):
    b, h, s, dk = q.shape
    dv = v.shape[-1]
    nb = s // Q_BLOCK
    scale = dk ** -0.5
    qb = q.reshape(b, h, nb, Q_BLOCK, dk).transpose(2, 0, 1, 3, 4)
    idx = jnp.arange(nb)
    key_pos = jnp.arange(s)
    if logf_cum is None:
        xs = (idx, qb)
    else:
        xs = (idx, qb, logf_cum.reshape(b, h, nb, Q_BLOCK).transpose(2, 0, 1, 3))

    def one(args):
        i, q_i = args[0], args[1]
        sc = jnp.einsum('bhqd,bhkd->bhqk', q_i, k).astype(jnp.float32) * scale
        if logf_cum is not None:
            sc = sc + args[2][..., :, None] - logf_cum[..., None, :]
        qpos = i * Q_BLOCK + jnp.arange(Q_BLOCK)
        mask = key_pos[None, :] <= qpos[:, None]
        p = jax.nn.softmax(jnp.where(mask, sc, -jnp.inf), axis=-1)
        return jnp.einsum('bhqk,bhkd->bhqd', p.astype(v.dtype), v)

    out = lax.map(one, xs)
    return out.transpose(1, 2, 0, 3, 4).reshape(b, h, s, dv)


def gla_chunked(q, k, v, g):
    b, h, s, dk = q.shape
    dv = v.shape[-1]
    L = GLA_CHUNK
    n = s // L

    def chunks(t):
        return t.reshape(b, h, n, L, t.shape[-1]).transpose(2, 0, 1, 3, 4)

    causal = jnp.tril(jnp.ones((L, L), dtype=bool))

    def step(state, inp):
        q_c, k_c, v_c, g_c = inp
        G = jnp.cumsum(g_c, axis=-2)
        o_inter = jnp.einsum('bhld,bhdv->bhlv', q_c * jnp.exp(G), state)
        diff = G[:, :, :, None, :] - G[:, :, None, :, :]
        decay = jnp.exp(jnp.where(causal[:, :, None], diff, -jnp.inf))
        scores = jnp.einsum('bhid,bhjd,bhijd->bhij', q_c, k_c, decay)
        o_intra = jnp.einsum('bhij,bhjv->bhiv', scores, v_c)
        G_last = G[:, :, -1:, :]
        k_dec = k_c * jnp.exp(G_last - G)
        new_state = jnp.exp(G_last[:, :, 0, :])[..., None] * state + jnp.einsum('bhld,bhlv->bhdv', k_dec, v_c)
        return new_state, (o_inter + o_intra).astype(v.dtype)

    state0 = jnp.zeros((b, h, dk, dv), jnp.float32)
    _, o = lax.scan(step, state0, (chunks(q), chunks(k), chunks(v), chunks(g)))
    return o.transpose(1, 2, 0, 3, 4).reshape(b, h, s, dv)


def segsum(a):
    T = a.shape[-1]
    cs = jnp.cumsum(a, axis=-1)
    diff = cs[..., :, None] - cs[..., None, :]
    return jnp.where(jnp.tril(jnp.ones((T, T), dtype=bool)), diff, -jnp.inf)


def ssd_chunked(X, a, Bh, Ch):
    b, s, h, p = X.shape
    n = Bh.shape[-1]
    L = SSM_CHUNK
    c = s // L
    X = X.reshape(b, c, L, h, p)
    Bh = Bh.reshape(b, c, L, h, n)
    Ch = Ch.reshape(b, c, L, h, n)
    a = a.reshape(b, c, L, h).transpose(0, 3, 1, 2)
    a_cs = jnp.cumsum(a, axis=-1)
    Lmat = jnp.exp(segsum(a))
    scores = jnp.einsum('bclhn,bcshn->bhcls', Ch, Bh) * Lmat
    y_diag = jnp.einsum('bhcls,bcshp->bclhp', scores, X)
    decay_states = jnp.exp(a_cs[..., -1:] - a_cs)
    states = jnp.einsum('bclhn,bhcl,bclhp->bchpn', Bh, decay_states, X)
    states = jnp.concatenate([jnp.zeros_like(states[:, :1]), states], axis=1)
    chunk_decay = jnp.exp(segsum(jnp.pad(a_cs[..., -1], ((0, 0), (0, 0), (1, 0)))))
    states = jnp.einsum('bhzc,bchpn->bzhpn', chunk_decay, states)[:, :-1]
    y_off = jnp.einsum('bclhn,bchpn,bhcl->bclhp', Ch, states, jnp.exp(a_cs))
    return (y_diag + y_off).reshape(b, s, h, p)


def causal_depthwise_conv(x, w, bias):
    K, C = w.shape
    y = lax.conv_general_dilated(x, w[:, None, :], window_strides=(1,), padding=[(K - 1, 0)],
                                 dimension_numbers=('NWC', 'WIO', 'NWC'), feature_group_count=C)
    return y + bias


def fox_mixer(q, k, v, f_logit, f_bias):
    logf = jax.nn.log_sigmoid(f_logit.astype(jnp.float32) + f_bias.astype(jnp.float32))
    F = jnp.cumsum(logf, axis=1).transpose(0, 2, 1)
    o = causal_block_attention(to_heads(q, FOX_HEADS), to_heads(k, FOX_HEADS), to_heads(v, FOX_HEADS), F)
    return from_heads(o)


def gla_mixer(q, k, v, r, gate_lr, w2, b2, out_norm):
    b, s, _ = q.shape
    g = jax.nn.log_sigmoid((gate_lr @ w2 + b2).astype(jnp.float32)) / GLA_GATE_TAU
    o = gla_chunked(to_heads(q, GLA_HEADS) * (GLA_DK ** -0.5), to_heads(k, GLA_HEADS),
                    to_heads(v, GLA_HEADS), to_heads(g, GLA_HEADS))
    o = rmsnorm(o.transpose(0, 2, 1, 3), out_norm) * jax.nn.silu(r.reshape(b, s, GLA_HEADS, GLA_DV))
    return o.reshape(b, s, GLA_HEADS * GLA_DV).astype(q.dtype)


def mla_mixer(c_q, c_kv, k_rope, positions, q_norm, w_uq, kv_norm, w_ukv):
    b, s, _ = c_q.shape
    q = (rmsnorm(c_q, q_norm) @ w_uq).reshape(b, s, MLA_HEADS, MLA_NOPE_DIM + MLA_ROPE_DIM)
    q = jnp.concatenate([q[..., :MLA_NOPE_DIM], rope(q[..., MLA_NOPE_DIM:], positions)], axis=-1)
    kv = (rmsnorm(c_kv, kv_norm) @ w_ukv).reshape(b, s, MLA_HEADS, MLA_NOPE_DIM + MLA_V_DIM)
    kr = jnp.broadcast_to(rope(k_rope, positions)[:, :, None, :], (b, s, MLA_HEADS, MLA_ROPE_DIM))
    k = jnp.concatenate([kv[..., :MLA_NOPE_DIM], kr.astype(kv.dtype)], axis=-1)
    v = kv[..., MLA_NOPE_DIM:]
    o = causal_block_attention(q.transpose(0, 2, 1, 3), k.transpose(0, 2, 1, 3), v.transpose(0, 2, 1, 3))
    return from_heads(o)


def mamba2_mixer(z, xbc, dt_raw, conv_w, conv_b, dt_bias, A_log, D_skip, norm_w):
    b, s, _ = z.shape
    xbc = jax.nn.silu(causal_depthwise_conv(xbc, conv_w, conv_b))
    xs, Bm, Cm = split_cols(xbc, (SSM_D_INNER, SSM_GROUPS * SSM_STATE, SSM_GROUPS * SSM_STATE))
    xs = xs.reshape(b, s, SSM_HEADS, SSM_HEAD_DIM)
    rep = SSM_HEADS // SSM_GROUPS
    Bh = jnp.repeat(Bm.reshape(b, s, SSM_GROUPS, SSM_STATE), rep, axis=2)
    Ch = jnp.repeat(Cm.reshape(b, s, SSM_GROUPS, SSM_STATE), rep, axis=2)
    dt = jax.nn.softplus(dt_raw.astype(jnp.float32) + dt_bias.astype(jnp.float32))
    A = -jnp.exp(A_log.astype(jnp.float32))
    y = ssd_chunked(xs * dt[..., None], A * dt, Bh, Ch)
    y = y + D_skip.astype(jnp.float32)[:, None] * xs
    y = (y.reshape(b, s, SSM_D_INNER) * jax.nn.silu(z)).reshape(b, s, SSM_GROUPS, SSM_D_INNER // SSM_GROUPS)
    y = rmsnorm(y, norm_w.reshape(SSM_GROUPS, -1))
    return y.reshape(b, s, SSM_D_INNER).astype(z.dtype)


def setup_inputs(seed: int = 0) -> dict:
    key = jax.random.key(seed)
    ks = jax.random.split(key, 32)
    f32 = jnp.float32

    def nrm(k, shape, scale):
        return jax.random.normal(k, shape, f32) * scale

    def gain(k, shape):
        return 1.0 + 0.02 * jax.random.normal(k, shape, f32)

    x = jax.random.normal(ks[0], (BATCH, SEQ, D_MODEL), f32)
    offset = jax.random.randint(ks[1], (BATCH, 1), 0, 4096, dtype=jnp.int32)
    positions = offset + jnp.arange(SEQ, dtype=jnp.int32)[None, :]
    dt0 = jnp.exp(jax.random.uniform(ks[14], (DEPTH, SSM_HEADS), f32, np.log(1e-3), np.log(1e-1)))
    return {
        "x": x,
        "positions": positions,
        "norm1": gain(ks[2], (DEPTH, D_MODEL)),
        "w_in": nrm(ks[3], (DEPTH, D_MODEL, IN_COLS), D_MODEL ** -0.5),
        "fox_f_bias": jax.random.uniform(ks[4], (DEPTH, FOX_HEADS), f32, 1.0, 5.0),
        "gla_gate_w2": nrm(ks[5], (DEPTH, GLA_GATE_RANK, GLA_HEADS * GLA_DK), GLA_GATE_RANK ** -0.5),
        "gla_gate_b": nrm(ks[6], (DEPTH, GLA_HEADS * GLA_DK), 0.1),
        "gla_out_norm": gain(ks[7], (DEPTH, GLA_DV)),
        "mla_q_norm": gain(ks[8], (DEPTH, MLA_Q_LORA)),
        "mla_w_uq": nrm(ks[9], (DEPTH, MLA_Q_LORA, MLA_HEADS * (MLA_NOPE_DIM + MLA_ROPE_DIM)), MLA_Q_LORA ** -0.5),
        "mla_kv_norm": gain(ks[10], (DEPTH, MLA_KV_LORA)),
        "mla_w_ukv": nrm(ks[11], (DEPTH, MLA_KV_LORA, MLA_HEADS * (MLA_NOPE_DIM + MLA_V_DIM)), MLA_KV_LORA ** -0.5),
        "ssm_conv_w": nrm(ks[12], (DEPTH, SSM_CONV, SSM_CONV_DIM), SSM_CONV ** -0.5),
        "ssm_conv_b": nrm(ks[13], (DEPTH, SSM_CONV_DIM), 0.02),
        "ssm_dt_bias": dt0 + jnp.log(-jnp.expm1(-dt0)),
        "ssm_A_log": jnp.log(jax.random.uniform(ks[15], (DEPTH, SSM_HEADS), f32, 1.0, 16.0)),
        "ssm_D": gain(ks[16], (DEPTH, SSM_HEADS)),
        "ssm_norm": gain(ks[17], (DEPTH, SSM_D_INNER)),
        "w_out": nrm(ks[18], (DEPTH, MIX_WIDTH, D_MODEL), MIX_WIDTH ** -0.5),
        "norm2": gain(ks[19], (DEPTH, D_MODEL)),
        "w_gate": nrm(ks[20], (DEPTH, D_MODEL, FFN_HIDDEN), D_MODEL ** -0.5),
        "w_up": nrm(ks[21], (DEPTH, D_MODEL, FFN_HIDDEN), D_MODEL ** -0.5),
        "w_down": nrm(ks[22], (DEPTH, FFN_HIDDEN, D_MODEL), FFN_HIDDEN ** -0.5),
        "final_norm": gain(ks[23], (D_MODEL,)),
    }


def reference(x, positions, norm1, w_in, fox_f_bias, gla_gate_w2, gla_gate_b, gla_out_norm,
              mla_q_norm, mla_w_uq, mla_kv_norm, mla_w_ukv, ssm_conv_w, ssm_conv_b, ssm_dt_bias,
              ssm_A_log, ssm_D, ssm_norm, w_out, norm2, w_gate, w_up, w_down, final_norm):
    for l in range(DEPTH):
        h = rmsnorm(x, norm1[l])
        (fq, fk, fv, ff, gq, gk, gv, gr, gg, mcq, mckv, mkr, sz, sxbc, sdt) = split_cols(h @ w_in[l], IN_SPLITS)
        y_a = fox_mixer(fq, fk, fv, ff, fox_f_bias[l])
        y_b = gla_mixer(gq, gk, gv, gr, gg, gla_gate_w2[l], gla_gate_b[l], gla_out_norm[l])
        y_c = mla_mixer(mcq, mckv, mkr, positions, mla_q_norm[l], mla_w_uq[l], mla_kv_norm[l], mla_w_ukv[l])
        y_d = mamba2_mixer(sz, sxbc, sdt, ssm_conv_w[l], ssm_conv_b[l], ssm_dt_bias[l], ssm_A_log[l],
                           ssm_D[l], ssm_norm[l])
        mix = jnp.concatenate([y_a, y_b, y_c, y_d], axis=-1)
        x = x + (mix @ w_out[l]).astype(x.dtype)
        h2 = rmsnorm(x, norm2[l])
        x = x + ((jax.nn.silu(h2 @ w_gate[l]) * (h2 @ w_up[l])) @ w_down[l]).astype(x.dtype)
    return rmsnorm(x, final_norm)
```

```python
import contextlib, math
import numpy as np
import ml_dtypes
import concourse.bass as bass
import concourse.mybir as mybir
from concourse.bass_utils import run_bass_kernel_spmd


F32 = mybir.dt.float32
BF16 = mybir.dt.bfloat16
I32 = mybir.dt.int32
AF = mybir.ActivationFunctionType
ALU = mybir.AluOpType
AX = mybir.AxisListType


class Prog:
    COMPUTE = ("pe", "act", "dve", "pool")
    NDMASEM = 8

    def __init__(self, nc, stack, dma_queues=("sp", "pool")):
        self.nc = nc
        self._stack = stack
        self.eng = {"pe": nc.tensor, "act": nc.scalar, "dve": nc.vector,
                    "pool": nc.gpsimd, "sp": nc.sync}
        self.streams = {e: [] for e in self.eng}
        self.csem = {e: stack.enter_context(nc.semaphore("c_" + e)) for e in self.COMPUTE}
        self.ccount = {e: 0 for e in self.COMPUTE}
        self.dsem = {q: [stack.enter_context(nc.semaphore("d_%s%d" % (q, i)))
                         for i in range(self.NDMASEM)] for q in dma_queues}
        self.dcount = {q: 0 for q in dma_queues}
        self.waited = {e: {} for e in self.eng}
        self.semobj = {}
        self.last_w = {}
        self.readers = {}
        self.nwaits = 0
        self.nops = 0
        self.pending = {e: False for e in self.COMPUTE}
        self.dma_rr = 0
        self.dma_queues = list(dma_queues)

    def _sid(self, sem):
        i = id(sem)
        self.semobj[i] = sem
        return i

    def _deps(self, reads, writes):
        deps = []
        for k in reads:
            w = self.last_w.get(k)
            if w is not None:
                deps.append(w)
        for k in writes:
            w = self.last_w.get(k)
            if w is not None:
                deps.append(w)
            deps.extend(self.readers.get(k, ()))
        return deps

    def _emit_waits(self, e, deps, own=None):
        wl = []
        wd = self.waited[e]
        best = {}
        own_sid = id(self.csem[e]) if e in self.csem else None
        for (sid, val) in deps:
            if sid == own_sid and val > self.ccount[e]:
                continue
            if wd.get(sid, 0) >= val:
                continue
            if best.get(sid, 0) < val:
                best[sid] = val
        for sid, val in best.items():
            wd[sid] = val
            wl.append((self.semobj[sid], val))
        return wl

    def op(self, e, fn, reads=(), writes=(), inc=True):
        assert e in self.COMPUTE
        pr = [k for k in reads if isinstance(k, str) and (k.startswith("ps") or k.startswith("fps"))]
        if pr:
            writes = list(writes) + pr
        deps = self._deps(reads, writes)
        sem = self.csem[e]
        sid = self._sid(sem)
        wl = self._emit_waits(e, deps)
        if inc:
            self.ccount[e] += 1
            val = self.ccount[e]
        else:
            val = self.ccount[e] + 1
        self.pending[e] = not inc
        self.nwaits += len(wl)
        self.nops += 1

        def run(eng, wl=wl, fn=fn, sem=sem, inc=inc):
            for (s, v) in wl:
                eng.wait_ge(s, v)
            ins = fn(eng)
            if inc:
                ins.then_inc(sem, 1)
        self.streams[e].append(run)
        tok = (sid, val)
        for k in reads:
            self.readers.setdefault(k, []).append(tok)
        for k in writes:
            self.last_w[k] = tok
            self.readers[k] = []
        return tok

    def dma(self, out, in_, reads=(), writes=(), q=None, **kw):
        if q is None:
            q = self.dma_queues[self.dma_rr % len(self.dma_queues)]
            self.dma_rr += 1
        n = self.dcount[q]
        self.dcount[q] += 1
        sem = self.dsem[q][n % self.NDMASEM]
        sid = self._sid(sem)
        val = 16 * (n // self.NDMASEM + 1)
        deps = self._deps(reads, writes)
        if n >= self.NDMASEM:
            deps.append((sid, val - 16))
        wl = self._emit_waits(q, deps)
        self.nwaits += len(wl)
        self.nops += 1

        def run(eng, wl=wl, sem=sem, out=out, in_=in_, kw=kw):
            for (s, v) in wl:
                eng.wait_ge(s, v)
            eng.dma_start(out=out, in_=in_, **kw).then_inc(sem, 16)
        self.streams[q].append(run)
        tok = (sid, val)
        for k in reads:
            self.readers.setdefault(k, []).append(tok)
        for k in writes:
            self.last_w[k] = tok
            self.readers[k] = []
        return tok

    def collective(self, kind, ins, outs, reads=(), writes=(), groups=None, **kw):
        q = "pool"
        if not hasattr(self, "ccsem"):
            self.ccsem = self._stack.enter_context(self.nc.semaphore("cc_sem"))
            self.cccount = 0
        sem = self.ccsem
        sid = self._sid(sem)
        self.cccount += 1
        val = self.cccount
        deps = self._deps(reads, writes)
        if val > 1:
            deps.append((sid, val - 1))
        wl = self._emit_waits(q, deps)
        self.nwaits += len(wl)
        self.nops += 1
        op = ALU.bypass if kind in ("AllGather", "AllToAll") else ALU.add

        def run(eng, wl=wl, sem=sem):
            for (s, v) in wl:
                eng.wait_ge(s, v)
            eng.collective_compute(kind, op, replica_groups=groups, ins=[a.opt() for a in ins], outs=[a.opt() for a in outs], **kw).then_inc(sem, 1)
        self.streams[q].append(run)
        tok = (sid, val)
        for k in reads:
            self.readers.setdefault(k, []).append(tok)
        for k in writes:
            self.last_w[k] = tok
            self.readers[k] = []
        return tok

    def finish(self, final_keys):
        deps = []
        for k in final_keys:
            w = self.last_w.get(k)
            if w is not None:
                deps.append(w)
        wl = self._emit_waits("sp", deps)

        def run(eng, wl=wl):
            for (s, v) in wl:
                eng.wait_ge(s, v)
        self.streams["sp"].append(run)

    def emit(self):
        nc = self.nc
        assert not any(self.pending.values()), self.pending
        streams = self.streams
        self.streams = {e: [] for e in self.eng}
        with nc.Block() as block:
            @block.sync
            def _(eng):
                for r in streams["sp"]:
                    r(eng)

            @block.tensor
            def _(eng):
                for r in streams["pe"]:
                    r(eng)

            @block.scalar
            def _(eng):
                for r in streams["act"]:
                    r(eng)

            @block.vector
            def _(eng):
                for r in streams["dve"]:
                    r(eng)

            @block.gpsimd
            def _(eng):
                for r in streams["pool"]:
                    r(eng)


NFM = 944
NTM = 962
NW = NFM + NTM
FM = {"fq": (0, 128), "fk": (128, 128), "fg": (256, 48), "sx": (304, 128), "sB": (432, 128),
      "sC": (560, 128), "gq": (688, 128), "kr2": (816, 128)}
TM = {"t1": (944, 448), "t2": (1392, 384), "t3": (1776, 130)}
NCOL = 28
NROW = 326
TWO_PI = 2.0 * math.pi
C1 = 6.28125
C2 = TWO_PI - C1


class Rot:
    def __init__(self, items):
        self.items = items
        self.i = 0

    def next(self):
        it = self.items[self.i % len(self.items)]
        self.i += 1
        return it


def emit_mixer(nc, st, P, NT, xin, pos, w_all, colc, rowc, cmat, w2, wuq, wukv, mixT, pfx="", xin_fn=None, out_key="mixT_out",
               tile_major=False, tile_hook=None):
    S = NT * 512
    NCH = NT * 4
    _n = [0]

    def sb(shape, dt, name=None):
        _n[0] += 1
        return st.enter_context(nc.sbuf_tensor("%s%s_%d" % (pfx, name or "t", _n[0]), shape, dt))

    def ps(shape, dt, name=None):
        _n[0] += 1
        return st.enter_context(nc.psum_tensor("%s%s_%d" % (pfx, name or "p", _n[0]), shape, dt))

    bg = {"q": [], "stride": 1, "burst": 1, "cnt": 0, "busy": False, "open": False, "ticks": 0, "done": [], "side_done": {}}

    def pump(n):
        bg["busy"] = True
        for _ in range(n):
            if not bg["q"]:
                break
            try:
                next(bg["q"][0])
            except StopIteration:
                bg["q"].pop(0)
        bg["busy"] = False

    def flush_to(keep):
        while len(bg["q"]) > keep:
            n0 = len(bg["q"])
            while len(bg["q"]) == n0:
                pump(1000)
        if tile_major and tile_hook is not None:
            while bg["done"]:
                tile_hook(bg["done"].pop(0))

    def tick():
        if bg["busy"]:
            return
        bg["ticks"] += 1
        if not bg["q"] or bg["open"]:
            return
        bg["cnt"] += 1
        if bg["cnt"] % bg["stride"] == 0:
            pump(bg["burst"])


    def A(out, in_, func, r, w, **kw):
        tk = P.op("act", lambda e: e.activation(out=out, in_=in_, func=func, **kw), reads=r, writes=w)
        tick()
        return tk

    def V(name, r, w, **kw):
        tk = P.op("dve", lambda e: getattr(e, name)(**kw), reads=r, writes=w)
        tick()
        return tk

    def G(name, r, w, **kw):
        return P.op("pool", lambda e: getattr(e, name)(**kw), reads=r, writes=w)

    def MM(out, lhsT, rhs, r, w, start=True, stop=True, inc=None):
        inc = stop if inc is None else inc
        tk = P.op("pe", lambda e: e.matmul(out=out, lhsT=lhsT, rhs=rhs, start=start, stop=stop), reads=r, writes=w, inc=inc)
        if not bg["busy"]:
            bg["open"] = not inc
            if inc:
                tick()
        return tk

    colc_sb = sb([128, NCOL], F32, "colc"); rowc_sb = sb([128, NROW], F32, "rowc")
    cm = sb([128, 640], F32, "cm"); identb = sb([128, 128], BF16, "identb")
    P.dma(colc_sb[:], colc, writes=["colc"])
    P.dma(rowc_sb[:], rowc.partition_broadcast(128), writes=["rowc"])
    P.dma(cm[:], cmat, writes=["cm"])
    identf = cm[:, 0:128]; tri = cm[:, 128:256]; su = cm[:, 256:384]; madd = cm[:, 384:512]; ones = cm[:, 512:640]
    V("tensor_copy", ["cm"], ["identb"], out=identb[:], in_=identf)
    mask2 = sb([128, 2, 128], F32, "mask2")
    V("tensor_copy", ["cm"], ["mask2"], out=mask2[:, 0, :], in_=tri)
    V("tensor_copy", ["cm", "mask2"], ["mask2"], out=mask2[:, 1, :], in_=tri)

    def TR(out, in_, r, w, inc=True):
        tk = P.op("pe", lambda e: e.transpose(out=out, in_=in_, identity=identb[:]), reads=list(r) + ["identb"], writes=w, inc=inc)
        if not bg["busy"]:
            bg["open"] = not inc
            if inc:
                tick()
        return tk

    g1 = colc_sb[:, 0:8]; qn = colc_sb[:, 8:10]; kvn = colc_sb[:, 10:11]
    cw = colc_sb[:, 11:23]; cb = colc_sb[:, 23:26]; inv = colc_sb[:, 26:27]; fb = colc_sb[:, 27:28]
    b2r = rowc_sb[:, 0:64]; gn2 = rowc_sb[:, 64:192]; dtb = rowc_sb[:, 192:194]; alog = rowc_sb[:, 194:196]
    dsk = rowc_sb[:, 196:198]; snorm = rowc_sb[:, 198:326]

    small = sb([128, 16], F32, "small")
    negfb = small[:, 0:1]; arep = small[:, 2:4]
    V("tensor_scalar", ["colc"], ["negfb"], out=negfb, in0=fb, scalar1=-1.0, scalar2=None, op0=ALU.mult)
    A(arep, alog, AF.Exp, ["rowc"], ["arep"])
    V("tensor_scalar", ["arep"], ["arep"], out=arep, in0=arep, scalar1=-1.0, scalar2=None, op0=ALU.mult)

    xt = Rot([(sb([128, 1024], F32, "xt"), "xt%d" % i) for i in range(2)])
    Wb = sb([128, 8, NW], BF16, "Wb")
    HALF = NW // 2
    for kc in range(8):
        for hf in range(2):
            t_, k_ = xt.next()
            P.dma(t_[:, 0:HALF], w_all[kc * 128:(kc + 1) * 128, hf * HALF:(hf + 1) * HALF], writes=[k_])
            V("tensor_scalar", [k_, "colc"], ["Wb"], out=Wb[:, kc, hf * HALF:(hf + 1) * HALF], in0=t_[:, 0:HALF],
              scalar1=g1[:, kc:kc + 1], scalar2=None, op0=ALU.mult)
    w2sb = sb([48, 64], F32, "w2sb")
    P.dma(w2sb[32:48, :], w2, writes=["w2sb"])
    wuqb = sb([128, 2, 512], BF16, "wuqb")
    t_, k_ = xt.next()
    wuqf = t_[:, 0:1024].rearrange("p (c n) -> p c n", c=2)
    P.dma(wuqf, wuq.rearrange("(c p) n -> p c n", p=128), writes=[k_])
    for c in range(2):
        V("tensor_scalar", [k_, "colc"], ["wuqb"], out=wuqb[:, c, :], in0=wuqf[:, c, :], scalar1=qn[:, c:c + 1],
          scalar2=None, op0=ALU.mult)
    wukvb = sb([128, 256], BF16, "wukvb")
    t_, k_ = xt.next()
    P.dma(t_[:, 0:256], wukv, writes=[k_])
    V("tensor_scalar", [k_, "colc"], ["wukvb"], out=wukvb[:], in0=t_[:, 0:256], scalar1=kvn, scalar2=None, op0=ALU.mult)

    SA = max(S, 8192)
    KTf = [sb([128, SA], BF16, "KTf%d" % h) for h in range(2)]
    KTm = [sb([128, SA], BF16, "KTm%d" % h) for h in range(2)]

    def carve(tile, r0, nr, slot, dt=F32, n=512):
        return tile[r0:r0 + nr, slot * 1024:slot * 1024 + (n * (4 if dt != BF16 else 2)) // 2].bitcast(dt) if dt != BF16 \
            else tile[r0:r0 + nr, slot * 1024:slot * 1024 + n]

    VAf = sb([128, NCH + 1, 2, 65], BF16, "VAf"); VAm = sb([128, NCH + 1, 2, 65], BF16, "VAm")
    for h in range(2):
        G("memset", [], ["KTf%d_init" % h], ap=KTf[h][64:70, :], constant=1.0)
    G("memset", [], ["VAf_init"], ap=VAf[:], constant=1.0)
    G("memset", [], ["VAm_init"], ap=VAm[:], constant=1.0)
    QTf2 = [[sb([70, 512], BF16, "QTf%d_%d" % (p_, h)) for h in range(2)] for p_ in range(2)]
    QTm2 = [[sb([96, 512], BF16, "QTm%d_%d" % (p_, h)) for h in range(2)] for p_ in range(2)]
    for p_ in range(2):
        for h in range(2):
            G("memset", [], ["QTf%d_%d" % (p_, h)], ap=QTf2[p_][h][64:70, :], constant=1.0)

    gen = Rot([(ps([128, 512], F32, "g%d" % i), "psg%d" % i) for i in range(3)])
    sps = Rot([(ps([128, 512], F32, "s%d" % i), "pss%d" % i) for i in range(2)])
    acc = Rot([(ps([128, 512], F32, "a%d" % i), "psa%d" % i) for i in range(1)])
    ptr = Rot([(ps([128, 8, 128], BF16, "t%d" % i), "pst%d" % i) for i in range(2)])

    junk = sb([128, 384], BF16, "junk")
    hb = sb([128, 1024], BF16, "hb")
    hT = sb([128, 8, 512], BF16, "hT")
    st8 = sb([128, 8], F32, "st8")
    PT = Rot([(sb([128, 512], BF16, "PT"), "PT%d" % i) for i in range(3)])
    osb = sb([65, 512], F32, "osb"); rcp = osb
    stmp = sb([128, 512], F32, "stmp")

    def silu_to(out_ap, x_ap, tmp_ap, rkeys, wkey, tmpkey):
        A(tmp_ap, x_ap, AF.Exp, list(rkeys), [tmpkey], scale=-1.0)
        V("tensor_scalar", [tmpkey], [tmpkey], out=tmp_ap, in0=tmp_ap, scalar1=1.0, scalar2=None, op0=ALU.add)
        V("reciprocal", [tmpkey], [tmpkey], out=tmp_ap, in_=tmp_ap)
        V("tensor_tensor", list(rkeys) + [tmpkey], [wkey], out=out_ap, in0=x_ap, in1=tmp_ap, op=ALU.mult)

    mixt = Rot([(sb([128, 4, 512], BF16, "mixt"), "mixt%d" % i) for i in range(2)])
    fe = carve(KTf[0], 96, 2, 0); fsp = carve(KTf[0], 96, 2, 1)
    Fc = Rot([(carve(KTf[0], 96, 2, 2 + i), "Fc%d" % i) for i in range(2)])
    fr1 = carve(KTf[0], 96, 2, 4); fr2 = carve(KTf[0], 96, 2, 5)
    ones2 = carve(KTf[0], 96, 2, 6)
    Fzero = carve(KTf[0], 96, 2, 7)
    Fq = KTf[1][96:98, 0:1536].rearrange("p (j n) -> p j n", j=3)
    Fk = KTf[1][96:98, 1536:3072].rearrange("p (j n) -> p j n", j=3)
    G("memset", [], ["Fzero"], ap=Fzero, constant=0.0)
    G("memset", [], ["ones2"], ap=ones2, constant=1.0)
    cn = sb([128, 384], BF16, "cn"); cnT = sb([128, 3, 512], BF16, "cnT")
    posi = carve(KTm[0], 96, 32, 0, I32); rti = posi
    ang = carve(KTm[0], 96, 32, 1); rtmp = carve(KTm[0], 96, 32, 2)
    cs = carve(KTm[0], 96, 32, 3); sn = carve(KTm[0], 96, 32, 4)
    krA = carve(KTm[1], 96, 32, 0); krB = carve(KTm[1], 96, 32, 1)
    gateT = sb([48, 512], F32, "gateT"); gqT = sb([64, 512], F32, "gqT")
    gt = sb([128, 64], F32, "gt"); gsp = sb([128, 64], F32, "gsp")
    eGT = sb([64, 128], F32, "eGT"); enG = sb([128, 64], F32, "enG")
    qtl = sb([64, 128], BF16, "qtl"); ktl = sb([128, 64], BF16, "ktl"); ktlT = sb([64, 128], BF16, "ktlT")
    Am = sb([128, 2, 128], BF16, "Am"); vb = sb([128, 128], BF16, "vb")
    Sst = sb([64, 128], F32, "Sst"); Sbf = sb([64, 128], BF16, "Sbf")
    sr = sb([128, 128], F32, "sr"); gtmp = sb([128, 128], F32, "gtmp"); yb = sb([128, 128], BF16, "yb")
    G("memset", [], ["Sst"], ap=Sst[:], constant=0.0)
    G("memset", [], ["Sbf"], ap=Sbf[:], constant=0.0)
    cin = [sb([128, 515], F32, "cin%d" % g) for g in range(3)]
    for g in range(3):
        G("memset", [], ["cin%d" % g], ap=cin[g][:, 0:3], constant=0.0)
    cacc = sb([128, 512], F32, "cacc")
    fT = [sb([128, 512], BF16, "fT%d" % g) for g in range(3)]
    xs_tok = sb([128, 128], BF16, "xs_tok"); B_tok = sb([128, 128], BF16, "B_tok")
    dtr = sb([128, 8], F32, "dtr"); dt8 = sb([128, 8], F32, "dt8"); a8 = sb([128, 8], F32, "a8")
    ea8 = sb([128, 8], F32, "ea8"); eal8 = sb([128, 8], F32, "eal8"); ds8 = sb([128, 8], F32, "ds8"); wl8 = sb([128, 8], F32, "wl8")
    Lh = sb([128, 128], F32, "Lh"); Dexp = sb([128, 128], F32, "Dexp"); t1s = sb([128, 128], F32, "t1s")
    MT = sb([128, 2, 128], BF16, "MT"); Xd = sb([128, 128], BF16, "Xd")
    Hs = sb([128, 128], F32, "Hs"); Hbf = sb([128, 128], BF16, "Hbf")
    ytmp = sb([128, 128], F32, "ytmp"); szt = sb([128, 4, 128], F32, "szt"); yd = sb([128, 128], BF16, "yd")
    G("memset", [], ["Hs"], ap=Hs[:], constant=0.0)
    G("memset", [], ["Hbf"], ap=Hbf[:], constant=0.0)

    def rstd_from(ssap, n, dim, key):
        A(ssap, ssap, AF.Ln, [key], [key], scale=1.0 / dim, bias=1e-6)
        A(ssap, ssap, AF.Exp, [key], [key], scale=-0.5)

    def attention(QT, qkey, KT, kkey, VA, vkey, h, t, scale, nrows, out_ap, out_keys):
        nk = 4 * t + 4
        a_t, a_k = acc.next()
        pend = None

        def issue_qk(j):
            i = j - 4 * t
            c0 = 0 if i < 0 else i * 128
            s_t, s_k = sps.next()
            MM(s_t[:, c0:512], KT[0:nrows, j * 128:(j + 1) * 128], QT[0:nrows, c0:512],
               [kkey(j // 4), qkey], [s_k])
            return (j, c0, s_t, s_k)

        nxt = issue_qk(0)
        for j in range(nk):
            cur = nxt
            if j + 1 < nk:
                nxt = issue_qk(j + 1)
            (_, c0, s_t, s_k) = cur
            if j >= 4 * t:
                V("tensor_tensor", [s_k, "cm"], [s_k], out=s_t[:, c0:c0 + 128], in0=s_t[:, c0:c0 + 128], in1=madd, op=ALU.add)
            p_t, p_k = PT.next()
            A(p_t[:, c0:512], s_t[:, c0:512], AF.Exp, [s_k], [p_k], scale=scale)
            vflat = VA[:, j:j + 2, :, :].rearrange("p c h d -> p (c h d)")[:, h * 65:h * 65 + 128]
            MM(a_t[:, c0:512], vflat, p_t[:, c0:512], [vkey(j // 4), vkey(min((j + 1) // 4, t)), p_k], [a_k], start=(j == 0), stop=(j == nk - 1))
            yield
        V("reciprocal", [a_k, "osb"], ["rcp"], out=rcp[64:65, :], in_=a_t[64:65, :])
        A(osb[0:64, :], a_t[0:64, :], AF.Copy, [a_k], ["osb"])
        b_t, b_k = sps.next()
        MM(b_t[0:64, :], ones[64:65, 0:64], rcp[64:65, :], ["cm", "rcp"], [b_k])
        V("tensor_tensor", ["osb", b_k, "rcp"], out_keys + ["rcp"], out=out_ap, in0=osb[0:64, :], in1=b_t[0:64, :], op=ALU.mult)
        yield

    def tile_attention(t, par, mx, mxk):
        for h in range(2):
            yield from attention(QTf2[par][h], "QTf%d_%d" % (par, h), KTf[h], lambda tt, h=h: ("KTf", h, tt), VAf, lambda tt: ("VAf", tt), h, t,
                                 1.0, 70, mx[64 * h:64 * h + 64, 0, :], [mxk])
        for h in range(2):
            yield from attention(QTm2[par][h], "QTm%d_%d" % (par, h), KTm[h], lambda tt, h=h: ("KTm", h, tt), VAm, lambda tt: ("VAm", tt), h, t,
                                 sc_mla, 96, mx[64 * h:64 * h + 64, 2, :], [mxk])
        while not bg["side_done"].get(t):
            yield
        if tile_major:
            P.dma(mixT[t].rearrange("(m p) s -> p m s", p=128), mx[:], reads=[mxk], writes=[(out_key, t)], q="pool")
        else:
            P.dma(mixT.rearrange("(m p) s -> p m s", p=128)[:, :, t * 512:(t + 1) * 512], mx[:], reads=[mxk], writes=[out_key], q="sp")
        bg["done"].append(t)
        yield

    Fprev = (Fzero, "Fzero")
    isq_fox = 0.125
    sc_mla = 96.0 ** -0.5
    sc_gq = 32.0 ** -0.5

    for t in range(NT):
        c512 = slice(t * 512, (t + 1) * 512)
        mx, mxk = mixt.next()
        par = t % 2
        QTf = QTf2[par]; QTm = QTm2[par]
        ticks0 = bg["ticks"]
        for s in range(4):
            x_t, x_k = xt.next()
            r0 = t * 512 + s * 128
            if xin_fn is None:
                P.dma(x_t[:], xin[r0:r0 + 128, :], writes=[x_k], q="sp")
            else:
                xap, xkeys = xin_fn(t, s)
                P.dma(x_t[:], xap, reads=list(xkeys), writes=[x_k], q="sp")
            A(hb[:], x_t[:], AF.Square, [x_k], ["hb", "st8a"], accum_out=st8[:, 0:1])
            rstd_from(st8[:, 0:1], 1, 1024.0, "st8a")
            V("tensor_scalar", [x_k, "st8a"], ["hb"], out=hb[:], in0=x_t[:], scalar1=st8[:, 0:1], scalar2=None, op0=ALU.mult)
            p_t, p_k = ptr.next()
            for kc in range(8):
                TR(p_t[:, kc, :], hb[:, kc * 128:(kc + 1) * 128], ["hb"], [p_k], inc=(kc == 7))
            A(hT[:, :, s * 128:(s + 1) * 128], p_t[:], AF.Copy, [p_k], ["hT"])

        def fm_group(name):
            off, M = FM[name]
            g_t, g_k = gen.next()
            for kc in range(8):
                MM(g_t[0:M, :], Wb[:, kc, off:off + M], hT[:, kc, :], ["Wb", "hT"], [g_k], start=(kc == 0), stop=(kc == 7))
            return g_t, g_k

        def tm_group(name, s):
            off, N = TM[name]
            g_t, g_k = gen.next()
            for kc in range(8):
                MM(g_t[:, 0:N], hT[:, kc, s * 128:(s + 1) * 128], Wb[:, kc, off:off + N], ["Wb", "hT"], [g_k], start=(kc == 0), stop=(kc == 7))
            return g_t, g_k

        g_t, g_k = fm_group("fq")
        for h in range(2):
            A(QTf[h][0:64, :], g_t[64 * h:64 * h + 64, :], AF.Copy, [g_k], ["QTf%d_%d" % (par, h)], scale=isq_fox)
        g_t, g_k = fm_group("fk")
        for h in range(2):
            V("tensor_copy", [g_k, "KTf%d_init" % h], [("KTf", h, t)], out=KTf[h][0:64, c512], in_=g_t[64 * h:64 * h + 64, :])
        g_t, g_k = fm_group("fg")
        A(fe, g_t[0:2, :], AF.Exp, [g_k, "negfb"], ["fe"], scale=-1.0, bias=negfb[0:2, :])
        A(gateT[32:48, :], g_t[32:48, :], AF.Copy, [g_k], ["gateT"])
        A(fsp, fe, AF.Ln, ["fe"], ["fsp"], bias=1.0)
        F_t, F_k = Fc.next()
        V("tensor_tensor_scan", ["ones2", "fsp", Fprev[1]], [F_k], out=F_t, data0=ones2, data1=fsp,
          initial=Fprev[0][:, 0:1] if Fprev[1] == "Fzero" else Fprev[0][:, 511:512], op0=ALU.mult, op1=ALU.subtract)
        Fprev = (F_t, F_k)
        V("tensor_copy", [F_k], ["Fq"], out=Fq[:, 0, :], in_=F_t)
        V("tensor_tensor", [F_k, "Fq"], ["fr1"], out=fr1, in0=F_t, in1=Fq[:, 0, :], op=ALU.subtract)
        V("tensor_copy", ["fr1", "Fq"], ["Fq"], out=Fq[:, 1, :], in_=fr1)
        V("tensor_tensor", ["fr1", "Fq"], ["fr2"], out=fr2, in0=fr1, in1=Fq[:, 1, :], op=ALU.subtract)
        V("tensor_copy", ["fr2", "Fq"], ["Fq"], out=Fq[:, 2, :], in_=fr2)
        V("tensor_scalar", ["Fq"], ["Fk"], out=Fk, in0=Fq, scalar1=-1.0, scalar2=None, op0=ALU.mult)
        for h in range(2):
            for j in range(3):
                P.dma(QTf[h][64 + j:65 + j, :], Fq[h:h + 1, j, :], reads=["Fq", "QTf%d_%d" % (par, h)], writes=["QTf%d_%d" % (par, h)], q="sp")
                P.dma(KTf[h][67 + j:68 + j, c512], Fk[h:h + 1, j, :], reads=["Fk", "KTf%d_init" % h, ("KTf", h, t)],
                      writes=[("KTf", h, t)], q="sp")
        for g, name in enumerate(("sx", "sB", "sC")):
            g_t, g_k = fm_group(name)
            ck = "cin%d" % g
            A(cin[g][:, 3:515], g_t[:, :], AF.Copy, [g_k, ck], [ck])

        def conv_silu(g):
            ck = "cin%d" % g
            V("tensor_scalar", [ck, "colc"], ["cacc"], out=cacc[:], in0=cin[g][:, 0:512], scalar1=cw[:, 4 * g:4 * g + 1],
              scalar2=cb[:, g:g + 1], op0=ALU.mult, op1=ALU.add)
            for k in range(1, 4):
                V("scalar_tensor_tensor", [ck, "colc", "cacc"], ["cacc"], out=cacc[:], in0=cin[g][:, k:k + 512],
                  scalar=cw[:, 4 * g + k:4 * g + k + 1], in1=cacc[:], op0=ALU.mult, op1=ALU.add)
            V("tensor_copy", [ck], [ck], out=cin[g][:, 0:3], in_=cin[g][:, 512:515])
            silu_to(fT[g][:], cacc[:], stmp[:], ["cacc"], "fT%d" % g, "stmp")

        R = slice(96, 128)
        RO = slice(64, 96)
        P.dma(posi, pos[0:1, c512].partition_broadcast(32), reads=["posi"], writes=["posi"], q="sp")
        V("tensor_copy", ["posi"], ["ang"], out=ang, in_=posi)
        V("tensor_scalar", ["ang", "colc"], ["ang"], out=ang, in0=ang, scalar1=inv[R, :], scalar2=None, op0=ALU.mult)
        for which, tab, tkey in ((0, sn, "sn"), (1, cs, "cs")):
            shift = 0.0 if which == 0 else math.pi / 2
            V("tensor_scalar", ["ang"], ["rtmp"], out=rtmp, in0=ang, scalar1=shift, scalar2=1.0 / TWO_PI, op0=ALU.add, op1=ALU.mult)
            V("tensor_copy", ["rtmp", "posi"], ["posi"], out=rti, in_=rtmp)
            V("tensor_copy", ["posi"], ["rtmp"], out=rtmp, in_=rti)
            V("scalar_tensor_tensor", ["rtmp", "ang"], [tkey], out=tab, in0=rtmp, scalar=-C1, in1=ang, op0=ALU.mult, op1=ALU.add)
            V("scalar_tensor_tensor", ["rtmp", tkey], [tkey], out=tab, in0=rtmp, scalar=-C2, in1=tab, op0=ALU.mult, op1=ALU.add)
            V("tensor_scalar", [tkey], [tkey], out=tab, in0=tab, scalar1=shift, scalar2=3.14159, op0=ALU.add, op1=ALU.min)
            V("tensor_scalar", [tkey], [tkey], out=tab, in0=tab, scalar1=-3.14159, scalar2=None, op0=ALU.max)
            A(tab, tab, AF.Sin, [tkey], [tkey])
        g_t, g_k = fm_group("gq")
        A(gqT[:], g_t[0:64, :], AF.Copy, [g_k], ["gqT"], scale=sc_gq)
        V("tensor_tensor", [g_k, "cs"], ["krA"], out=krA, in0=g_t[R, :], in1=cs, op=ALU.mult)
        g_t, g_k = fm_group("kr2")
        V("tensor_tensor", [g_k, "sn"], ["krB"], out=krB, in0=g_t[R, :], in1=sn, op=ALU.mult)
        for h in range(2):
            V("tensor_tensor", ["krA", "krB"], [("KTm", h, t)], out=KTm[h][RO, c512], in0=krA, in1=krB, op=ALU.add)

        for s in range(4):
            ch = 4 * t + s
            g_t, g_k = tm_group("t1", s)
            for h in range(2):
                V("tensor_copy", [g_k, "VAf_init"], [("VAf", t)], out=VAf[:, ch, h, 0:64], in_=g_t[:, 64 * h:64 * h + 64])
            cc = slice(s * 128, (s + 1) * 128)
            gp_t, gp_k = gen.next()
            MM(gp_t[:, 0:64], gateT[32:48, cc], w2sb[32:48, :], ["gateT", "w2sb"], [gp_k])
            V("tensor_tensor", [gp_k, "rowc"], ["gt"], out=gt[:], in0=gp_t[:, 0:64], in1=b2r, op=ALU.add)
            A(gt[:], gt[:], AF.Exp, ["gt"], ["gt"], scale=-1.0)
            A(gsp[:], gt[:], AF.Ln, ["gt"], ["gsp"], bias=1.0)
            MM(gp_t[:, 128:192], tri, gsp[:], ["cm", "gsp"], [gp_k])
            MM(gp_t[0:64, 256:384], gsp[:], tri, ["cm", "gsp"], [gp_k])
            A(eGT[:], gp_t[0:64, 256:384], AF.Exp, [gp_k], ["eGT"], scale=-1.0 / 16)
            A(enG[:], gp_t[:, 128:192], AF.Exp, [gp_k], ["enG"], scale=1.0 / 16)
            V("tensor_tensor", ["gqT", "eGT"], ["qtl"], out=qtl[:], in0=gqT[:, cc], in1=eGT[:], op=ALU.mult)
            V("tensor_tensor", [g_k, "enG"], ["ktl"], out=ktl[:], in0=g_t[:, 128:192], in1=enG[:], op=ALU.mult)
            A(vb[:], g_t[:, 192:320], AF.Copy, [g_k], ["vb"])
            silu_to(sr[:], g_t[:, 320:448], sr[:], [g_k], "sr", "sr")
            p_t, p_k = ptr.next()
            TR(p_t[0:64, 0, :], ktl[:], ["ktl"], [p_k])
            A(ktlT[:], p_t[0:64, 0, :], AF.Copy, [p_k], ["ktlT"])
            a_t, a_k = gen.next()
            for h in range(2):
                hs = slice(32 * h, 32 * h + 32)
                MM(a_t[:, 128 * h:128 * h + 128], ktlT[hs, :], qtl[hs, :], ["ktlT", "qtl"], [a_k])
            V("tensor_tensor", [a_k, "mask2"], ["Am"], out=Am[:], in0=a_t[:, 0:256].rearrange("p (h n) -> p h n", h=2), in1=mask2[:], op=ALU.mult)
            for h in range(2):
                hs = slice(32 * h, 32 * h + 32)
                vs = slice(64 * h, 64 * h + 64)
                MM(a_t[:, 256 + 64 * h:256 + 64 * h + 64], Am[:, h, :], vb[:, vs], ["Am", "vb"], [a_k], start=True, stop=False)
                MM(a_t[:, 256 + 64 * h:256 + 64 * h + 64], qtl[hs, :], Sbf[hs, vs], ["qtl", "Sbf"], [a_k], start=False, stop=True)
            MM(gp_t[0:64, 384:512], ktl[:], vb[:], ["ktl", "vb"], [gp_k])
            V("tensor_scalar", ["Sst", "eGT"], ["Sst"], out=Sst[:], in0=Sst[:], scalar1=eGT[:, 127:128], scalar2=None, op0=ALU.mult)
            V("scalar_tensor_tensor", [gp_k, "eGT", "Sst"], ["Sst"], out=Sst[:], in0=gp_t[0:64, 384:512], scalar=eGT[:, 127:128],
              in1=Sst[:], op0=ALU.mult, op1=ALU.add)
            V("tensor_copy", ["Sst"], ["Sbf"], out=Sbf[:], in_=Sst[:])
            for h in range(2):
                A(gtmp[:, 64 * h:64 * h + 64], a_t[:, 256 + 64 * h:256 + 64 * h + 64], AF.Square, [a_k], ["gtmp", "st8g"],
                  accum_out=st8[:, 2 + h:3 + h])
            rstd_from(st8[:, 2:4], 2, 64.0, "st8g")
            for h in range(2):
                V("scalar_tensor_tensor", [a_k, "st8g", "rowc"], ["gtmp"], out=gtmp[:, 64 * h:64 * h + 64],
                  in0=a_t[:, 256 + 64 * h:256 + 64 * h + 64], scalar=st8[:, 2 + h:3 + h], in1=gn2[:, 64 * h:64 * h + 64],
                  op0=ALU.mult, op1=ALU.mult)
            V("tensor_tensor", ["gtmp", "sr"], ["yb"], out=yb[:], in0=gtmp[:], in1=sr[:], op=ALU.mult)
            p_t, p_k = ptr.next()
            TR(p_t[:, 0, :], yb[:], ["yb"], [p_k])
            A(mx[:, 1, cc], p_t[:, 0, :], AF.Copy, [p_k], [mxk])

            if s < 3:
                conv_silu(s)

            g_t, g_k = tm_group("t2", s)
            A(junk[:, 0:256], g_t[:, 0:256], AF.Square, [g_k], ["junk", "st8m"], accum_out=st8[:, 4:5])
            A(junk[:, 256:384], g_t[:, 256:384], AF.Square, [g_k], ["junk", "st8m"], accum_out=st8[:, 5:6])
            A(st8[:, 4:5], st8[:, 4:5], AF.Ln, ["st8m"], ["st8m"], scale=1.0 / 256, bias=1e-6)
            A(st8[:, 5:6], st8[:, 5:6], AF.Ln, ["st8m"], ["st8m"], scale=1.0 / 128, bias=1e-6)
            A(st8[:, 4:6], st8[:, 4:6], AF.Exp, ["st8m"], ["st8m"], scale=-0.5)
            V("tensor_scalar", [g_k, "st8m"], ["cn"], out=cn[:, 0:256], in0=g_t[:, 0:256], scalar1=st8[:, 4:5], scalar2=None, op0=ALU.mult)
            V("tensor_scalar", [g_k, "st8m"], ["cn"], out=cn[:, 256:384], in0=g_t[:, 256:384], scalar1=st8[:, 5:6], scalar2=None, op0=ALU.mult)
            p_t, p_k = ptr.next()
            for c in range(3):
                TR(p_t[:, c, :], cn[:, c * 128:(c + 1) * 128], ["cn"], [p_k], inc=(c == 2))
            A(cnT[:, :, cc], p_t[:, 0:3, :], AF.Copy, [p_k], ["cnT"])
            v_t, v_k = gen.next()
            p2_t, p2_k = p_t, p_k
            MM(v_t[:, 0:128], cnT[:, 2, cc], wukvb[:, 128:256], ["cnT", "wukvb"], [v_k])
            for h in range(2):
                V("tensor_copy", [v_k, "VAm_init"], [("VAm", t)], out=VAm[:, ch, h, 0:64], in_=v_t[:, 64 * h:64 * h + 64])

            g_t, g_k = tm_group("t3", s)
            silu_to(szt[:, s, :], g_t[:, 0:128], szt[:, s, :], [g_k], "szt", "szt")
            V("tensor_copy", [g_k], ["dtr"], out=dtr[:, 2 * s:2 * s + 2], in_=g_t[:, 128:130])

        for h in range(2):
            qa_t, qa_k = gen.next()
            for c in range(2):
                MM(qa_t[:, :], wuqb[:, c, 256 * h:256 * h + 128], cnT[:, c, :], ["wuqb", "cnT"], [qa_k], start=(c == 0), stop=(c == 1))
            qb_t, qb_k = gen.next()
            for c in range(2):
                MM(qb_t[:, :], wuqb[:, c, 256 * h + 128:256 * h + 256], cnT[:, c, :], ["wuqb", "cnT"], [qb_k], start=(c == 0), stop=(c == 1))
            qk = "QTm%d_%d" % (par, h)
            A(QTm[h][0:64, :], qa_t[0:64, :], AF.Copy, [qa_k], [qk])
            V("tensor_tensor", [qa_k, "cs"], ["krA"], out=krA, in0=qa_t[R, :], in1=cs, op=ALU.mult)
            V("tensor_tensor", [qb_k, "sn"], ["krB"], out=krB, in0=qb_t[R, :], in1=sn, op=ALU.mult)
            V("tensor_tensor", ["krA", "krB"], [qk], out=QTm[h][RO, :], in0=krA, in1=krB, op=ALU.add)
        kn_t, kn_k = gen.next()
        MM(kn_t[:, :], wukvb[:, 0:128], cnT[:, 2, :], ["wukvb", "cnT"], [kn_k])
        for h in range(2):
            A(KTm[h][0:64, c512], kn_t[64 * h:64 * h + 64, :], AF.Copy, [kn_k], [("KTm", h, t)])

        units = 16 * (t + 1) + 9
        bg["q"].append(tile_attention(t, par, mx, mxk))
        budget = bg.get("nticks", 900) if t < NT - 1 else max(1, int(0.35 * bg.get("nticks", 900)))
        bg["stride"] = max(1, budget // units)
        bg["burst"] = max(1, -(-units // budget))
        bg["cnt"] = 0

        V("tensor_tensor", ["dtr", "rowc"], ["dt8"], out=dt8[:].rearrange("p (s h) -> p s h", h=2),
          in0=dtr[:].rearrange("p (s h) -> p s h", h=2), in1=dtb.unsqueeze(1).to_broadcast([128, 4, 2]), op=ALU.add)
        A(dt8[:], dt8[:], AF.Exp, ["dt8"], ["dt8"])
        A(dt8[:], dt8[:], AF.Ln, ["dt8"], ["dt8"], bias=1.0)
        V("tensor_tensor", ["dt8", "arep"], ["a8"], out=a8[:].rearrange("p (s h) -> p s h", h=2),
          in0=dt8[:].rearrange("p (s h) -> p s h", h=2), in1=arep.unsqueeze(1).to_broadcast([128, 4, 2]), op=ALU.mult)
        sc_t, sc_k = gen.next()
        MM(sc_t[:, 0:8], tri, a8[:], ["cm", "a8"], [sc_k])
        MM(sc_t[:, 8:16], ones, a8[:], ["cm", "a8"], [sc_k])
        A(ea8[:], sc_t[:, 0:8], AF.Exp, [sc_k], ["ea8"])
        A(eal8[:], sc_t[:, 8:16], AF.Exp, [sc_k], ["eal8"])
        A(wl8[:], sc_t[:, 0:8], AF.Copy, [sc_k], ["wl8"])
        V("tensor_tensor", [sc_k, "wl8"], ["ds8"], out=ds8[:], in0=sc_t[:, 8:16], in1=wl8[:], op=ALU.subtract)
        A(ds8[:], ds8[:], AF.Exp, ["ds8"], ["ds8"])
        V("tensor_tensor", ["ds8", "dt8"], ["wl8"], out=wl8[:], in0=ds8[:], in1=dt8[:], op=ALU.mult)

        for s in range(4):
            cc = slice(s * 128, (s + 1) * 128)
            p_t, p_k = ptr.next()
            TR(p_t[:, 0, :], fT[0][:, cc], ["fT0"], [p_k])
            TR(p_t[:, 1, :], fT[1][:, cc], ["fT1"], [p_k])
            A(xs_tok[:], p_t[:, 0, :], AF.Copy, [p_k], ["xs_tok"])
            A(B_tok[:], p_t[:, 1, :], AF.Copy, [p_k], ["B_tok"])
            cb_t, cb_k = gen.next()
            MM(cb_t[:, 0:128], fT[1][:, cc], fT[2][:, cc], ["fT1", "fT2"], [cb_k])
            e_t, e_k = gen.next()
            for h in range(2):
                col = 2 * s + h
                V("tensor_scalar", ["cm", "a8"], ["Lh"], out=Lh[:], in0=su, scalar1=a8[:, col:col + 1], scalar2=None, op0=ALU.mult)
                MM(e_t[:, 128 * h:128 * h + 128], Lh[:], tri, ["Lh", "cm"], [e_k])
                A(Dexp[:], e_t[:, 128 * h:128 * h + 128], AF.Exp, [e_k], ["Dexp"])
                V("tensor_tensor", [cb_k, "Dexp"], ["t1s"], out=t1s[:], in0=cb_t[:, 0:128], in1=Dexp[:], op=ALU.mult)
                V("scalar_tensor_tensor", ["t1s", "dt8", "cm"], ["MT"], out=MT[:, h, :], in0=t1s[:], scalar=dt8[:, col:col + 1], in1=tri,
                  op0=ALU.mult, op1=ALU.mult)
                MM(e_t[:, 256 + 64 * h:256 + 64 * h + 64], MT[:, h, :], xs_tok[:, 64 * h:64 * h + 64], ["MT", "xs_tok"], [e_k])
            MM(e_t[:, 384:512], fT[2][:, cc], Hbf[:], ["fT2", "Hbf"], [e_k])
            for h in range(2):
                col = 2 * s + h
                V("tensor_scalar", ["xs_tok", "wl8"], ["Xd"], out=Xd[:, 64 * h:64 * h + 64], in0=xs_tok[:, 64 * h:64 * h + 64],
                  scalar1=wl8[:, col:col + 1], scalar2=None, op0=ALU.mult)
            MM(cb_t[:, 128:256], B_tok[:], Xd[:], ["B_tok", "Xd"], [cb_k])
            for h in range(2):
                col = 2 * s + h
                hs = slice(64 * h, 64 * h + 64)
                V("tensor_scalar", [e_k, "ea8"], ["ytmp"], out=ytmp[:, hs], in0=e_t[:, 384 + 64 * h:384 + 64 * h + 64],
                  scalar1=ea8[:, col:col + 1], scalar2=None, op0=ALU.mult)
                V("tensor_tensor", ["ytmp", e_k], ["ytmp"], out=ytmp[:, hs], in0=ytmp[:, hs], in1=e_t[:, 256 + 64 * h:256 + 64 * h + 64], op=ALU.add)
                V("scalar_tensor_tensor", ["xs_tok", "rowc", "ytmp"], ["ytmp"], out=ytmp[:, hs], in0=xs_tok[:, hs], scalar=dsk[:, h:h + 1],
                  in1=ytmp[:, hs], op0=ALU.mult, op1=ALU.add)
                V("scalar_tensor_tensor", ["Hs", "eal8", cb_k], ["Hs"], out=Hs[:, hs], in0=Hs[:, hs], scalar=eal8[:, col:col + 1],
                  in1=cb_t[:, 128 + 64 * h:128 + 64 * h + 64], op0=ALU.mult, op1=ALU.add)
            V("tensor_copy", ["Hs"], ["Hbf"], out=Hbf[:], in_=Hs[:])
            V("tensor_tensor", ["ytmp", "szt"], ["ytmp"], out=ytmp[:], in0=ytmp[:], in1=szt[:, s, :], op=ALU.mult)
            A(junk[:, 0:128], ytmp[:], AF.Square, ["ytmp"], ["junk", "st8s"], accum_out=st8[:, 6:7])
            rstd_from(st8[:, 6:7], 1, 128.0, "st8s")
            V("scalar_tensor_tensor", ["ytmp", "st8s", "rowc"], ["yd"], out=yd[:], in0=ytmp[:], scalar=st8[:, 6:7], in1=snorm,
              op0=ALU.mult, op1=ALU.mult)
            p_t, p_k = ptr.next()
            TR(p_t[:, 0, :], yd[:], ["yd"], [p_k])
            A(mx[:, 3, cc], p_t[:, 0, :], AF.Copy, [p_k], [mxk])

        bg["side_done"][t] = True
        flush_to(1)
        bg["nticks"] = max(1, bg["ticks"] - ticks0)
    flush_to(0)
    return [out_key]


FH = 2816
NFC = 22


def emit_ffn(nc, st, P, NTOK, final, mixTin, xres, wo, wg, wu, wd, colc2, fnorm, xo, pfx="f", mix_all=None, sel=None,
             xres_keys=(), mix_key=None, out_key="xo_out", tile_hook=None, g2row=None):
    _n = [0]

    def sb(shape, dt, name=None):
        _n[0] += 1
        return st.enter_context(nc.sbuf_tensor("%s%s_%d" % (pfx, name or "t", _n[0]), shape, dt))

    def ps(shape, dt, name=None):
        _n[0] += 1
        return st.enter_context(nc.psum_tensor("%s%s_%d" % (pfx, name or "p", _n[0]), shape, dt))

    def A(out, in_, func, r, w, **kw):
        return P.op("act", lambda e: e.activation(out=out, in_=in_, func=func, **kw), reads=r, writes=w)

    def V(name, r, w, **kw):
        return P.op("dve", lambda e: getattr(e, name)(**kw), reads=r, writes=w)

    def G(name, r, w, **kw):
        return P.op("pool", lambda e: getattr(e, name)(**kw), reads=r, writes=w)

    def MM(out, lhsT, rhs, r, w, start=True, stop=True, inc=None):
        return P.op("pe", lambda e: e.matmul(out=out, lhsT=lhsT, rhs=rhs, start=start, stop=stop), reads=r, writes=w,
                    inc=stop if inc is None else inc)

    c2 = sb([128, 8], F32, "c2"); identf = sb([128, 128], F32, "identf"); identb = sb([128, 128], BF16, "identb")
    P.dma(c2[:], colc2[:, 0:8], writes=["c2"])
    P.dma(identf[:], colc2[:, 8:136], writes=["identf"])
    V("tensor_copy", ["identf"], ["identb"], out=identb[:], in_=identf[:])

    def TR(out, in_, r, w, inc=True):
        return P.op("pe", lambda e: e.transpose(out=out, in_=in_, identity=identb[:]), reads=list(r) + ["identb"], writes=w, inc=inc)

    Wob = sb([128, 8, 1024], BF16, "Wob"); Wgb = sb([128, 8, FH], BF16, "Wgb"); Wub = sb([128, 8, FH], BF16, "Wub")
    Wdb = sb([128, NFC, 1024], BF16, "Wdb")
    g2rep = sb([128, 1024], F32, "g2rep")
    P.dma(g2rep[:], g2row.partition_broadcast(128), writes=["g2rep"], q="sp")

    def load_cast(dst, src, n, scale_ap, wkey):
        P.dma(dst, src, writes=[wkey], q="pool")

    for kc in range(8):
        load_cast(Wob[:, kc, :], wo[kc * 128:(kc + 1) * 128, :], 1024, None, "Wob")

    def load_rest_of_weights():
        for bi, (c0, n) in enumerate(((0, 1024), (1024, 1024), (2048, 768))):
            for kc in range(8):
                load_cast(Wgb[:, kc, c0:c0 + n], wg[kc * 128:(kc + 1) * 128, c0:c0 + n], n, c2[:, kc:kc + 1], ("Wgb", bi))
                load_cast(Wub[:, kc, c0:c0 + n], wu[kc * 128:(kc + 1) * 128, c0:c0 + n], n, c2[:, kc:kc + 1], ("Wub", bi))
        for fc in range(NFC):
            load_cast(Wdb[:, fc, :], wd[fc * 128:(fc + 1) * 128, :], 1024, None, ("Wdb", fc))

    gen = Rot([(ps([128, 512], F32, "g%d" % i), "fpsg%d" % i) for i in range(6)])
    ptr = Rot([(ps([128, 8, 128], BF16, "t%d" % i), "fpst%d" % i) for i in range(2)])
    TT = 512
    NS = TT // 128
    HFC = NFC // 2
    mixin = sb([128, 8, TT], BF16, "mixin"); mik = "mixin"
    x1 = sb([128, NS, 1024], F32, "x1")
    h2b = sb([128, 1024], BF16, "h2b")
    h2T = sb([128, 8, TT], BF16, "h2T"); actT = sb([128, HFC, TT], BF16, "actT")
    st8 = sb([128, 8], F32, "st8")

    def rstd_from(ssap, dim, key):
        A(ssap, ssap, AF.Ln, [key], [key], scale=1.0 / dim, bias=1e-6)
        A(ssap, ssap, AF.Exp, [key], [key], scale=-0.5)

    if mix_all is None:
        mview = mixTin.rearrange("(c p) s -> p c s", p=128)
    else:
        NTH = NTOK // 512
        selt = sb([128, 2], F32, "selt")
        P.dma(selt[:], sel, writes=["selt"], q="sp")
        candB_t = sb([128, 8, 256], BF16, "candB")
        candB = candB_t[:]
        candBk = "candB"
    fn_t = None

    def load_mix(t):
        t0 = t * TT
        if mix_all is None:
            P.dma(mixin[:], mview[:, :, t0:t0 + TT], writes=[mik], q="sp")
        else:
            for hf in range(2):
                c0 = hf * 256
                dsts = ((mixin[:, :, c0:c0 + 256], mik), (candB, candBk))
                for h in range(2):
                    for q, (dst, dk) in enumerate(dsts):
                        T = q * NTH + t
                        P.dma(dst.rearrange("p (m h) s -> p m h s", h=2)[:, :, h, :],
                              mix_all[T, h].rearrange("(m p) s -> p m s", p=128)[:, :, c0:c0 + 256],
                              reads=[(mix_key, T)], writes=[dk], q="sp")
                V("tensor_scalar", [mik, "selt"], [mik], out=mixin[:, :, c0:c0 + 256], in0=mixin[:, :, c0:c0 + 256], scalar1=selt[:, 0:1],
                  scalar2=None, op0=ALU.mult)
                V("scalar_tensor_tensor", [candBk, "selt", mik], [mik], out=mixin[:, :, c0:c0 + 256], in0=candB, scalar=selt[:, 1:2],
                  in1=mixin[:, :, c0:c0 + 256], op0=ALU.mult, op1=ALU.add)

    def load_x(t):
        t0 = t * TT
        for s in range(NS):
            P.dma(x1[:, s, :], xres[t0 + s * 128:t0 + (s + 1) * 128, :], reads=[(k_, t) for k_ in xres_keys] + [("x1", s)],
                  writes=[("x1", s)], q="sp")

    NTILES = NTOK // TT
    load_mix(0)
    load_x(0)
    load_rest_of_weights()
    if final:
        fn_t = sb([128, 1024], F32, "fnrep"); fn_k = "fnrep"
        P.dma(fn_t[:], fnorm.partition_broadcast(128), writes=[fn_k], q="sp")
    for t in range(NTOK // TT):
        t0 = t * TT
        if t > 0:
            load_x(t)
        for s in range(NS):
            for n in range(2):
                g_t, g_k = gen.next()
                for kc in range(8):
                    MM(g_t[:, :], mixin[:, kc, s * 128:(s + 1) * 128], Wob[:, kc, n * 512:(n + 1) * 512], [mik, "Wob"], [g_k],
                       start=(kc == 0), stop=(kc == 7))
                V("tensor_tensor", [g_k, ("x1", s)], [("x1", s)], out=x1[:, s, n * 512:(n + 1) * 512], in0=x1[:, s, n * 512:(n + 1) * 512], in1=g_t[:, :], op=ALU.add)
            A(h2b[:], x1[:, s, :], AF.Square, [("x1", s)], ["h2b", "st8a"], accum_out=st8[:, 0:1])
            rstd_from(st8[:, 0:1], 1024.0, "st8a")
            V("scalar_tensor_tensor", [("x1", s), "st8a", "g2rep"], ["h2b"], out=h2b[:], in0=x1[:, s, :], scalar=st8[:, 0:1], in1=g2rep[:],
              op0=ALU.mult, op1=ALU.mult)
            p_t, p_k = ptr.next()
            for kc in range(8):
                TR(p_t[:, kc, :], h2b[:, kc * 128:(kc + 1) * 128], ["h2b"], [p_k], inc=(kc == 7))
            A(h2T[:, :, s * 128:(s + 1) * 128], p_t[:], AF.Copy, [p_k], ["h2T"])
        if t + 1 < NTILES:
            load_mix(t + 1)
        for fh in range(2):
            for fi in range(HFC):
                fc = fh * HFC + fi
                pg, pgk = gen.next()
                pu, puk = gen.next()
                for kc in range(8):
                    MM(pg[:, :], Wgb[:, kc, fc * 128:(fc + 1) * 128], h2T[:, kc, :], [("Wgb", fc // 8), "h2T"], [pgk], start=(kc == 0), stop=(kc == 7))
                for kc in range(8):
                    MM(pu[:, :], Wub[:, kc, fc * 128:(fc + 1) * 128], h2T[:, kc, :], [("Wub", fc // 8), "h2T"], [puk], start=(kc == 0), stop=(kc == 7))
                A(actT[:, fi, :], pg[:, :], AF.Silu, [pgk], [("actT", fi)])
                V("tensor_tensor", [("actT", fi), puk], [("actT", fi)], out=actT[:, fi, :], in0=actT[:, fi, :], in1=pu[:, :], op=ALU.mult)
            for s in range(NS):
                for n in range(2):
                    pd, pdk = gen.next()
                    for fi in range(HFC):
                        fc = fh * HFC + fi
                        MM(pd[:, :], actT[:, fi, s * 128:(s + 1) * 128], Wdb[:, fc, n * 512:(n + 1) * 512], [("actT", fi), ("Wdb", fc)], [pdk],
                           start=(fi == 0), stop=(fi == HFC - 1))
                    V("tensor_tensor", [pdk, ("x1", s)], [("x1", s)], out=x1[:, s, n * 512:(n + 1) * 512], in0=x1[:, s, n * 512:(n + 1) * 512], in1=pd[:, :], op=ALU.add)
        for s in range(NS):
            if final:
                A(h2b[:], x1[:, s, :], AF.Square, [("x1", s)], ["h2b", "st8b"], accum_out=st8[:, 1:2])
                rstd_from(st8[:, 1:2], 1024.0, "st8b")
                V("scalar_tensor_tensor", [("x1", s), "st8b", fn_k], [("x1", s)], out=x1[:, s, :], in0=x1[:, s, :], scalar=st8[:, 1:2], in1=fn_t[:],
                  op0=ALU.mult, op1=ALU.mult)
            P.dma(xo[t0 + s * 128:t0 + (s + 1) * 128, :], x1[:, s, :], reads=[("x1", s)], writes=[(out_key, t)], q="pool")
        if tile_hook is not None:
            tile_hook(t)
    return [(out_key, j) for j in range(NTOK // TT)]


def consts_cmat():
    ident = np.eye(128, dtype=np.float32)
    tri = np.triu(np.ones((128, 128), np.float32))
    su = np.tril(np.ones((128, 128), np.float32), -1)
    madd = np.where(np.arange(128)[:, None] <= np.arange(128)[None, :], 0.0, -30000.0).astype(np.float32)
    ones = np.ones((128, 128), np.float32)
    return np.ascontiguousarray(np.concatenate([ident, tri, su, madd, ones], axis=1))

def mixer_inputs(inp, l, hh, xb, posb):
    W = inp["w_in"][l]
    hA, hB = 2 * hh, 2 * hh + 1
    r = lambda a, n: np.arange(a, a + n)
    fq = lambda h: r(64 * h, 64); fk = lambda h: r(256 + 64 * h, 64); fv = lambda h: r(512 + 64 * h, 64); ff = lambda h: r(768 + h, 1)
    gq = lambda h: r(772 + 32 * h, 32); gk = lambda h: r(900 + 32 * h, 32); gv = lambda h: r(1028 + 64 * h, 64); gr = lambda h: r(1284 + 64 * h, 64)
    gate = r(1540, 16); mcq = r(1556, 256); mckv = r(1812, 128); mkr = r(1940, 32)
    sz = lambda h: r(1972 + 64 * h, 64); sx = lambda h: r(2228 + 64 * h, 64)
    sB = r(2484 + 128 * hh, 128); sC = r(2740 + 128 * hh, 128); sdt = lambda h: r(2996 + h, 1)
    Z = lambda n: np.zeros((1024, n), np.float32)
    cols = [W[:, fq(hA)], W[:, fq(hB)],
            W[:, fk(hA)], W[:, fk(hB)],
            W[:, ff(hA)], W[:, ff(hB)], Z(30), W[:, gate],
            W[:, sx(hA)], W[:, sx(hB)], W[:, sB], W[:, sC],
            W[:, gq(hA)], W[:, gq(hB)], Z(32), W[:, mkr],
            Z(96), W[:, mkr[16:32]], W[:, mkr[0:16]],
            W[:, fv(hA)], W[:, fv(hB)], W[:, gk(hA)], W[:, gk(hB)], W[:, gv(hA)], W[:, gv(hB)], W[:, gr(hA)], W[:, gr(hB)],
            W[:, mcq], W[:, mckv],
            W[:, sz(hA)], W[:, sz(hB)], W[:, sdt(hA)], W[:, sdt(hB)]]
    w_all = np.ascontiguousarray(np.concatenate(cols, axis=1))
    assert w_all.shape == (1024, 1906), w_all.shape
    colc = np.zeros((128, 28), np.float32)
    colc[:, 0:8] = inp["norm1"][l].reshape(8, 128).T
    colc[:, 8:10] = inp["mla_q_norm"][l].reshape(2, 128).T
    colc[:, 10] = inp["mla_kv_norm"][l]
    cwl = inp["ssm_conv_w"][l]; cbl = inp["ssm_conv_b"][l]
    ccols = [np.concatenate([r(64 * hA, 64), r(64 * hB, 64)]), r(256 + 128 * hh, 128), r(512 + 128 * hh, 128)]
    for g in range(3):
        colc[:, 11 + 4 * g:15 + 4 * g] = cwl[:, ccols[g]].T
        colc[:, 23 + g] = cbl[ccols[g]]
    half = 16
    inv = (10000.0 ** (-np.arange(half, dtype=np.float32) / half)).astype(np.float32)
    colc[96:112, 26] = -inv; colc[112:128, 26] = inv
    colc[0, 27] = inp["fox_f_bias"][l][hA]; colc[1, 27] = inp["fox_f_bias"][l][hB]
    rowc = np.zeros((1, 326), np.float32)
    rowc[0, 0:64] = inp["gla_gate_b"][l][np.concatenate([gq(hA), gq(hB)]) - 772]
    rowc[0, 64:128] = inp["gla_out_norm"][l]; rowc[0, 128:192] = inp["gla_out_norm"][l]
    rowc[0, 192:194] = inp["ssm_dt_bias"][l][[hA, hB]]
    rowc[0, 194:196] = inp["ssm_A_log"][l][[hA, hB]]
    rowc[0, 196:198] = inp["ssm_D"][l][[hA, hB]]
    rowc[0, 198:326] = inp["ssm_norm"][l][128 * hh:128 * hh + 128]
    w2 = np.ascontiguousarray(inp["gla_gate_w2"][l][:, np.concatenate([gq(hA), gq(hB)]) - 772])
    Wq = inp["mla_w_uq"][l]
    qc = []
    for h in (hA, hB):
        base = 96 * h
        z32 = np.zeros((256, 32), np.float32); z96 = np.zeros((256, 96), np.float32)
        qc += [Wq[:, base:base + 64], z32, Wq[:, base + 64:base + 96], z96, Wq[:, base + 80:base + 96], Wq[:, base + 64:base + 80]]
    wuq = np.ascontiguousarray(np.concatenate(qc, axis=1)); assert wuq.shape == (256, 512)
    Wkv = inp["mla_w_ukv"][l]
    wukv = np.ascontiguousarray(np.concatenate([Wkv[:, 128 * hA:128 * hA + 64], Wkv[:, 128 * hB:128 * hB + 64],
                                                Wkv[:, 128 * hA + 64:128 * hA + 128], Wkv[:, 128 * hB + 64:128 * hB + 128]], axis=1))
    d = dict(w_all=w_all, colc=colc, rowc=rowc, w2=w2, wuq=wuq, wukv=wukv)
    if xb is not None:
        d.update(xin=np.ascontiguousarray(xb), pos=np.ascontiguousarray(posb.reshape(1, -1).astype(np.int32)), cmat=consts_cmat())
    return d


_CACHE = {}
PAIRS = [[0, 1], [2, 3], [4, 5], [6, 7]]


def _build_fused_nc(NT):
    S = NT * 512
    H = S // 2
    nc = bass.Bass("TRN2", target_bir_lowering=False, num_devices=8)
    di = lambda name, shape, dt=F32: nc.dram_tensor(name, shape, dt, kind="ExternalInput").ap()
    xin = di("xin", [S, 1024]); xhalf = di("xhalf", [H, 1024]); pos = di("pos", [1, S], I32)
    cmat = di("cmat", [128, 640]); sel = di("sel", [128, 2]); fnorm = di("fnorm", [1, 1024])
    L = []
    for l in range(2):
        L.append(dict(w_all=di("w_all%d" % l, [1024, NW]), colc=di("colc%d" % l, [128, NCOL]), rowc=di("rowc%d" % l, [1, NROW]),
                      w2=di("w2%d" % l, [16, 64]), wuq=di("wuq%d" % l, [256, 512]), wukv=di("wukv%d" % l, [128, 256]),
                      wo=di("wo%d" % l, [1024, 1024]), wg=di("wg%d" % l, [1024, 2816]), wu=di("wu%d" % l, [1024, 2816]),
                      wd=di("wd%d" % l, [2816, 1024]), colc2=di("colc2%d" % l, [128, 136]), g2row=di("g2row%d" % l, [1, 1024])))
    xo = nc.dram_tensor("xo", [H, 1024], F32, kind="ExternalOutput").ap()
    NJ = H // 512
    mx_loc = [nc.dram_tensor("mx_loc%d" % l, [NT, 512, 512], BF16).ap() for l in range(2)]
    mx_all = [nc.dram_tensor("mx_all%d" % l, [NT, 2, 512, 512], BF16).ap() for l in range(2)]
    xn_loc = nc.dram_tensor("xn_loc", [H, 1024], F32).ap()
    xn_all = nc.dram_tensor("xn_all", [NJ, 2, 512, 1024], F32).ap()
    with contextlib.ExitStack() as st0:
        P = Prog(nc, st0)
        P.dma_queues = ["sp"]
        for l in range(2):
            w = L[l]

            def mix_hook(t, l=l):
                P.collective("AllGather", [mx_loc[l][t]], [mx_all[l][t].rearrange("r f s -> (r f) s")], reads=[("mx_loc%d" % l, t)],
                             writes=[("mx_all%d" % l, t)], groups=PAIRS)

            def x_hook(j):
                P.collective("AllGather", [xn_loc[j * 512:(j + 1) * 512, :]], [xn_all[j].rearrange("r t d -> (r t) d")],
                             reads=[("xn_loc", j)], writes=[("xn_all", j)], groups=PAIRS)

            def xin_fn(t, s):
                r, j = t // NJ, t % NJ
                return xn_all[j, r, s * 128:(s + 1) * 128, :], [("xn_all", j)]

            with contextlib.ExitStack() as st:
                emit_mixer(nc, st, P, NT, xin, pos, w["w_all"], w["colc"], w["rowc"], cmat, w["w2"], w["wuq"], w["wukv"],
                           mx_loc[l], pfx="m%d" % l, xin_fn=None if l == 0 else xin_fn, out_key="mx_loc%d" % l, tile_major=True,
                           tile_hook=mix_hook)
                P.emit()
            with contextlib.ExitStack() as st:
                fk = emit_ffn(nc, st, P, H, l == 1, None, xhalf if l == 0 else xn_loc, w["wo"], w["wg"], w["wu"], w["wd"], w["colc2"], fnorm,
                              xn_loc if l == 0 else xo, pfx="f%d" % l, mix_all=mx_all[l], sel=sel,
                              xres_keys=() if l == 0 else ("xn_loc",), mix_key="mx_all%d" % l, out_key="xn_loc" if l == 0 else "xo_out",
                              tile_hook=x_hook if l == 0 else None, g2row=w["g2row"])
                if l == 1:
                    P.finish(fk)
                P.emit()
    return nc


def kernel(**inputs):
    inp = {k: np.asarray(v) for k, v in inputs.items()}
    B, S = 4, 8192
    NT = S // 512
    x = np.ascontiguousarray(inp["x"], dtype=np.float32)
    pos = inp["positions"]
    if "fused" not in _CACHE:
        _CACHE["fused"] = _build_fused_nc(NT)
    cm = consts_cmat()
    fn = np.ascontiguousarray(inp["final_norm"].reshape(1, 1024))
    maps = []
    for c in range(8):
        b, hh = c // 2, c % 2
        m = dict(xin=x[b], xhalf=np.ascontiguousarray(x[b][hh * (S // 2):(hh + 1) * (S // 2)]),
                 pos=np.ascontiguousarray(pos[b].reshape(1, -1).astype(np.int32)), cmat=cm, fnorm=fn)
        sel = np.zeros((128, 2), np.float32); sel[:, hh] = 1.0
        m["sel"] = sel
        for l in range(2):
            mi = mixer_inputs(inp, l, hh, None, None)
            for k in ("w_all", "colc", "rowc", "w2", "wuq", "wukv"):
                m["%s%d" % (k, l)] = mi[k]
            colc2 = np.zeros((128, 136), np.float32)
            colc2[:, 0:8] = inp["norm2"][l].reshape(8, 128).T
            colc2[:, 8:136] = np.eye(128, dtype=np.float32)
            m["wo%d" % l] = inp["w_out"][l]; m["wg%d" % l] = inp["w_gate"][l]; m["wu%d" % l] = inp["w_up"][l]
            m["wd%d" % l] = inp["w_down"][l]; m["colc2%d" % l] = colc2
            m["g2row%d" % l] = np.ascontiguousarray(inp["norm2"][l].reshape(1, 1024))
        maps.append(m)
    res = run_bass_kernel_spmd(_CACHE["fused"], maps, core_ids=list(range(8)))
    out = np.empty((B, S, 1024), np.float32)
    for c in range(8):
        b, hh = c // 2, c % 2
        out[b, hh * (S // 2):(hh + 1) * (S // 2)] = res.results[c]["xo"]
    return out
```

```python
import contextlib, math
import numpy as np
import ml_dtypes
import concourse.bass as bass
import concourse.mybir as mybir
from concourse.bass_utils import run_bass_kernel_spmd


F32 = mybir.dt.float32
BF16 = mybir.dt.bfloat16
I32 = mybir.dt.int32
AF = mybir.ActivationFunctionType
ALU = mybir.AluOpType
AX = mybir.AxisListType


class Prog:
    COMPUTE = ("pe", "act", "dve", "pool")
    NDMASEM = 8

    def __init__(self, nc, stack, dma_queues=("sp", "pool")):
        self.nc = nc
        self._stack = stack
        self.eng = {"pe": nc.tensor, "act": nc.scalar, "dve": nc.vector,
                    "pool": nc.gpsimd, "sp": nc.sync}
        self.streams = {e: [] for e in self.eng}
        self.csem = {e: stack.enter_context(nc.semaphore("c_" + e)) for e in self.COMPUTE}
        self.ccount = {e: 0 for e in self.COMPUTE}
        self.dsem = {q: [stack.enter_context(nc.semaphore("d_%s%d" % (q, i)))
                         for i in range(self.NDMASEM)] for q in dma_queues}
        self.dcount = {q: 0 for q in dma_queues}
        self.waited = {e: {} for e in self.eng}
        self.semobj = {}
        self.last_w = {}
        self.readers = {}
        self.nwaits = 0
        self.nops = 0
        self.pending = {e: False for e in self.COMPUTE}
        self.dma_rr = 0
        self.dma_queues = list(dma_queues)

    def _sid(self, sem):
        i = id(sem)
        self.semobj[i] = sem
        return i

    def _deps(self, reads, writes):
        deps = []
        for k in reads:
            w = self.last_w.get(k)
            if w is not None:
                deps.append(w)
        for k in writes:
            w = self.last_w.get(k)
            if w is not None:
                deps.append(w)
            deps.extend(self.readers.get(k, ()))
        return deps

    def _emit_waits(self, e, deps, own=None):
        wl = []
        wd = self.waited[e]
        best = {}
        own_sid = id(self.csem[e]) if e in self.csem else None
        for (sid, val) in deps:
            if sid == own_sid and val > self.ccount[e]:
                continue
            if wd.get(sid, 0) >= val:
                continue
            if best.get(sid, 0) < val:
                best[sid] = val
        for sid, val in best.items():
            wd[sid] = val
            wl.append((self.semobj[sid], val))
        return wl

    def op(self, e, fn, reads=(), writes=(), inc=True):
        assert e in self.COMPUTE
        pr = [k for k in reads if isinstance(k, str) and (k.startswith("ps") or k.startswith("fps"))]
        if pr:
            writes = list(writes) + pr
        deps = self._deps(reads, writes)
        sem = self.csem[e]
        sid = self._sid(sem)
        wl = self._emit_waits(e, deps)
        if inc:
            self.ccount[e] += 1
            val = self.ccount[e]
        else:
            val = self.ccount[e] + 1
        self.pending[e] = not inc
        self.nwaits += len(wl)
        self.nops += 1

        def run(eng, wl=wl, fn=fn, sem=sem, inc=inc):
            for (s, v) in wl:
                eng.wait_ge(s, v)
            ins = fn(eng)
            if inc:
                ins.then_inc(sem, 1)
        self.streams[e].append(run)
        tok = (sid, val)
        for k in reads:
            self.readers.setdefault(k, []).append(tok)
        for k in writes:
            self.last_w[k] = tok
            self.readers[k] = []
        return tok

    def dma(self, out, in_, reads=(), writes=(), q=None, **kw):
        if q is None:
            q = self.dma_queues[self.dma_rr % len(self.dma_queues)]
            self.dma_rr += 1
        n = self.dcount[q]
        self.dcount[q] += 1
        sem = self.dsem[q][n % self.NDMASEM]
        sid = self._sid(sem)
        val = 16 * (n // self.NDMASEM + 1)
        deps = self._deps(reads, writes)
        if n >= self.NDMASEM:
            deps.append((sid, val - 16))
        wl = self._emit_waits(q, deps)
        self.nwaits += len(wl)
        self.nops += 1

        def run(eng, wl=wl, sem=sem, out=out, in_=in_, kw=kw):
            for (s, v) in wl:
                eng.wait_ge(s, v)
            eng.dma_start(out=out, in_=in_, **kw).then_inc(sem, 16)
        self.streams[q].append(run)
        tok = (sid, val)
        for k in reads:
            self.readers.setdefault(k, []).append(tok)
        for k in writes:
            self.last_w[k] = tok
            self.readers[k] = []
        return tok

    def collective(self, kind, ins, outs, reads=(), writes=(), groups=None, **kw):
        q = "pool"
        if not hasattr(self, "ccsem"):
            self.ccsem = self._stack.enter_context(self.nc.semaphore("cc_sem"))
            self.cccount = 0
        sem = self.ccsem
        sid = self._sid(sem)
        self.cccount += 1
        val = self.cccount
        deps = self._deps(reads, writes)
        if val > 1:
            deps.append((sid, val - 1))
        wl = self._emit_waits(q, deps)
        self.nwaits += len(wl)
        self.nops += 1
        op = ALU.bypass if kind in ("AllGather", "AllToAll") else ALU.add

        def run(eng, wl=wl, sem=sem):
            for (s, v) in wl:
                eng.wait_ge(s, v)
            eng.collective_compute(kind, op, replica_groups=groups, ins=[a.opt() for a in ins], outs=[a.opt() for a in outs], **kw).then_inc(sem, 1)
        self.streams[q].append(run)
        tok = (sid, val)
        for k in reads:
            self.readers.setdefault(k, []).append(tok)
        for k in writes:
            self.last_w[k] = tok
            self.readers[k] = []
        return tok

    def finish(self, final_keys):
        deps = []
        for k in final_keys:
            w = self.last_w.get(k)
            if w is not None:
                deps.append(w)
        wl = self._emit_waits("sp", deps)

        def run(eng, wl=wl):
            for (s, v) in wl:
                eng.wait_ge(s, v)
        self.streams["sp"].append(run)

    def emit(self):
        nc = self.nc
        assert not any(self.pending.values()), self.pending
        streams = self.streams
        self.streams = {e: [] for e in self.eng}
        with nc.Block() as block:
            @block.sync
            def _(eng):
                for r in streams["sp"]:
                    r(eng)

            @block.tensor
            def _(eng):
                for r in streams["pe"]:
                    r(eng)

            @block.scalar
            def _(eng):
                for r in streams["act"]:
                    r(eng)

            @block.vector
            def _(eng):
                for r in streams["dve"]:
                    r(eng)

            @block.gpsimd
            def _(eng):
                for r in streams["pool"]:
                    r(eng)


NFM = 944
NTM = 962
NW = NFM + NTM
FM = {"fq": (0, 128), "fk": (128, 128), "fg": (256, 48), "sx": (304, 128), "sB": (432, 128),
      "sC": (560, 128), "gq": (688, 128), "kr2": (816, 128)}
TM = {"t1": (944, 448), "t2": (1392, 384), "t3": (1776, 130)}
NCOL = 28
NROW = 326
TWO_PI = 2.0 * math.pi
C1 = 6.28125
C2 = TWO_PI - C1


class Rot:
    def __init__(self, items):
        self.items = items
        self.i = 0

    def next(self):
        it = self.items[self.i % len(self.items)]
        self.i += 1
        return it


def emit_mixer(nc, st, P, NT, xin, pos, w_all, colc, rowc, cmat, w2, wuq, wukv, mixT, pfx="", xin_fn=None, out_key="mixT_out",
               tile_major=False, tile_hook=None):
    S = NT * 512
    NCH = NT * 4
    _n = [0]

    def sb(shape, dt, name=None):
        _n[0] += 1
        return st.enter_context(nc.sbuf_tensor("%s%s_%d" % (pfx, name or "t", _n[0]), shape, dt))

    def ps(shape, dt, name=None):
        _n[0] += 1
        return st.enter_context(nc.psum_tensor("%s%s_%d" % (pfx, name or "p", _n[0]), shape, dt))

    bg = {"q": [], "stride": 1, "burst": 1, "cnt": 0, "busy": False, "open": False, "ticks": 0, "done": [], "side_done": {}}

    def pump(n):
        bg["busy"] = True
        for _ in range(n):
            if not bg["q"]:
                break
            try:
                next(bg["q"][0])
            except StopIteration:
                bg["q"].pop(0)
        bg["busy"] = False

    def flush_to(keep):
        while len(bg["q"]) > keep:
            n0 = len(bg["q"])
            while len(bg["q"]) == n0:
                pump(1000)
        if tile_major and tile_hook is not None:
            while bg["done"]:
                tile_hook(bg["done"].pop(0))

    def tick():
        if bg["busy"]:
            return
        bg["ticks"] += 1
        if not bg["q"] or bg["open"]:
            return
        bg["cnt"] += 1
        if bg["cnt"] % bg["stride"] == 0:
            pump(bg["burst"])


    def A(out, in_, func, r, w, **kw):
        tk = P.op("act", lambda e: e.activation(out=out, in_=in_, func=func, **kw), reads=r, writes=w)
        tick()
        return tk

    def V(name, r, w, **kw):
        tk = P.op("dve", lambda e: getattr(e, name)(**kw), reads=r, writes=w)
        tick()
        return tk

    def G(name, r, w, **kw):
        return P.op("pool", lambda e: getattr(e, name)(**kw), reads=r, writes=w)

    def MM(out, lhsT, rhs, r, w, start=True, stop=True, inc=None):
        inc = stop if inc is None else inc
        tk = P.op("pe", lambda e: e.matmul(out=out, lhsT=lhsT, rhs=rhs, start=start, stop=stop), reads=r, writes=w, inc=inc)
        if not bg["busy"]:
            bg["open"] = not inc
            if inc:
                tick()
        return tk

    colc_sb = sb([128, NCOL], F32, "colc"); rowc_sb = sb([128, NROW], F32, "rowc")
    cm = sb([128, 640], F32, "cm"); identb = sb([128, 128], BF16, "identb")
    P.dma(colc_sb[:], colc, writes=["colc"])
    P.dma(rowc_sb[:], rowc.partition_broadcast(128), writes=["rowc"])
    P.dma(cm[:], cmat, writes=["cm"])
    identf = cm[:, 0:128]; tri = cm[:, 128:256]; su = cm[:, 256:384]; madd = cm[:, 384:512]; ones = cm[:, 512:640]
    V("tensor_copy", ["cm"], ["identb"], out=identb[:], in_=identf)
    mask2 = sb([128, 2, 128], F32, "mask2")
    V("tensor_copy", ["cm"], ["mask2"], out=mask2[:, 0, :], in_=tri)
    V("tensor_copy", ["cm", "mask2"], ["mask2"], out=mask2[:, 1, :], in_=tri)

    def TR(out, in_, r, w, inc=True):
        tk = P.op("pe", lambda e: e.transpose(out=out, in_=in_, identity=identb[:]), reads=list(r) + ["identb"], writes=w, inc=inc)
        if not bg["busy"]:
            bg["open"] = not inc
            if inc:
                tick()
        return tk

    g1 = colc_sb[:, 0:8]; qn = colc_sb[:, 8:10]; kvn = colc_sb[:, 10:11]
    cw = colc_sb[:, 11:23]; cb = colc_sb[:, 23:26]; inv = colc_sb[:, 26:27]; fb = colc_sb[:, 27:28]
    b2r = rowc_sb[:, 0:64]; gn2 = rowc_sb[:, 64:192]; dtb = rowc_sb[:, 192:194]; alog = rowc_sb[:, 194:196]
    dsk = rowc_sb[:, 196:198]; snorm = rowc_sb[:, 198:326]

    small = sb([128, 16], F32, "small")
    negfb = small[:, 0:1]; arep = small[:, 2:4]
    V("tensor_scalar", ["colc"], ["negfb"], out=negfb, in0=fb, scalar1=-1.0, scalar2=None, op0=ALU.mult)
    A(arep, alog, AF.Exp, ["rowc"], ["arep"])
    V("tensor_scalar", ["arep"], ["arep"], out=arep, in0=arep, scalar1=-1.0, scalar2=None, op0=ALU.mult)

    xt = Rot([(sb([128, 1024], F32, "xt"), "xt%d" % i) for i in range(2)])
    Wb = sb([128, 8, NW], BF16, "Wb")
    HALF = NW // 2
    for kc in range(8):
        for hf in range(2):
            t_, k_ = xt.next()
            P.dma(t_[:, 0:HALF], w_all[kc * 128:(kc + 1) * 128, hf * HALF:(hf + 1) * HALF], writes=[k_])
            V("tensor_scalar", [k_, "colc"], ["Wb"], out=Wb[:, kc, hf * HALF:(hf + 1) * HALF], in0=t_[:, 0:HALF],
              scalar1=g1[:, kc:kc + 1], scalar2=None, op0=ALU.mult)
    w2sb = sb([48, 64], F32, "w2sb")
    P.dma(w2sb[32:48, :], w2, writes=["w2sb"])
    wuqb = sb([128, 2, 512], BF16, "wuqb")
    t_, k_ = xt.next()
    wuqf = t_[:, 0:1024].rearrange("p (c n) -> p c n", c=2)
    P.dma(wuqf, wuq.rearrange("(c p) n -> p c n", p=128), writes=[k_])
    for c in range(2):
        V("tensor_scalar", [k_, "colc"], ["wuqb"], out=wuqb[:, c, :], in0=wuqf[:, c, :], scalar1=qn[:, c:c + 1],
          scalar2=None, op0=ALU.mult)
    wukvb = sb([128, 256], BF16, "wukvb")
    t_, k_ = xt.next()
    P.dma(t_[:, 0:256], wukv, writes=[k_])
    V("tensor_scalar", [k_, "colc"], ["wukvb"], out=wukvb[:], in0=t_[:, 0:256], scalar1=kvn, scalar2=None, op0=ALU.mult)

    SA = max(S, 8192)
    KTf = [sb([128, SA], BF16, "KTf%d" % h) for h in range(2)]
    KTm = [sb([128, SA], BF16, "KTm%d" % h) for h in range(2)]

    def carve(tile, r0, nr, slot, dt=F32, n=512):
        return tile[r0:r0 + nr, slot * 1024:slot * 1024 + (n * (4 if dt != BF16 else 2)) // 2].bitcast(dt) if dt != BF16 \
            else tile[r0:r0 + nr, slot * 1024:slot * 1024 + n]

    VAf = sb([128, NCH + 1, 2, 65], BF16, "VAf"); VAm = sb([128, NCH + 1, 2, 65], BF16, "VAm")
    for h in range(2):
        G("memset", [], ["KTf%d_init" % h], ap=KTf[h][64:70, :], constant=1.0)
    G("memset", [], ["VAf_init"], ap=VAf[:], constant=1.0)
    G("memset", [], ["VAm_init"], ap=VAm[:], constant=1.0)
    QTf2 = [[sb([70, 512], BF16, "QTf%d_%d" % (p_, h)) for h in range(2)] for p_ in range(2)]
    QTm2 = [[sb([96, 512], BF16, "QTm%d_%d" % (p_, h)) for h in range(2)] for p_ in range(2)]
    for p_ in range(2):
        for h in range(2):
            G("memset", [], ["QTf%d_%d" % (p_, h)], ap=QTf2[p_][h][64:70, :], constant=1.0)

    gen = Rot([(ps([128, 512], F32, "g%d" % i), "psg%d" % i) for i in range(3)])
    sps = Rot([(ps([128, 512], F32, "s%d" % i), "pss%d" % i) for i in range(2)])
    acc = Rot([(ps([128, 512], F32, "a%d" % i), "psa%d" % i) for i in range(1)])
    ptr = Rot([(ps([128, 8, 128], BF16, "t%d" % i), "pst%d" % i) for i in range(2)])

    junk = sb([128, 384], BF16, "junk")
    hb = sb([128, 1024], BF16, "hb")
    hT = sb([128, 8, 512], BF16, "hT")
    st8 = sb([128, 8], F32, "st8")
    PT = Rot([(sb([128, 512], BF16, "PT"), "PT%d" % i) for i in range(3)])
    osb = sb([65, 512], F32, "osb"); rcp = osb
    stmp = sb([128, 512], F32, "stmp")

    def silu_to(out_ap, x_ap, tmp_ap, rkeys, wkey, tmpkey):
        A(tmp_ap, x_ap, AF.Exp, list(rkeys), [tmpkey], scale=-1.0)
        V("tensor_scalar", [tmpkey], [tmpkey], out=tmp_ap, in0=tmp_ap, scalar1=1.0, scalar2=None, op0=ALU.add)
        V("reciprocal", [tmpkey], [tmpkey], out=tmp_ap, in_=tmp_ap)
        V("tensor_tensor", list(rkeys) + [tmpkey], [wkey], out=out_ap, in0=x_ap, in1=tmp_ap, op=ALU.mult)

    mixt = Rot([(sb([128, 4, 512], BF16, "mixt"), "mixt%d" % i) for i in range(2)])
    fe = carve(KTf[0], 96, 2, 0); fsp = carve(KTf[0], 96, 2, 1)
    Fc = Rot([(carve(KTf[0], 96, 2, 2 + i), "Fc%d" % i) for i in range(2)])
    fr1 = carve(KTf[0], 96, 2, 4); fr2 = carve(KTf[0], 96, 2, 5)
    ones2 = carve(KTf[0], 96, 2, 6)
    Fzero = carve(KTf[0], 96, 2, 7)
    Fq = KTf[1][96:98, 0:1536].rearrange("p (j n) -> p j n", j=3)
    Fk = KTf[1][96:98, 1536:3072].rearrange("p (j n) -> p j n", j=3)
    G("memset", [], ["Fzero"], ap=Fzero, constant=0.0)
    G("memset", [], ["ones2"], ap=ones2, constant=1.0)
    cn = sb([128, 384], BF16, "cn"); cnT = sb([128, 3, 512], BF16, "cnT")
    posi = carve(KTm[0], 96, 32, 0, I32); rti = posi
    ang = carve(KTm[0], 96, 32, 1); rtmp = carve(KTm[0], 96, 32, 2)
    cs = carve(KTm[0], 96, 32, 3); sn = carve(KTm[0], 96, 32, 4)
    krA = carve(KTm[1], 96, 32, 0); krB = carve(KTm[1], 96, 32, 1)
    gateT = sb([48, 512], F32, "gateT"); gqT = sb([64, 512], F32, "gqT")
    gt = sb([128, 64], F32, "gt"); gsp = sb([128, 64], F32, "gsp")
    eGT = sb([64, 128], F32, "eGT"); enG = sb([128, 64], F32, "enG")
    qtl = sb([64, 128], BF16, "qtl"); ktl = sb([128, 64], BF16, "ktl"); ktlT = sb([64, 128], BF16, "ktlT")
    Am = sb([128, 2, 128], BF16, "Am"); vb = sb([128, 128], BF16, "vb")
    Sst = sb([64, 128], F32, "Sst"); Sbf = sb([64, 128], BF16, "Sbf")
    sr = sb([128, 128], F32, "sr"); gtmp = sb([128, 128], F32, "gtmp"); yb = sb([128, 128], BF16, "yb")
    G("memset", [], ["Sst"], ap=Sst[:], constant=0.0)
    G("memset", [], ["Sbf"], ap=Sbf[:], constant=0.0)
    cin = [sb([128, 515], F32, "cin%d" % g) for g in range(3)]
    for g in range(3):
        G("memset", [], ["cin%d" % g], ap=cin[g][:, 0:3], constant=0.0)
    cacc = sb([128, 512], F32, "cacc")
    fT = [sb([128, 512], BF16, "fT%d" % g) for g in range(3)]
    xs_tok = sb([128, 128], BF16, "xs_tok"); B_tok = sb([128, 128], BF16, "B_tok")
    dtr = sb([128, 8], F32, "dtr"); dt8 = sb([128, 8], F32, "dt8"); a8 = sb([128, 8], F32, "a8")
    ea8 = sb([128, 8], F32, "ea8"); eal8 = sb([128, 8], F32, "eal8"); ds8 = sb([128, 8], F32, "ds8"); wl8 = sb([128, 8], F32, "wl8")
    Lh = sb([128, 128], F32, "Lh"); Dexp = sb([128, 128], F32, "Dexp"); t1s = sb([128, 128], F32, "t1s")
    MT = sb([128, 2, 128], BF16, "MT"); Xd = sb([128, 128], BF16, "Xd")
    Hs = sb([128, 128], F32, "Hs"); Hbf = sb([128, 128], BF16, "Hbf")
    ytmp = sb([128, 128], F32, "ytmp"); szt = sb([128, 4, 128], F32, "szt"); yd = sb([128, 128], BF16, "yd")
    G("memset", [], ["Hs"], ap=Hs[:], constant=0.0)
    G("memset", [], ["Hbf"], ap=Hbf[:], constant=0.0)

    def rstd_from(ssap, n, dim, key):
        A(ssap, ssap, AF.Ln, [key], [key], scale=1.0 / dim, bias=1e-6)
        A(ssap, ssap, AF.Exp, [key], [key], scale=-0.5)

    def attention(QT, qkey, KT, kkey, VA, vkey, h, t, scale, nrows, out_ap, out_keys):
        nk = 4 * t + 4
        a_t, a_k = acc.next()
        pend = None

        def issue_qk(j):
            i = j - 4 * t
            c0 = 0 if i < 0 else i * 128
            s_t, s_k = sps.next()
            MM(s_t[:, c0:512], KT[0:nrows, j * 128:(j + 1) * 128], QT[0:nrows, c0:512],
               [kkey(j // 4), qkey], [s_k])
            return (j, c0, s_t, s_k)

        nxt = issue_qk(0)
        for j in range(nk):
            cur = nxt
            if j + 1 < nk:
                nxt = issue_qk(j + 1)
            (_, c0, s_t, s_k) = cur
            if j >= 4 * t:
                V("tensor_tensor", [s_k, "cm"], [s_k], out=s_t[:, c0:c0 + 128], in0=s_t[:, c0:c0 + 128], in1=madd, op=ALU.add)
            p_t, p_k = PT.next()
            A(p_t[:, c0:512], s_t[:, c0:512], AF.Exp, [s_k], [p_k], scale=scale)
            vflat = VA[:, j:j + 2, :, :].rearrange("p c h d -> p (c h d)")[:, h * 65:h * 65 + 128]
            MM(a_t[:, c0:512], vflat, p_t[:, c0:512], [vkey(j // 4), vkey(min((j + 1) // 4, t)), p_k], [a_k], start=(j == 0), stop=(j == nk - 1))
            yield
        V("reciprocal", [a_k, "osb"], ["rcp"], out=rcp[64:65, :], in_=a_t[64:65, :])
        A(osb[0:64, :], a_t[0:64, :], AF.Copy, [a_k], ["osb"])
        b_t, b_k = sps.next()
        MM(b_t[0:64, :], ones[64:65, 0:64], rcp[64:65, :], ["cm", "rcp"], [b_k])
        V("tensor_tensor", ["osb", b_k, "rcp"], out_keys + ["rcp"], out=out_ap, in0=osb[0:64, :], in1=b_t[0:64, :], op=ALU.mult)
        yield

    def tile_attention(t, par, mx, mxk):
        for h in range(2):
            yield from attention(QTf2[par][h], "QTf%d_%d" % (par, h), KTf[h], lambda tt, h=h: ("KTf", h, tt), VAf, lambda tt: ("VAf", tt), h, t,
                                 1.0, 70, mx[64 * h:64 * h + 64, 0, :], [mxk])
        for h in range(2):
            yield from attention(QTm2[par][h], "QTm%d_%d" % (par, h), KTm[h], lambda tt, h=h: ("KTm", h, tt), VAm, lambda tt: ("VAm", tt), h, t,
                                 sc_mla, 96, mx[64 * h:64 * h + 64, 2, :], [mxk])
        while not bg["side_done"].get(t):
            yield
        if tile_major:
            P.dma(mixT[t].rearrange("(m p) s -> p m s", p=128), mx[:], reads=[mxk], writes=[(out_key, t)], q="pool")
        else:
            P.dma(mixT.rearrange("(m p) s -> p m s", p=128)[:, :, t * 512:(t + 1) * 512], mx[:], reads=[mxk], writes=[out_key], q="sp")
        bg["done"].append(t)
        yield

    Fprev = (Fzero, "Fzero")
    isq_fox = 0.125
    sc_mla = 96.0 ** -0.5
    sc_gq = 32.0 ** -0.5

    for t in range(NT):
        c512 = slice(t * 512, (t + 1) * 512)
        mx, mxk = mixt.next()
        par = t % 2
        QTf = QTf2[par]; QTm = QTm2[par]
        ticks0 = bg["ticks"]
        for s in range(4):
            x_t, x_k = xt.next()
            r0 = t * 512 + s * 128
            if xin_fn is None:
                P.dma(x_t[:], xin[r0:r0 + 128, :], writes=[x_k], q="sp")
            else:
                xap, xkeys = xin_fn(t, s)
                P.dma(x_t[:], xap, reads=list(xkeys), writes=[x_k], q="sp")
            A(hb[:], x_t[:], AF.Square, [x_k], ["hb", "st8a"], accum_out=st8[:, 0:1])
            rstd_from(st8[:, 0:1], 1, 1024.0, "st8a")
            V("tensor_scalar", [x_k, "st8a"], ["hb"], out=hb[:], in0=x_t[:], scalar1=st8[:, 0:1], scalar2=None, op0=ALU.mult)
            p_t, p_k = ptr.next()
            for kc in range(8):
                TR(p_t[:, kc, :], hb[:, kc * 128:(kc + 1) * 128], ["hb"], [p_k], inc=(kc == 7))
            A(hT[:, :, s * 128:(s + 1) * 128], p_t[:], AF.Copy, [p_k], ["hT"])

        def fm_group(name):
            off, M = FM[name]
            g_t, g_k = gen.next()
            for kc in range(8):
                MM(g_t[0:M, :], Wb[:, kc, off:off + M], hT[:, kc, :], ["Wb", "hT"], [g_k], start=(kc == 0), stop=(kc == 7))
            return g_t, g_k

        def tm_group(name, s):
            off, N = TM[name]
            g_t, g_k = gen.next()
            for kc in range(8):
                MM(g_t[:, 0:N], hT[:, kc, s * 128:(s + 1) * 128], Wb[:, kc, off:off + N], ["Wb", "hT"], [g_k], start=(kc == 0), stop=(kc == 7))
            return g_t, g_k

        g_t, g_k = fm_group("fq")
        for h in range(2):
            A(QTf[h][0:64, :], g_t[64 * h:64 * h + 64, :], AF.Copy, [g_k], ["QTf%d_%d" % (par, h)], scale=isq_fox)
        g_t, g_k = fm_group("fk")
        for h in range(2):
            V("tensor_copy", [g_k, "KTf%d_init" % h], [("KTf", h, t)], out=KTf[h][0:64, c512], in_=g_t[64 * h:64 * h + 64, :])
        g_t, g_k = fm_group("fg")
        A(fe, g_t[0:2, :], AF.Exp, [g_k, "negfb"], ["fe"], scale=-1.0, bias=negfb[0:2, :])
        A(gateT[32:48, :], g_t[32:48, :], AF.Copy, [g_k], ["gateT"])
        A(fsp, fe, AF.Ln, ["fe"], ["fsp"], bias=1.0)
        F_t, F_k = Fc.next()
        V("tensor_tensor_scan", ["ones2", "fsp", Fprev[1]], [F_k], out=F_t, data0=ones2, data1=fsp,
          initial=Fprev[0][:, 0:1] if Fprev[1] == "Fzero" else Fprev[0][:, 511:512], op0=ALU.mult, op1=ALU.subtract)
        Fprev = (F_t, F_k)
        V("tensor_copy", [F_k], ["Fq"], out=Fq[:, 0, :], in_=F_t)
        V("tensor_tensor", [F_k, "Fq"], ["fr1"], out=fr1, in0=F_t, in1=Fq[:, 0, :], op=ALU.subtract)
        V("tensor_copy", ["fr1", "Fq"], ["Fq"], out=Fq[:, 1, :], in_=fr1)
        V("tensor_tensor", ["fr1", "Fq"], ["fr2"], out=fr2, in0=fr1, in1=Fq[:, 1, :], op=ALU.subtract)
        V("tensor_copy", ["fr2", "Fq"], ["Fq"], out=Fq[:, 2, :], in_=fr2)
        V("tensor_scalar", ["Fq"], ["Fk"], out=Fk, in0=Fq, scalar1=-1.0, scalar2=None, op0=ALU.mult)
        for h in range(2):
            for j in range(3):
                P.dma(QTf[h][64 + j:65 + j, :], Fq[h:h + 1, j, :], reads=["Fq", "QTf%d_%d" % (par, h)], writes=["QTf%d_%d" % (par, h)], q="sp")
                P.dma(KTf[h][67 + j:68 + j, c512], Fk[h:h + 1, j, :], reads=["Fk", "KTf%d_init" % h, ("KTf", h, t)],
                      writes=[("KTf", h, t)], q="sp")
        for g, name in enumerate(("sx", "sB", "sC")):
            g_t, g_k = fm_group(name)
            ck = "cin%d" % g
            A(cin[g][:, 3:515], g_t[:, :], AF.Copy, [g_k, ck], [ck])

        def conv_silu(g):
            ck = "cin%d" % g
            V("tensor_scalar", [ck, "colc"], ["cacc"], out=cacc[:], in0=cin[g][:, 0:512], scalar1=cw[:, 4 * g:4 * g + 1],
              scalar2=cb[:, g:g + 1], op0=ALU.mult, op1=ALU.add)
            for k in range(1, 4):
                V("scalar_tensor_tensor", [ck, "colc", "cacc"], ["cacc"], out=cacc[:], in0=cin[g][:, k:k + 512],
                  scalar=cw[:, 4 * g + k:4 * g + k + 1], in1=cacc[:], op0=ALU.mult, op1=ALU.add)
            V("tensor_copy", [ck], [ck], out=cin[g][:, 0:3], in_=cin[g][:, 512:515])
            silu_to(fT[g][:], cacc[:], stmp[:], ["cacc"], "fT%d" % g, "stmp")

        R = slice(96, 128)
        RO = slice(64, 96)
        P.dma(posi, pos[0:1, c512].partition_broadcast(32), reads=["posi"], writes=["posi"], q="sp")
        V("tensor_copy", ["posi"], ["ang"], out=ang, in_=posi)
        V("tensor_scalar", ["ang", "colc"], ["ang"], out=ang, in0=ang, scalar1=inv[R, :], scalar2=None, op0=ALU.mult)
        for which, tab, tkey in ((0, sn, "sn"), (1, cs, "cs")):
            shift = 0.0 if which == 0 else math.pi / 2
            V("tensor_scalar", ["ang"], ["rtmp"], out=rtmp, in0=ang, scalar1=shift, scalar2=1.0 / TWO_PI, op0=ALU.add, op1=ALU.mult)
            V("tensor_copy", ["rtmp", "posi"], ["posi"], out=rti, in_=rtmp)
            V("tensor_copy", ["posi"], ["rtmp"], out=rtmp, in_=rti)
            V("scalar_tensor_tensor", ["rtmp", "ang"], [tkey], out=tab, in0=rtmp, scalar=-C1, in1=ang, op0=ALU.mult, op1=ALU.add)
            V("scalar_tensor_tensor", ["rtmp", tkey], [tkey], out=tab, in0=rtmp, scalar=-C2, in1=tab, op0=ALU.mult, op1=ALU.add)
            V("tensor_scalar", [tkey], [tkey], out=tab, in0=tab, scalar1=shift, scalar2=3.14159, op0=ALU.add, op1=ALU.min)
            V("tensor_scalar", [tkey], [tkey], out=tab, in0=tab, scalar1=-3.14159, scalar2=None, op0=ALU.max)
            A(tab, tab, AF.Sin, [tkey], [tkey])
        g_t, g_k = fm_group("gq")
        A(gqT[:], g_t[0:64, :], AF.Copy, [g_k], ["gqT"], scale=sc_gq)
        V("tensor_tensor", [g_k, "cs"], ["krA"], out=krA, in0=g_t[R, :], in1=cs, op=ALU.mult)
        g_t, g_k = fm_group("kr2")
        V("tensor_tensor", [g_k, "sn"], ["krB"], out=krB, in0=g_t[R, :], in1=sn, op=ALU.mult)
        for h in range(2):
            V("tensor_tensor", ["krA", "krB"], [("KTm", h, t)], out=KTm[h][RO, c512], in0=krA, in1=krB, op=ALU.add)

        for s in range(4):
            ch = 4 * t + s
            g_t, g_k = tm_group("t1", s)
            for h in range(2):
                V("tensor_copy", [g_k, "VAf_init"], [("VAf", t)], out=VAf[:, ch, h, 0:64], in_=g_t[:, 64 * h:64 * h + 64])
            cc = slice(s * 128, (s + 1) * 128)
            gp_t, gp_k = gen.next()
            MM(gp_t[:, 0:64], gateT[32:48, cc], w2sb[32:48, :], ["gateT", "w2sb"], [gp_k])
            V("tensor_tensor", [gp_k, "rowc"], ["gt"], out=gt[:], in0=gp_t[:, 0:64], in1=b2r, op=ALU.add)
            A(gt[:], gt[:], AF.Exp, ["gt"], ["gt"], scale=-1.0)
            A(gsp[:], gt[:], AF.Ln, ["gt"], ["gsp"], bias=1.0)
            MM(gp_t[:, 128:192], tri, gsp[:], ["cm", "gsp"], [gp_k])
            MM(gp_t[0:64, 256:384], gsp[:], tri, ["cm", "gsp"], [gp_k])
            A(eGT[:], gp_t[0:64, 256:384], AF.Exp, [gp_k], ["eGT"], scale=-1.0 / 16)
            A(enG[:], gp_t[:, 128:192], AF.Exp, [gp_k], ["enG"], scale=1.0 / 16)
            V("tensor_tensor", ["gqT", "eGT"], ["qtl"], out=qtl[:], in0=gqT[:, cc], in1=eGT[:], op=ALU.mult)
            V("tensor_tensor", [g_k, "enG"], ["ktl"], out=ktl[:], in0=g_t[:, 128:192], in1=enG[:], op=ALU.mult)
            A(vb[:], g_t[:, 192:320], AF.Copy, [g_k], ["vb"])
            silu_to(sr[:], g_t[:, 320:448], sr[:], [g_k], "sr", "sr")
            p_t, p_k = ptr.next()
            TR(p_t[0:64, 0, :], ktl[:], ["ktl"], [p_k])
            A(ktlT[:], p_t[0:64, 0, :], AF.Copy, [p_k], ["ktlT"])
            a_t, a_k = gen.next()
            for h in range(2):
                hs = slice(32 * h, 32 * h + 32)
                MM(a_t[:, 128 * h:128 * h + 128], ktlT[hs, :], qtl[hs, :], ["ktlT", "qtl"], [a_k])
            V("tensor_tensor", [a_k, "mask2"], ["Am"], out=Am[:], in0=a_t[:, 0:256].rearrange("p (h n) -> p h n", h=2), in1=mask2[:], op=ALU.mult)
            for h in range(2):
                hs = slice(32 * h, 32 * h + 32)
                vs = slice(64 * h, 64 * h + 64)
                MM(a_t[:, 256 + 64 * h:256 + 64 * h + 64], Am[:, h, :], vb[:, vs], ["Am", "vb"], [a_k], start=True, stop=False)
                MM(a_t[:, 256 + 64 * h:256 + 64 * h + 64], qtl[hs, :], Sbf[hs, vs], ["qtl", "Sbf"], [a_k], start=False, stop=True)
            MM(gp_t[0:64, 384:512], ktl[:], vb[:], ["ktl", "vb"], [gp_k])
            V("tensor_scalar", ["Sst", "eGT"], ["Sst"], out=Sst[:], in0=Sst[:], scalar1=eGT[:, 127:128], scalar2=None, op0=ALU.mult)
            V("scalar_tensor_tensor", [gp_k, "eGT", "Sst"], ["Sst"], out=Sst[:], in0=gp_t[0:64, 384:512], scalar=eGT[:, 127:128],
              in1=Sst[:], op0=ALU.mult, op1=ALU.add)
            V("tensor_copy", ["Sst"], ["Sbf"], out=Sbf[:], in_=Sst[:])
            for h in range(2):
                A(gtmp[:, 64 * h:64 * h + 64], a_t[:, 256 + 64 * h:256 + 64 * h + 64], AF.Square, [a_k], ["gtmp", "st8g"],
                  accum_out=st8[:, 2 + h:3 + h])
            rstd_from(st8[:, 2:4], 2, 64.0, "st8g")
            for h in range(2):
                V("scalar_tensor_tensor", [a_k, "st8g", "rowc"], ["gtmp"], out=gtmp[:, 64 * h:64 * h + 64],
                  in0=a_t[:, 256 + 64 * h:256 + 64 * h + 64], scalar=st8[:, 2 + h:3 + h], in1=gn2[:, 64 * h:64 * h + 64],
                  op0=ALU.mult, op1=ALU.mult)
            V("tensor_tensor", ["gtmp", "sr"], ["yb"], out=yb[:], in0=gtmp[:], in1=sr[:], op=ALU.mult)
            p_t, p_k = ptr.next()
            TR(p_t[:, 0, :], yb[:], ["yb"], [p_k])
            A(mx[:, 1, cc], p_t[:, 0, :], AF.Copy, [p_k], [mxk])

            if s < 3:
                conv_silu(s)

            g_t, g_k = tm_group("t2", s)
            A(junk[:, 0:256], g_t[:, 0:256], AF.Square, [g_k], ["junk", "st8m"], accum_out=st8[:, 4:5])
            A(junk[:, 256:384], g_t[:, 256:384], AF.Square, [g_k], ["junk", "st8m"], accum_out=st8[:, 5:6])
            A(st8[:, 4:5], st8[:, 4:5], AF.Ln, ["st8m"], ["st8m"], scale=1.0 / 256, bias=1e-6)
            A(st8[:, 5:6], st8[:, 5:6], AF.Ln, ["st8m"], ["st8m"], scale=1.0 / 128, bias=1e-6)
            A(st8[:, 4:6], st8[:, 4:6], AF.Exp, ["st8m"], ["st8m"], scale=-0.5)
            V("tensor_scalar", [g_k, "st8m"], ["cn"], out=cn[:, 0:256], in0=g_t[:, 0:256], scalar1=st8[:, 4:5], scalar2=None, op0=ALU.mult)
            V("tensor_scalar", [g_k, "st8m"], ["cn"], out=cn[:, 256:384], in0=g_t[:, 256:384], scalar1=st8[:, 5:6], scalar2=None, op0=ALU.mult)
            p_t, p_k = ptr.next()
            for c in range(3):
                TR(p_t[:, c, :], cn[:, c * 128:(c + 1) * 128], ["cn"], [p_k], inc=(c == 2))
            A(cnT[:, :, cc], p_t[:, 0:3, :], AF.Copy, [p_k], ["cnT"])
            v_t, v_k = gen.next()
            p2_t, p2_k = p_t, p_k
            MM(v_t[:, 0:128], cnT[:, 2, cc], wukvb[:, 128:256], ["cnT", "wukvb"], [v_k])
            for h in range(2):
                V("tensor_copy", [v_k, "VAm_init"], [("VAm", t)], out=VAm[:, ch, h, 0:64], in_=v_t[:, 64 * h:64 * h + 64])

            g_t, g_k = tm_group("t3", s)
            silu_to(szt[:, s, :], g_t[:, 0:128], szt[:, s, :], [g_k], "szt", "szt")
            V("tensor_copy", [g_k], ["dtr"], out=dtr[:, 2 * s:2 * s + 2], in_=g_t[:, 128:130])

        for h in range(2):
            qa_t, qa_k = gen.next()
            for c in range(2):
                MM(qa_t[:, :], wuqb[:, c, 256 * h:256 * h + 128], cnT[:, c, :], ["wuqb", "cnT"], [qa_k], start=(c == 0), stop=(c == 1))
            qb_t, qb_k = gen.next()
            for c in range(2):
                MM(qb_t[:, :], wuqb[:, c, 256 * h + 128:256 * h + 256], cnT[:, c, :], ["wuqb", "cnT"], [qb_k], start=(c == 0), stop=(c == 1))
            qk = "QTm%d_%d" % (par, h)
            A(QTm[h][0:64, :], qa_t[0:64, :], AF.Copy, [qa_k], [qk])
            V("tensor_tensor", [qa_k, "cs"], ["krA"], out=krA, in0=qa_t[R, :], in1=cs, op=ALU.mult)
            V("tensor_tensor", [qb_k, "sn"], ["krB"], out=krB, in0=qb_t[R, :], in1=sn, op=ALU.mult)
            V("tensor_tensor", ["krA", "krB"], [qk], out=QTm[h][RO, :], in0=krA, in1=krB, op=ALU.add)
        kn_t, kn_k = gen.next()
        MM(kn_t[:, :], wukvb[:, 0:128], cnT[:, 2, :], ["wukvb", "cnT"], [kn_k])
        for h in range(2):
            A(KTm[h][0:64, c512], kn_t[64 * h:64 * h + 64, :], AF.Copy, [kn_k], [("KTm", h, t)])

        units = 16 * (t + 1) + 9
        bg["q"].append(tile_attention(t, par, mx, mxk))
        budget = bg.get("nticks", 900)
        bg["stride"] = max(1, budget // units)
        bg["burst"] = max(1, -(-units // budget))
        bg["cnt"] = 0

        V("tensor_tensor", ["dtr", "rowc"], ["dt8"], out=dt8[:].rearrange("p (s h) -> p s h", h=2),
          in0=dtr[:].rearrange("p (s h) -> p s h", h=2), in1=dtb.unsqueeze(1).to_broadcast([128, 4, 2]), op=ALU.add)
        A(dt8[:], dt8[:], AF.Exp, ["dt8"], ["dt8"])
        A(dt8[:], dt8[:], AF.Ln, ["dt8"], ["dt8"], bias=1.0)
        V("tensor_tensor", ["dt8", "arep"], ["a8"], out=a8[:].rearrange("p (s h) -> p s h", h=2),
          in0=dt8[:].rearrange("p (s h) -> p s h", h=2), in1=arep.unsqueeze(1).to_broadcast([128, 4, 2]), op=ALU.mult)
        sc_t, sc_k = gen.next()
        MM(sc_t[:, 0:8], tri, a8[:], ["cm", "a8"], [sc_k])
        MM(sc_t[:, 8:16], ones, a8[:], ["cm", "a8"], [sc_k])
        A(ea8[:], sc_t[:, 0:8], AF.Exp, [sc_k], ["ea8"])
        A(eal8[:], sc_t[:, 8:16], AF.Exp, [sc_k], ["eal8"])
        A(wl8[:], sc_t[:, 0:8], AF.Copy, [sc_k], ["wl8"])
        V("tensor_tensor", [sc_k, "wl8"], ["ds8"], out=ds8[:], in0=sc_t[:, 8:16], in1=wl8[:], op=ALU.subtract)
        A(ds8[:], ds8[:], AF.Exp, ["ds8"], ["ds8"])
        V("tensor_tensor", ["ds8", "dt8"], ["wl8"], out=wl8[:], in0=ds8[:], in1=dt8[:], op=ALU.mult)

        for s in range(4):
            cc = slice(s * 128, (s + 1) * 128)
            p_t, p_k = ptr.next()
            TR(p_t[:, 0, :], fT[0][:, cc], ["fT0"], [p_k])
            TR(p_t[:, 1, :], fT[1][:, cc], ["fT1"], [p_k])
            A(xs_tok[:], p_t[:, 0, :], AF.Copy, [p_k], ["xs_tok"])
            A(B_tok[:], p_t[:, 1, :], AF.Copy, [p_k], ["B_tok"])
            cb_t, cb_k = gen.next()
            MM(cb_t[:, 0:128], fT[1][:, cc], fT[2][:, cc], ["fT1", "fT2"], [cb_k])
            e_t, e_k = gen.next()
            for h in range(2):
                col = 2 * s + h
                V("tensor_scalar", ["cm", "a8"], ["Lh"], out=Lh[:], in0=su, scalar1=a8[:, col:col + 1], scalar2=None, op0=ALU.mult)
                MM(e_t[:, 128 * h:128 * h + 128], Lh[:], tri, ["Lh", "cm"], [e_k])
                A(Dexp[:], e_t[:, 128 * h:128 * h + 128], AF.Exp, [e_k], ["Dexp"])
                V("tensor_tensor", [cb_k, "Dexp"], ["t1s"], out=t1s[:], in0=cb_t[:, 0:128], in1=Dexp[:], op=ALU.mult)
                V("scalar_tensor_tensor", ["t1s", "dt8", "cm"], ["MT"], out=MT[:, h, :], in0=t1s[:], scalar=dt8[:, col:col + 1], in1=tri,
                  op0=ALU.mult, op1=ALU.mult)
                MM(e_t[:, 256 + 64 * h:256 + 64 * h + 64], MT[:, h, :], xs_tok[:, 64 * h:64 * h + 64], ["MT", "xs_tok"], [e_k])
            MM(e_t[:, 384:512], fT[2][:, cc], Hbf[:], ["fT2", "Hbf"], [e_k])
            for h in range(2):
                col = 2 * s + h
                V("tensor_scalar", ["xs_tok", "wl8"], ["Xd"], out=Xd[:, 64 * h:64 * h + 64], in0=xs_tok[:, 64 * h:64 * h + 64],
                  scalar1=wl8[:, col:col + 1], scalar2=None, op0=ALU.mult)
            MM(cb_t[:, 128:256], B_tok[:], Xd[:], ["B_tok", "Xd"], [cb_k])
            for h in range(2):
                col = 2 * s + h
                hs = slice(64 * h, 64 * h + 64)
                V("tensor_scalar", [e_k, "ea8"], ["ytmp"], out=ytmp[:, hs], in0=e_t[:, 384 + 64 * h:384 + 64 * h + 64],
                  scalar1=ea8[:, col:col + 1], scalar2=None, op0=ALU.mult)
                V("tensor_tensor", ["ytmp", e_k], ["ytmp"], out=ytmp[:, hs], in0=ytmp[:, hs], in1=e_t[:, 256 + 64 * h:256 + 64 * h + 64], op=ALU.add)
                V("scalar_tensor_tensor", ["xs_tok", "rowc", "ytmp"], ["ytmp"], out=ytmp[:, hs], in0=xs_tok[:, hs], scalar=dsk[:, h:h + 1],
                  in1=ytmp[:, hs], op0=ALU.mult, op1=ALU.add)
                V("scalar_tensor_tensor", ["Hs", "eal8", cb_k], ["Hs"], out=Hs[:, hs], in0=Hs[:, hs], scalar=eal8[:, col:col + 1],
                  in1=cb_t[:, 128 + 64 * h:128 + 64 * h + 64], op0=ALU.mult, op1=ALU.add)
            V("tensor_copy", ["Hs"], ["Hbf"], out=Hbf[:], in_=Hs[:])
            V("tensor_tensor", ["ytmp", "szt"], ["ytmp"], out=ytmp[:], in0=ytmp[:], in1=szt[:, s, :], op=ALU.mult)
            A(junk[:, 0:128], ytmp[:], AF.Square, ["ytmp"], ["junk", "st8s"], accum_out=st8[:, 6:7])
            rstd_from(st8[:, 6:7], 1, 128.0, "st8s")
            V("scalar_tensor_tensor", ["ytmp", "st8s", "rowc"], ["yd"], out=yd[:], in0=ytmp[:], scalar=st8[:, 6:7], in1=snorm,
              op0=ALU.mult, op1=ALU.mult)
            p_t, p_k = ptr.next()
            TR(p_t[:, 0, :], yd[:], ["yd"], [p_k])
            A(mx[:, 3, cc], p_t[:, 0, :], AF.Copy, [p_k], [mxk])

        bg["side_done"][t] = True
        flush_to(1)
        bg["nticks"] = max(1, bg["ticks"] - ticks0)
    flush_to(0)
    return [out_key]


FH = 2816
NFC = 22


def emit_ffn(nc, st, P, NTOK, final, mixTin, xres, wo, wg, wu, wd, colc2, fnorm, xo, pfx="f", mix_all=None, sel=None,
             xres_keys=(), mix_key=None, out_key="xo_out", tile_hook=None, g2row=None):
    _n = [0]

    def sb(shape, dt, name=None):
        _n[0] += 1
        return st.enter_context(nc.sbuf_tensor("%s%s_%d" % (pfx, name or "t", _n[0]), shape, dt))

    def ps(shape, dt, name=None):
        _n[0] += 1
        return st.enter_context(nc.psum_tensor("%s%s_%d" % (pfx, name or "p", _n[0]), shape, dt))

    def A(out, in_, func, r, w, **kw):
        return P.op("act", lambda e: e.activation(out=out, in_=in_, func=func, **kw), reads=r, writes=w)

    def V(name, r, w, **kw):
        return P.op("dve", lambda e: getattr(e, name)(**kw), reads=r, writes=w)

    def G(name, r, w, **kw):
        return P.op("pool", lambda e: getattr(e, name)(**kw), reads=r, writes=w)

    def MM(out, lhsT, rhs, r, w, start=True, stop=True, inc=None):
        return P.op("pe", lambda e: e.matmul(out=out, lhsT=lhsT, rhs=rhs, start=start, stop=stop), reads=r, writes=w,
                    inc=stop if inc is None else inc)

    c2 = sb([128, 8], F32, "c2"); identf = sb([128, 128], F32, "identf"); identb = sb([128, 128], BF16, "identb")
    P.dma(c2[:], colc2[:, 0:8], writes=["c2"])
    P.dma(identf[:], colc2[:, 8:136], writes=["identf"])
    V("tensor_copy", ["identf"], ["identb"], out=identb[:], in_=identf[:])

    def TR(out, in_, r, w, inc=True):
        return P.op("pe", lambda e: e.transpose(out=out, in_=in_, identity=identb[:]), reads=list(r) + ["identb"], writes=w, inc=inc)

    Wob = sb([128, 8, 1024], BF16, "Wob"); Wgb = sb([128, 8, FH], BF16, "Wgb"); Wub = sb([128, 8, FH], BF16, "Wub")
    Wdb = sb([128, NFC, 1024], BF16, "Wdb")
    g2rep = sb([128, 1024], F32, "g2rep")
    P.dma(g2rep[:], g2row.partition_broadcast(128), writes=["g2rep"], q="sp")

    def load_cast(dst, src, n, scale_ap, wkey):
        P.dma(dst, src, writes=[wkey], q="pool")

    for kc in range(8):
        load_cast(Wob[:, kc, :], wo[kc * 128:(kc + 1) * 128, :], 1024, None, "Wob")

    def load_rest_of_weights():
        for bi, (c0, n) in enumerate(((0, 1024), (1024, 1024), (2048, 768))):
            for kc in range(8):
                load_cast(Wgb[:, kc, c0:c0 + n], wg[kc * 128:(kc + 1) * 128, c0:c0 + n], n, c2[:, kc:kc + 1], ("Wgb", bi))
                load_cast(Wub[:, kc, c0:c0 + n], wu[kc * 128:(kc + 1) * 128, c0:c0 + n], n, c2[:, kc:kc + 1], ("Wub", bi))
        for fc in range(NFC):
            load_cast(Wdb[:, fc, :], wd[fc * 128:(fc + 1) * 128, :], 1024, None, ("Wdb", fc))

    gen = Rot([(ps([128, 512], F32, "g%d" % i), "fpsg%d" % i) for i in range(6)])
    ptr = Rot([(ps([128, 8, 128], BF16, "t%d" % i), "fpst%d" % i) for i in range(2)])
    TT = 512
    NS = TT // 128
    HFC = NFC // 2
    mixin = sb([128, 8, TT], BF16, "mixin"); mik = "mixin"
    x1 = sb([128, NS, 1024], F32, "x1")
    h2b = sb([128, 1024], BF16, "h2b")
    h2T = sb([128, 8, TT], BF16, "h2T"); actT = sb([128, HFC, TT], BF16, "actT")
    st8 = sb([128, 8], F32, "st8")

    def rstd_from(ssap, dim, key):
        A(ssap, ssap, AF.Ln, [key], [key], scale=1.0 / dim, bias=1e-6)
        A(ssap, ssap, AF.Exp, [key], [key], scale=-0.5)

    if mix_all is None:
        mview = mixTin.rearrange("(c p) s -> p c s", p=128)
    else:
        NTH = NTOK // 512
        selt = sb([128, 2], F32, "selt")
        P.dma(selt[:], sel, writes=["selt"], q="sp")
        candB_t = sb([128, 8, 256], BF16, "candB")
        candB = candB_t[:]
        candBk = "candB"
    fn_t = None

    def load_mix(t):
        t0 = t * TT
        if mix_all is None:
            P.dma(mixin[:], mview[:, :, t0:t0 + TT], writes=[mik], q="sp")
        else:
            for hf in range(2):
                c0 = hf * 256
                dsts = ((mixin[:, :, c0:c0 + 256], mik), (candB, candBk))
                for h in range(2):
                    for q, (dst, dk) in enumerate(dsts):
                        T = q * NTH + t
                        P.dma(dst.rearrange("p (m h) s -> p m h s", h=2)[:, :, h, :],
                              mix_all[T, h].rearrange("(m p) s -> p m s", p=128)[:, :, c0:c0 + 256],
                              reads=[(mix_key, T)], writes=[dk], q="sp")
                V("tensor_scalar", [mik, "selt"], [mik], out=mixin[:, :, c0:c0 + 256], in0=mixin[:, :, c0:c0 + 256], scalar1=selt[:, 0:1],
                  scalar2=None, op0=ALU.mult)
                V("scalar_tensor_tensor", [candBk, "selt", mik], [mik], out=mixin[:, :, c0:c0 + 256], in0=candB, scalar=selt[:, 1:2],
                  in1=mixin[:, :, c0:c0 + 256], op0=ALU.mult, op1=ALU.add)

    def load_x(t):
        t0 = t * TT
        for s in range(NS):
            P.dma(x1[:, s, :], xres[t0 + s * 128:t0 + (s + 1) * 128, :], reads=[(k_, t) for k_ in xres_keys] + [("x1", s)],
                  writes=[("x1", s)], q="sp")

    NTILES = NTOK // TT
    load_mix(0)
    load_x(0)
    load_rest_of_weights()
    if final:
        fn_t = sb([128, 1024], F32, "fnrep"); fn_k = "fnrep"
        P.dma(fn_t[:], fnorm.partition_broadcast(128), writes=[fn_k], q="sp")
    for t in range(NTOK // TT):
        t0 = t * TT
        if t > 0:
            load_x(t)
        for s in range(NS):
            for n in range(2):
                g_t, g_k = gen.next()
                for kc in range(8):
                    MM(g_t[:, :], mixin[:, kc, s * 128:(s + 1) * 128], Wob[:, kc, n * 512:(n + 1) * 512], [mik, "Wob"], [g_k],
                       start=(kc == 0), stop=(kc == 7))
                V("tensor_tensor", [g_k, ("x1", s)], [("x1", s)], out=x1[:, s, n * 512:(n + 1) * 512], in0=x1[:, s, n * 512:(n + 1) * 512], in1=g_t[:, :], op=ALU.add)
            A(h2b[:], x1[:, s, :], AF.Square, [("x1", s)], ["h2b", "st8a"], accum_out=st8[:, 0:1])
            rstd_from(st8[:, 0:1], 1024.0, "st8a")
            V("scalar_tensor_tensor", [("x1", s), "st8a", "g2rep"], ["h2b"], out=h2b[:], in0=x1[:, s, :], scalar=st8[:, 0:1], in1=g2rep[:],
              op0=ALU.mult, op1=ALU.mult)
            p_t, p_k = ptr.next()
            for kc in range(8):
                TR(p_t[:, kc, :], h2b[:, kc * 128:(kc + 1) * 128], ["h2b"], [p_k], inc=(kc == 7))
            A(h2T[:, :, s * 128:(s + 1) * 128], p_t[:], AF.Copy, [p_k], ["h2T"])
        if t + 1 < NTILES:
            load_mix(t + 1)
        for fh in range(2):
            for fi in range(HFC):
                fc = fh * HFC + fi
                pg, pgk = gen.next()
                pu, puk = gen.next()
                for kc in range(8):
                    MM(pg[:, :], Wgb[:, kc, fc * 128:(fc + 1) * 128], h2T[:, kc, :], [("Wgb", fc // 8), "h2T"], [pgk], start=(kc == 0), stop=(kc == 7))
                for kc in range(8):
                    MM(pu[:, :], Wub[:, kc, fc * 128:(fc + 1) * 128], h2T[:, kc, :], [("Wub", fc // 8), "h2T"], [puk], start=(kc == 0), stop=(kc == 7))
                A(actT[:, fi, :], pg[:, :], AF.Silu, [pgk], [("actT", fi)])
                V("tensor_tensor", [("actT", fi), puk], [("actT", fi)], out=actT[:, fi, :], in0=actT[:, fi, :], in1=pu[:, :], op=ALU.mult)
            for s in range(NS):
                for n in range(2):
                    pd, pdk = gen.next()
                    for fi in range(HFC):
                        fc = fh * HFC + fi
                        MM(pd[:, :], actT[:, fi, s * 128:(s + 1) * 128], Wdb[:, fc, n * 512:(n + 1) * 512], [("actT", fi), ("Wdb", fc)], [pdk],
                           start=(fi == 0), stop=(fi == HFC - 1))
                    V("tensor_tensor", [pdk, ("x1", s)], [("x1", s)], out=x1[:, s, n * 512:(n + 1) * 512], in0=x1[:, s, n * 512:(n + 1) * 512], in1=pd[:, :], op=ALU.add)
        for s in range(NS):
            if final:
                A(h2b[:], x1[:, s, :], AF.Square, [("x1", s)], ["h2b", "st8b"], accum_out=st8[:, 1:2])
                rstd_from(st8[:, 1:2], 1024.0, "st8b")
                V("scalar_tensor_tensor", [("x1", s), "st8b", fn_k], [("x1", s)], out=x1[:, s, :], in0=x1[:, s, :], scalar=st8[:, 1:2], in1=fn_t[:],
                  op0=ALU.mult, op1=ALU.mult)
            P.dma(xo[t0 + s * 128:t0 + (s + 1) * 128, :], x1[:, s, :], reads=[("x1", s)], writes=[(out_key, t)], q="pool")
        if tile_hook is not None:
            tile_hook(t)
    return [(out_key, j) for j in range(NTOK // TT)]


def consts_cmat():
    ident = np.eye(128, dtype=np.float32)
    tri = np.triu(np.ones((128, 128), np.float32))
    su = np.tril(np.ones((128, 128), np.float32), -1)
    madd = np.where(np.arange(128)[:, None] <= np.arange(128)[None, :], 0.0, -30000.0).astype(np.float32)
    ones = np.ones((128, 128), np.float32)
    return np.ascontiguousarray(np.concatenate([ident, tri, su, madd, ones], axis=1))

def mixer_inputs(inp, l, hh, xb, posb):
    W = inp["w_in"][l]
    hA, hB = 2 * hh, 2 * hh + 1
    r = lambda a, n: np.arange(a, a + n)
    fq = lambda h: r(64 * h, 64); fk = lambda h: r(256 + 64 * h, 64); fv = lambda h: r(512 + 64 * h, 64); ff = lambda h: r(768 + h, 1)
    gq = lambda h: r(772 + 32 * h, 32); gk = lambda h: r(900 + 32 * h, 32); gv = lambda h: r(1028 + 64 * h, 64); gr = lambda h: r(1284 + 64 * h, 64)
    gate = r(1540, 16); mcq = r(1556, 256); mckv = r(1812, 128); mkr = r(1940, 32)
    sz = lambda h: r(1972 + 64 * h, 64); sx = lambda h: r(2228 + 64 * h, 64)
    sB = r(2484 + 128 * hh, 128); sC = r(2740 + 128 * hh, 128); sdt = lambda h: r(2996 + h, 1)
    Z = lambda n: np.zeros((1024, n), np.float32)
    cols = [W[:, fq(hA)], W[:, fq(hB)],
            W[:, fk(hA)], W[:, fk(hB)],
            W[:, ff(hA)], W[:, ff(hB)], Z(30), W[:, gate],
            W[:, sx(hA)], W[:, sx(hB)], W[:, sB], W[:, sC],
            W[:, gq(hA)], W[:, gq(hB)], Z(32), W[:, mkr],
            Z(96), W[:, mkr[16:32]], W[:, mkr[0:16]],
            W[:, fv(hA)], W[:, fv(hB)], W[:, gk(hA)], W[:, gk(hB)], W[:, gv(hA)], W[:, gv(hB)], W[:, gr(hA)], W[:, gr(hB)],
            W[:, mcq], W[:, mckv],
            W[:, sz(hA)], W[:, sz(hB)], W[:, sdt(hA)], W[:, sdt(hB)]]
    w_all = np.ascontiguousarray(np.concatenate(cols, axis=1))
    assert w_all.shape == (1024, 1906), w_all.shape
    colc = np.zeros((128, 28), np.float32)
    colc[:, 0:8] = inp["norm1"][l].reshape(8, 128).T
    colc[:, 8:10] = inp["mla_q_norm"][l].reshape(2, 128).T
    colc[:, 10] = inp["mla_kv_norm"][l]
    cwl = inp["ssm_conv_w"][l]; cbl = inp["ssm_conv_b"][l]
    ccols = [np.concatenate([r(64 * hA, 64), r(64 * hB, 64)]), r(256 + 128 * hh, 128), r(512 + 128 * hh, 128)]
    for g in range(3):
        colc[:, 11 + 4 * g:15 + 4 * g] = cwl[:, ccols[g]].T
        colc[:, 23 + g] = cbl[ccols[g]]
    half = 16
    inv = (10000.0 ** (-np.arange(half, dtype=np.float32) / half)).astype(np.float32)
    colc[96:112, 26] = -inv; colc[112:128, 26] = inv
    colc[0, 27] = inp["fox_f_bias"][l][hA]; colc[1, 27] = inp["fox_f_bias"][l][hB]
    rowc = np.zeros((1, 326), np.float32)
    rowc[0, 0:64] = inp["gla_gate_b"][l][np.concatenate([gq(hA), gq(hB)]) - 772]
    rowc[0, 64:128] = inp["gla_out_norm"][l]; rowc[0, 128:192] = inp["gla_out_norm"][l]
    rowc[0, 192:194] = inp["ssm_dt_bias"][l][[hA, hB]]
    rowc[0, 194:196] = inp["ssm_A_log"][l][[hA, hB]]
    rowc[0, 196:198] = inp["ssm_D"][l][[hA, hB]]
    rowc[0, 198:326] = inp["ssm_norm"][l][128 * hh:128 * hh + 128]
    w2 = np.ascontiguousarray(inp["gla_gate_w2"][l][:, np.concatenate([gq(hA), gq(hB)]) - 772])
    Wq = inp["mla_w_uq"][l]
    qc = []
    for h in (hA, hB):
        base = 96 * h
        z32 = np.zeros((256, 32), np.float32); z96 = np.zeros((256, 96), np.float32)
        qc += [Wq[:, base:base + 64], z32, Wq[:, base + 64:base + 96], z96, Wq[:, base + 80:base + 96], Wq[:, base + 64:base + 80]]
    wuq = np.ascontiguousarray(np.concatenate(qc, axis=1)); assert wuq.shape == (256, 512)
    Wkv = inp["mla_w_ukv"][l]
    wukv = np.ascontiguousarray(np.concatenate([Wkv[:, 128 * hA:128 * hA + 64], Wkv[:, 128 * hB:128 * hB + 64],
                                                Wkv[:, 128 * hA + 64:128 * hA + 128], Wkv[:, 128 * hB + 64:128 * hB + 128]], axis=1))
    d = dict(w_all=w_all, colc=colc, rowc=rowc, w2=w2, wuq=wuq, wukv=wukv)
    if xb is not None:
        d.update(xin=np.ascontiguousarray(xb), pos=np.ascontiguousarray(posb.reshape(1, -1).astype(np.int32)), cmat=consts_cmat())
    return d


_CACHE = {}
PAIRS = [[0, 1], [2, 3], [4, 5], [6, 7]]


def _build_fused_nc(NT):
    S = NT * 512
    H = S // 2
    nc = bass.Bass("TRN2", target_bir_lowering=False, num_devices=8)
    di = lambda name, shape, dt=F32: nc.dram_tensor(name, shape, dt, kind="ExternalInput").ap()
    xin = di("xin", [S, 1024]); xhalf = di("xhalf", [H, 1024]); pos = di("pos", [1, S], I32)
    cmat = di("cmat", [128, 640]); sel = di("sel", [128, 2]); fnorm = di("fnorm", [1, 1024])
    L = []
    for l in range(2):
        L.append(dict(w_all=di("w_all%d" % l, [1024, NW]), colc=di("colc%d" % l, [128, NCOL]), rowc=di("rowc%d" % l, [1, NROW]),
                      w2=di("w2%d" % l, [16, 64]), wuq=di("wuq%d" % l, [256, 512]), wukv=di("wukv%d" % l, [128, 256]),
                      wo=di("wo%d" % l, [1024, 1024]), wg=di("wg%d" % l, [1024, 2816]), wu=di("wu%d" % l, [1024, 2816]),
                      wd=di("wd%d" % l, [2816, 1024]), colc2=di("colc2%d" % l, [128, 136]), g2row=di("g2row%d" % l, [1, 1024])))
    xo = nc.dram_tensor("xo", [H, 1024], F32, kind="ExternalOutput").ap()
    NJ = H // 512
    mx_loc = [nc.dram_tensor("mx_loc%d" % l, [NT, 512, 512], BF16).ap() for l in range(2)]
    mx_all = [nc.dram_tensor("mx_all%d" % l, [NT, 2, 512, 512], BF16).ap() for l in range(2)]
    xn_loc = nc.dram_tensor("xn_loc", [H, 1024], F32).ap()
    xn_all = nc.dram_tensor("xn_all", [NJ, 2, 512, 1024], F32).ap()
    with contextlib.ExitStack() as st0:
        P = Prog(nc, st0)
        P.dma_queues = ["sp"]
        for l in range(2):
            w = L[l]

            def mix_hook(t, l=l):
                P.collective("AllGather", [mx_loc[l][t]], [mx_all[l][t].rearrange("r f s -> (r f) s")], reads=[("mx_loc%d" % l, t)],
                             writes=[("mx_all%d" % l, t)], groups=PAIRS)

            def x_hook(j):
                P.collective("AllGather", [xn_loc[j * 512:(j + 1) * 512, :]], [xn_all[j].rearrange("r t d -> (r t) d")],
                             reads=[("xn_loc", j)], writes=[("xn_all", j)], groups=PAIRS)

            def xin_fn(t, s):
                r, j = t // NJ, t % NJ
                return xn_all[j, r, s * 128:(s + 1) * 128, :], [("xn_all", j)]

            with contextlib.ExitStack() as st:
                emit_mixer(nc, st, P, NT, xin, pos, w["w_all"], w["colc"], w["rowc"], cmat, w["w2"], w["wuq"], w["wukv"],
                           mx_loc[l], pfx="m%d" % l, xin_fn=None if l == 0 else xin_fn, out_key="mx_loc%d" % l, tile_major=True,
                           tile_hook=mix_hook)
                P.emit()
            with contextlib.ExitStack() as st:
                fk = emit_ffn(nc, st, P, H, l == 1, None, xhalf if l == 0 else xn_loc, w["wo"], w["wg"], w["wu"], w["wd"], w["colc2"], fnorm,
                              xn_loc if l == 0 else xo, pfx="f%d" % l, mix_all=mx_all[l], sel=sel,
                              xres_keys=() if l == 0 else ("xn_loc",), mix_key="mx_all%d" % l, out_key="xn_loc" if l == 0 else "xo_out",
                              tile_hook=x_hook if l == 0 else None, g2row=w["g2row"])
                if l == 1:
                    P.finish(fk)
                P.emit()
    return nc


def kernel(**inputs):
    inp = {k: np.asarray(v) for k, v in inputs.items()}
    B, S = 4, 8192
    NT = S // 512
    x = np.ascontiguousarray(inp["x"], dtype=np.float32)
    pos = inp["positions"]
    if "fused" not in _CACHE:
        _CACHE["fused"] = _build_fused_nc(NT)
    cm = consts_cmat()
    fn = np.ascontiguousarray(inp["final_norm"].reshape(1, 1024))
    maps = []
    for c in range(8):
        b, hh = c // 2, c % 2
        m = dict(xin=x[b], xhalf=np.ascontiguousarray(x[b][hh * (S // 2):(hh + 1) * (S // 2)]),
                 pos=np.ascontiguousarray(pos[b].reshape(1, -1).astype(np.int32)), cmat=cm, fnorm=fn)
        sel = np.zeros((128, 2), np.float32); sel[:, hh] = 1.0
        m["sel"] = sel
        for l in range(2):
            mi = mixer_inputs(inp, l, hh, None, None)
            for k in ("w_all", "colc", "rowc", "w2", "wuq", "wukv"):
                m["%s%d" % (k, l)] = mi[k]
            colc2 = np.zeros((128, 136), np.float32)
            colc2[:, 0:8] = inp["norm2"][l].reshape(8, 128).T
            colc2[:, 8:136] = np.eye(128, dtype=np.float32)
            m["wo%d" % l] = inp["w_out"][l]; m["wg%d" % l] = inp["w_gate"][l]; m["wu%d" % l] = inp["w_up"][l]
            m["wd%d" % l] = inp["w_down"][l]; m["colc2%d" % l] = colc2
            m["g2row%d" % l] = np.ascontiguousarray(inp["norm2"][l].reshape(1, 1024))
        maps.append(m)
    res = run_bass_kernel_spmd(_CACHE["fused"], maps, core_ids=list(range(8)))
    out = np.empty((B, S, 1024), np.float32)
    for c in range(8):
        b, hh = c // 2, c % 2
        out[b, hh * (S // 2):(hh + 1) * (S // 2)] = res.results[c]["xo"]
    return out
```

```python
import contextlib, math
import numpy as np
import ml_dtypes
import concourse.bass as bass
import concourse.mybir as mybir
from concourse.bass_utils import run_bass_kernel_spmd


F32 = mybir.dt.float32
BF16 = mybir.dt.bfloat16
I32 = mybir.dt.int32
AF = mybir.ActivationFunctionType
ALU = mybir.AluOpType
AX = mybir.AxisListType


class Prog:
    COMPUTE = ("pe", "act", "dve", "pool")
    NDMASEM = 8

    def __init__(self, nc, stack, dma_queues=("sp", "pool")):
        self.nc = nc
        self._stack = stack
        self.eng = {"pe": nc.tensor, "act": nc.scalar, "dve": nc.vector,
                    "pool": nc.gpsimd, "sp": nc.sync}
        self.streams = {e: [] for e in self.eng}
        self.csem = {e: stack.enter_context(nc.semaphore("c_" + e)) for e in self.COMPUTE}
        self.ccount = {e: 0 for e in self.COMPUTE}
        self.dsem = {q: [stack.enter_context(nc.semaphore("d_%s%d" % (q, i)))
                         for i in range(self.NDMASEM)] for q in dma_queues}
        self.dcount = {q: 0 for q in dma_queues}
        self.waited = {e: {} for e in self.eng}
        self.semobj = {}
        self.last_w = {}
        self.readers = {}
        self.nwaits = 0
        self.nops = 0
        self.pending = {e: False for e in self.COMPUTE}
        self.dma_rr = 0
        self.dma_queues = list(dma_queues)

    def _sid(self, sem):
        i = id(sem)
        self.semobj[i] = sem
        return i

    def _deps(self, reads, writes):
        deps = []
        for k in reads:
            w = self.last_w.get(k)
            if w is not None:
                deps.append(w)
        for k in writes:
            w = self.last_w.get(k)
            if w is not None:
                deps.append(w)
            deps.extend(self.readers.get(k, ()))
        return deps

    def _emit_waits(self, e, deps, own=None):
        wl = []
        wd = self.waited[e]
        best = {}
        own_sid = id(self.csem[e]) if e in self.csem else None
        for (sid, val) in deps:
            if sid == own_sid and val > self.ccount[e]:
                continue
            if wd.get(sid, 0) >= val:
                continue
            if best.get(sid, 0) < val:
                best[sid] = val
        for sid, val in best.items():
            wd[sid] = val
            wl.append((self.semobj[sid], val))
        return wl

    def op(self, e, fn, reads=(), writes=(), inc=True):
        assert e in self.COMPUTE
        pr = [k for k in reads if isinstance(k, str) and (k.startswith("ps") or k.startswith("fps"))]
        if pr:
            writes = list(writes) + pr
        deps = self._deps(reads, writes)
        sem = self.csem[e]
        sid = self._sid(sem)
        wl = self._emit_waits(e, deps)
        if inc:
            self.ccount[e] += 1
            val = self.ccount[e]
        else:
            val = self.ccount[e] + 1
        self.pending[e] = not inc
        self.nwaits += len(wl)
        self.nops += 1

        def run(eng, wl=wl, fn=fn, sem=sem, inc=inc):
            for (s, v) in wl:
                eng.wait_ge(s, v)
            ins = fn(eng)
            if inc:
                ins.then_inc(sem, 1)
        self.streams[e].append(run)
        tok = (sid, val)
        for k in reads:
            self.readers.setdefault(k, []).append(tok)
        for k in writes:
            self.last_w[k] = tok
            self.readers[k] = []
        return tok

    def dma(self, out, in_, reads=(), writes=(), q=None, **kw):
        if q is None:
            q = self.dma_queues[self.dma_rr % len(self.dma_queues)]
            self.dma_rr += 1
        n = self.dcount[q]
        self.dcount[q] += 1
        sem = self.dsem[q][n % self.NDMASEM]
        sid = self._sid(sem)
        val = 16 * (n // self.NDMASEM + 1)
        deps = self._deps(reads, writes)
        if n >= self.NDMASEM:
            deps.append((sid, val - 16))
        wl = self._emit_waits(q, deps)
        self.nwaits += len(wl)
        self.nops += 1

        def run(eng, wl=wl, sem=sem, out=out, in_=in_, kw=kw):
            for (s, v) in wl:
                eng.wait_ge(s, v)
            eng.dma_start(out=out, in_=in_, **kw).then_inc(sem, 16)
        self.streams[q].append(run)
        tok = (sid, val)
        for k in reads:
            self.readers.setdefault(k, []).append(tok)
        for k in writes:
            self.last_w[k] = tok
            self.readers[k] = []
        return tok

    def collective(self, kind, ins, outs, reads=(), writes=(), groups=None, **kw):
        q = "pool"
        if not hasattr(self, "ccsem"):
            self.ccsem = self._stack.enter_context(self.nc.semaphore("cc_sem"))
            self.cccount = 0
        sem = self.ccsem
        sid = self._sid(sem)
        self.cccount += 1
        val = self.cccount
        deps = self._deps(reads, writes)
        if val > 1:
            deps.append((sid, val - 1))
        wl = self._emit_waits(q, deps)
        self.nwaits += len(wl)
        self.nops += 1
        op = ALU.bypass if kind in ("AllGather", "AllToAll") else ALU.add

        def run(eng, wl=wl, sem=sem):
            for (s, v) in wl:
                eng.wait_ge(s, v)
            eng.collective_compute(kind, op, replica_groups=groups, ins=[a.opt() for a in ins], outs=[a.opt() for a in outs], **kw).then_inc(sem, 1)
        self.streams[q].append(run)
        tok = (sid, val)
        for k in reads:
            self.readers.setdefault(k, []).append(tok)
        for k in writes:
            self.last_w[k] = tok
            self.readers[k] = []
        return tok

    def finish(self, final_keys):
        deps = []
        for k in final_keys:
            w = self.last_w.get(k)
            if w is not None:
                deps.append(w)
        wl = self._emit_waits("sp", deps)

        def run(eng, wl=wl):
            for (s, v) in wl:
                eng.wait_ge(s, v)
        self.streams["sp"].append(run)

    def emit(self):
        nc = self.nc
        assert not any(self.pending.values()), self.pending
        streams = self.streams
        self.streams = {e: [] for e in self.eng}
        with nc.Block() as block:
            @block.sync
            def _(eng):
                for r in streams["sp"]:
                    r(eng)

            @block.tensor
            def _(eng):
                for r in streams["pe"]:
                    r(eng)

            @block.scalar
            def _(eng):
                for r in streams["act"]:
                    r(eng)

            @block.vector
            def _(eng):
                for r in streams["dve"]:
                    r(eng)

            @block.gpsimd
            def _(eng):
                for r in streams["pool"]:
                    r(eng)


NFM = 944
NTM = 962
NW = NFM + NTM
FM = {"fq": (0, 128), "fk": (128, 128), "fg": (256, 48), "sx": (304, 128), "sB": (432, 128),
      "sC": (560, 128), "gq": (688, 128), "kr2": (816, 128)}
TM = {"t1": (944, 448), "t2": (1392, 384), "t3": (1776, 130)}
NCOL = 28
NROW = 326
TWO_PI = 2.0 * math.pi
C1 = 6.28125
C2 = TWO_PI - C1


class Rot:
    def __init__(self, items):
        self.items = items
        self.i = 0

    def next(self):
        it = self.items[self.i % len(self.items)]
        self.i += 1
        return it


def emit_mixer(nc, st, P, NT, xin, pos, w_all, colc, rowc, cmat, w2, wuq, wukv, mixT, pfx="", xin_fn=None, out_key="mixT_out",
               tile_major=False, tile_hook=None):
    S = NT * 512
    NCH = NT * 4
    _n = [0]

    def sb(shape, dt, name=None):
        _n[0] += 1
        return st.enter_context(nc.sbuf_tensor("%s%s_%d" % (pfx, name or "t", _n[0]), shape, dt))

    def ps(shape, dt, name=None):
        _n[0] += 1
        return st.enter_context(nc.psum_tensor("%s%s_%d" % (pfx, name or "p", _n[0]), shape, dt))

    bg = {"q": [], "stride": 1, "burst": 1, "cnt": 0, "busy": False, "open": False, "ticks": 0, "done": [], "side_done": {}}

    def pump(n):
        bg["busy"] = True
        for _ in range(n):
            if not bg["q"]:
                break
            try:
                next(bg["q"][0])
            except StopIteration:
                bg["q"].pop(0)
        bg["busy"] = False

    def flush_to(keep):
        while len(bg["q"]) > keep:
            n0 = len(bg["q"])
            while len(bg["q"]) == n0:
                pump(1000)
        if tile_major and tile_hook is not None:
            while bg["done"]:
                tile_hook(bg["done"].pop(0))

    def tick():
        if bg["busy"]:
            return
        bg["ticks"] += 1
        if not bg["q"] or bg["open"]:
            return
        bg["cnt"] += 1
        if bg["cnt"] % bg["stride"] == 0:
            pump(bg["burst"])


    def A(out, in_, func, r, w, **kw):
        tk = P.op("act", lambda e: e.activation(out=out, in_=in_, func=func, **kw), reads=r, writes=w)
        tick()
        return tk

    def V(name, r, w, **kw):
        tk = P.op("dve", lambda e: getattr(e, name)(**kw), reads=r, writes=w)
        tick()
        return tk

    def G(name, r, w, **kw):
        return P.op("pool", lambda e: getattr(e, name)(**kw), reads=r, writes=w)

    def MM(out, lhsT, rhs, r, w, start=True, stop=True, inc=None):
        inc = stop if inc is None else inc
        tk = P.op("pe", lambda e: e.matmul(out=out, lhsT=lhsT, rhs=rhs, start=start, stop=stop), reads=r, writes=w, inc=inc)
        if not bg["busy"]:
            bg["open"] = not inc
            if inc:
                tick()
        return tk

    colc_sb = sb([128, NCOL], F32, "colc"); rowc_sb = sb([128, NROW], F32, "rowc")
    cm = sb([128, 640], F32, "cm"); identb = sb([128, 128], BF16, "identb")
    P.dma(colc_sb[:], colc, writes=["colc"])
    P.dma(rowc_sb[:], rowc.partition_broadcast(128), writes=["rowc"])
    P.dma(cm[:], cmat, writes=["cm"])
    identf = cm[:, 0:128]; tri = cm[:, 128:256]; su = cm[:, 256:384]; madd = cm[:, 384:512]; ones = cm[:, 512:640]
    V("tensor_copy", ["cm"], ["identb"], out=identb[:], in_=identf)
    mask2 = sb([128, 2, 128], F32, "mask2")
    V("tensor_copy", ["cm"], ["mask2"], out=mask2[:, 0, :], in_=tri)
    V("tensor_copy", ["cm", "mask2"], ["mask2"], out=mask2[:, 1, :], in_=tri)

    def TR(out, in_, r, w, inc=True):
        tk = P.op("pe", lambda e: e.transpose(out=out, in_=in_, identity=identb[:]), reads=list(r) + ["identb"], writes=w, inc=inc)
        if not bg["busy"]:
            bg["open"] = not inc
            if inc:
                tick()
        return tk

    g1 = colc_sb[:, 0:8]; qn = colc_sb[:, 8:10]; kvn = colc_sb[:, 10:11]
    cw = colc_sb[:, 11:23]; cb = colc_sb[:, 23:26]; inv = colc_sb[:, 26:27]; fb = colc_sb[:, 27:28]
    b2r = rowc_sb[:, 0:64]; gn2 = rowc_sb[:, 64:192]; dtb = rowc_sb[:, 192:194]; alog = rowc_sb[:, 194:196]
    dsk = rowc_sb[:, 196:198]; snorm = rowc_sb[:, 198:326]

    small = sb([128, 16], F32, "small")
    negfb = small[:, 0:1]; arep = small[:, 2:4]
    V("tensor_scalar", ["colc"], ["negfb"], out=negfb, in0=fb, scalar1=-1.0, scalar2=None, op0=ALU.mult)
    A(arep, alog, AF.Exp, ["rowc"], ["arep"])
    V("tensor_scalar", ["arep"], ["arep"], out=arep, in0=arep, scalar1=-1.0, scalar2=None, op0=ALU.mult)

    xt = Rot([(sb([128, 1024], F32, "xt"), "xt%d" % i) for i in range(2)])
    Wb = sb([128, 8, NW], BF16, "Wb")
    HALF = NW // 2
    for kc in range(8):
        for hf in range(2):
            t_, k_ = xt.next()
            P.dma(t_[:, 0:HALF], w_all[kc * 128:(kc + 1) * 128, hf * HALF:(hf + 1) * HALF], writes=[k_])
            V("tensor_scalar", [k_, "colc"], ["Wb"], out=Wb[:, kc, hf * HALF:(hf + 1) * HALF], in0=t_[:, 0:HALF],
              scalar1=g1[:, kc:kc + 1], scalar2=None, op0=ALU.mult)
    w2sb = sb([48, 64], F32, "w2sb")
    P.dma(w2sb[32:48, :], w2, writes=["w2sb"])
    wuqb = sb([128, 2, 512], BF16, "wuqb")
    t_, k_ = xt.next()
    wuqf = t_[:, 0:1024].rearrange("p (c n) -> p c n", c=2)
    P.dma(wuqf, wuq.rearrange("(c p) n -> p c n", p=128), writes=[k_])
    for c in range(2):
        V("tensor_scalar", [k_, "colc"], ["wuqb"], out=wuqb[:, c, :], in0=wuqf[:, c, :], scalar1=qn[:, c:c + 1],
          scalar2=None, op0=ALU.mult)
    wukvb = sb([128, 256], BF16, "wukvb")
    t_, k_ = xt.next()
    P.dma(t_[:, 0:256], wukv, writes=[k_])
    V("tensor_scalar", [k_, "colc"], ["wukvb"], out=wukvb[:], in0=t_[:, 0:256], scalar1=kvn, scalar2=None, op0=ALU.mult)

    SA = max(S, 8192)
    KTf = [sb([128, SA], BF16, "KTf%d" % h) for h in range(2)]
    KTm = [sb([128, SA], BF16, "KTm%d" % h) for h in range(2)]

    def carve(tile, r0, nr, slot, dt=F32, n=512):
        return tile[r0:r0 + nr, slot * 1024:slot * 1024 + (n * (4 if dt != BF16 else 2)) // 2].bitcast(dt) if dt != BF16 \
            else tile[r0:r0 + nr, slot * 1024:slot * 1024 + n]

    VAf = sb([128, NCH + 1, 2, 65], BF16, "VAf"); VAm = sb([128, NCH + 1, 2, 65], BF16, "VAm")
    for h in range(2):
        G("memset", [], ["KTf%d_init" % h], ap=KTf[h][64:70, :], constant=1.0)
    G("memset", [], ["VAf_init"], ap=VAf[:], constant=1.0)
    G("memset", [], ["VAm_init"], ap=VAm[:], constant=1.0)
    QTf2 = [[sb([70, 512], BF16, "QTf%d_%d" % (p_, h)) for h in range(2)] for p_ in range(2)]
    QTm2 = [[sb([96, 512], BF16, "QTm%d_%d" % (p_, h)) for h in range(2)] for p_ in range(2)]
    for p_ in range(2):
        for h in range(2):
            G("memset", [], ["QTf%d_%d" % (p_, h)], ap=QTf2[p_][h][64:70, :], constant=1.0)

    gen = Rot([(ps([128, 512], F32, "g%d" % i), "psg%d" % i) for i in range(3)])
    sps = Rot([(ps([128, 512], F32, "s%d" % i), "pss%d" % i) for i in range(2)])
    acc = Rot([(ps([128, 512], F32, "a%d" % i), "psa%d" % i) for i in range(1)])
    ptr = Rot([(ps([128, 8, 128], BF16, "t%d" % i), "pst%d" % i) for i in range(2)])

    junk = sb([128, 384], BF16, "junk")
    hb = sb([128, 1024], BF16, "hb")
    hT = sb([128, 8, 512], BF16, "hT")
    st8 = sb([128, 8], F32, "st8")
    PT = Rot([(sb([128, 512], BF16, "PT"), "PT%d" % i) for i in range(3)])
    osb = sb([65, 512], F32, "osb"); rcp = osb
    stmp = sb([128, 512], F32, "stmp")

    def silu_to(out_ap, x_ap, tmp_ap, rkeys, wkey, tmpkey):
        A(tmp_ap, x_ap, AF.Exp, list(rkeys), [tmpkey], scale=-1.0)
        V("tensor_scalar", [tmpkey], [tmpkey], out=tmp_ap, in0=tmp_ap, scalar1=1.0, scalar2=None, op0=ALU.add)
        V("reciprocal", [tmpkey], [tmpkey], out=tmp_ap, in_=tmp_ap)
        V("tensor_tensor", list(rkeys) + [tmpkey], [wkey], out=out_ap, in0=x_ap, in1=tmp_ap, op=ALU.mult)

    mixt = Rot([(sb([128, 4, 512], BF16, "mixt"), "mixt%d" % i) for i in range(2)])
    fe = carve(KTf[0], 96, 2, 0); fsp = carve(KTf[0], 96, 2, 1)
    Fc = Rot([(carve(KTf[0], 96, 2, 2 + i), "Fc%d" % i) for i in range(2)])
    fr1 = carve(KTf[0], 96, 2, 4); fr2 = carve(KTf[0], 96, 2, 5)
    ones2 = carve(KTf[0], 96, 2, 6)
    Fzero = carve(KTf[0], 96, 2, 7)
    Fq = KTf[1][96:98, 0:1536].rearrange("p (j n) -> p j n", j=3)
    Fk = KTf[1][96:98, 1536:3072].rearrange("p (j n) -> p j n", j=3)
    G("memset", [], ["Fzero"], ap=Fzero, constant=0.0)
    G("memset", [], ["ones2"], ap=ones2, constant=1.0)
    cn = sb([128, 384], BF16, "cn"); cnT = sb([128, 3, 512], BF16, "cnT")
    posi = carve(KTm[0], 96, 32, 0, I32); rti = posi
    ang = carve(KTm[0], 96, 32, 1); rtmp = carve(KTm[0], 96, 32, 2)
    cs = carve(KTm[0], 96, 32, 3); sn = carve(KTm[0], 96, 32, 4)
    krA = carve(KTm[1], 96, 32, 0); krB = carve(KTm[1], 96, 32, 1)
    gateT = sb([48, 512], F32, "gateT"); gqT = sb([64, 512], F32, "gqT")
    gt = sb([128, 64], F32, "gt"); gsp = sb([128, 64], F32, "gsp")
    eGT = sb([64, 128], F32, "eGT"); enG = sb([128, 64], F32, "enG")
    qtl = sb([64, 128], BF16, "qtl"); ktl = sb([128, 64], BF16, "ktl"); ktlT = sb([64, 128], BF16, "ktlT")
    Am = sb([128, 2, 128], BF16, "Am"); vb = sb([128, 128], BF16, "vb")
    Sst = sb([64, 128], F32, "Sst"); Sbf = sb([64, 128], BF16, "Sbf")
    sr = sb([128, 128], F32, "sr"); gtmp = sb([128, 128], F32, "gtmp"); yb = sb([128, 128], BF16, "yb")
    G("memset", [], ["Sst"], ap=Sst[:], constant=0.0)
    G("memset", [], ["Sbf"], ap=Sbf[:], constant=0.0)
    cin = [sb([128, 515], F32, "cin%d" % g) for g in range(3)]
    for g in range(3):
        G("memset", [], ["cin%d" % g], ap=cin[g][:, 0:3], constant=0.0)
    cacc = sb([128, 512], F32, "cacc")
    fT = [sb([128, 512], BF16, "fT%d" % g) for g in range(3)]
    xs_tok = sb([128, 128], BF16, "xs_tok"); B_tok = sb([128, 128], BF16, "B_tok")
    dtr = sb([128, 8], F32, "dtr"); dt8 = sb([128, 8], F32, "dt8"); a8 = sb([128, 8], F32, "a8")
    ea8 = sb([128, 8], F32, "ea8"); eal8 = sb([128, 8], F32, "eal8"); ds8 = sb([128, 8], F32, "ds8"); wl8 = sb([128, 8], F32, "wl8")
    Lh = sb([128, 128], F32, "Lh"); Dexp = sb([128, 128], F32, "Dexp"); t1s = sb([128, 128], F32, "t1s")
    MT = sb([128, 2, 128], BF16, "MT"); Xd = sb([128, 128], BF16, "Xd")
    Hs = sb([128, 128], F32, "Hs"); Hbf = sb([128, 128], BF16, "Hbf")
    ytmp = sb([128, 128], F32, "ytmp"); szt = sb([128, 4, 128], F32, "szt"); yd = sb([128, 128], BF16, "yd")
    G("memset", [], ["Hs"], ap=Hs[:], constant=0.0)
    G("memset", [], ["Hbf"], ap=Hbf[:], constant=0.0)

    def rstd_from(ssap, n, dim, key):
        A(ssap, ssap, AF.Ln, [key], [key], scale=1.0 / dim, bias=1e-6)
        A(ssap, ssap, AF.Exp, [key], [key], scale=-0.5)

    def attention(QT, qkey, KT, kkey, VA, vkey, h, t, scale, nrows, out_ap, out_keys):
        nk = 4 * t + 4
        a_t, a_k = acc.next()
        pend = None

        def issue_qk(j):
            i = j - 4 * t
            c0 = 0 if i < 0 else i * 128
            s_t, s_k = sps.next()
            MM(s_t[:, c0:512], KT[0:nrows, j * 128:(j + 1) * 128], QT[0:nrows, c0:512],
               [kkey(j // 4), qkey], [s_k])
            return (j, c0, s_t, s_k)

        nxt = issue_qk(0)
        for j in range(nk):
            cur = nxt
            if j + 1 < nk:
                nxt = issue_qk(j + 1)
            (_, c0, s_t, s_k) = cur
            if j >= 4 * t:
                V("tensor_tensor", [s_k, "cm"], [s_k], out=s_t[:, c0:c0 + 128], in0=s_t[:, c0:c0 + 128], in1=madd, op=ALU.add)
            p_t, p_k = PT.next()
            A(p_t[:, c0:512], s_t[:, c0:512], AF.Exp, [s_k], [p_k], scale=scale)
            vflat = VA[:, j:j + 2, :, :].rearrange("p c h d -> p (c h d)")[:, h * 65:h * 65 + 128]
            MM(a_t[:, c0:512], vflat, p_t[:, c0:512], [vkey(j // 4), vkey(min((j + 1) // 4, t)), p_k], [a_k], start=(j == 0), stop=(j == nk - 1))
            yield
        V("reciprocal", [a_k, "osb"], ["rcp"], out=rcp[64:65, :], in_=a_t[64:65, :])
        A(osb[0:64, :], a_t[0:64, :], AF.Copy, [a_k], ["osb"])
        b_t, b_k = sps.next()
        MM(b_t[0:64, :], ones[64:65, 0:64], rcp[64:65, :], ["cm", "rcp"], [b_k])
        V("tensor_tensor", ["osb", b_k, "rcp"], out_keys + ["rcp"], out=out_ap, in0=osb[0:64, :], in1=b_t[0:64, :], op=ALU.mult)
        yield

    def tile_attention(t, par, mx, mxk):
        for h in range(2):
            yield from attention(QTf2[par][h], "QTf%d_%d" % (par, h), KTf[h], lambda tt, h=h: ("KTf", h, tt), VAf, lambda tt: ("VAf", tt), h, t,
                                 1.0, 70, mx[64 * h:64 * h + 64, 0, :], [mxk])
        for h in range(2):
            yield from attention(QTm2[par][h], "QTm%d_%d" % (par, h), KTm[h], lambda tt, h=h: ("KTm", h, tt), VAm, lambda tt: ("VAm", tt), h, t,
                                 sc_mla, 96, mx[64 * h:64 * h + 64, 2, :], [mxk])
        while not bg["side_done"].get(t):
            yield
        if tile_major:
            P.dma(mixT[t].rearrange("(m p) s -> p m s", p=128), mx[:], reads=[mxk], writes=[(out_key, t)], q="pool")
        else:
            P.dma(mixT.rearrange("(m p) s -> p m s", p=128)[:, :, t * 512:(t + 1) * 512], mx[:], reads=[mxk], writes=[out_key], q="sp")
        bg["done"].append(t)
        yield

    Fprev = (Fzero, "Fzero")
    isq_fox = 0.125
    sc_mla = 96.0 ** -0.5
    sc_gq = 32.0 ** -0.5

    for t in range(NT):
        c512 = slice(t * 512, (t + 1) * 512)
        mx, mxk = mixt.next()
        par = t % 2
        QTf = QTf2[par]; QTm = QTm2[par]
        ticks0 = bg["ticks"]
        for s in range(4):
            x_t, x_k = xt.next()
            r0 = t * 512 + s * 128
            if xin_fn is None:
                P.dma(x_t[:], xin[r0:r0 + 128, :], writes=[x_k], q="sp")
            else:
                xap, xkeys = xin_fn(t, s)
                P.dma(x_t[:], xap, reads=list(xkeys), writes=[x_k], q="sp")
            A(hb[:], x_t[:], AF.Square, [x_k], ["hb", "st8a"], accum_out=st8[:, 0:1])
            rstd_from(st8[:, 0:1], 1, 1024.0, "st8a")
            V("tensor_scalar", [x_k, "st8a"], ["hb"], out=hb[:], in0=x_t[:], scalar1=st8[:, 0:1], scalar2=None, op0=ALU.mult)
            p_t, p_k = ptr.next()
            for kc in range(8):
                TR(p_t[:, kc, :], hb[:, kc * 128:(kc + 1) * 128], ["hb"], [p_k], inc=(kc == 7))
            V("tensor_copy", [p_k], ["hT"], out=hT[:, :, s * 128:(s + 1) * 128], in_=p_t[:])

        def fm_group(name):
            off, M = FM[name]
            g_t, g_k = gen.next()
            for kc in range(8):
                MM(g_t[0:M, :], Wb[:, kc, off:off + M], hT[:, kc, :], ["Wb", "hT"], [g_k], start=(kc == 0), stop=(kc == 7))
            return g_t, g_k

        def tm_group(name, s):
            off, N = TM[name]
            g_t, g_k = gen.next()
            for kc in range(8):
                MM(g_t[:, 0:N], hT[:, kc, s * 128:(s + 1) * 128], Wb[:, kc, off:off + N], ["Wb", "hT"], [g_k], start=(kc == 0), stop=(kc == 7))
            return g_t, g_k

        g_t, g_k = fm_group("fq")
        for h in range(2):
            A(QTf[h][0:64, :], g_t[64 * h:64 * h + 64, :], AF.Copy, [g_k], ["QTf%d_%d" % (par, h)], scale=isq_fox)
        g_t, g_k = fm_group("fk")
        for h in range(2):
            V("tensor_copy", [g_k, "KTf%d_init" % h], [("KTf", h, t)], out=KTf[h][0:64, c512], in_=g_t[64 * h:64 * h + 64, :])
        g_t, g_k = fm_group("fg")
        A(fe, g_t[0:2, :], AF.Exp, [g_k, "negfb"], ["fe"], scale=-1.0, bias=negfb[0:2, :])
        A(gateT[32:48, :], g_t[32:48, :], AF.Copy, [g_k], ["gateT"])
        A(fsp, fe, AF.Ln, ["fe"], ["fsp"], bias=1.0)
        F_t, F_k = Fc.next()
        V("tensor_tensor_scan", ["ones2", "fsp", Fprev[1]], [F_k], out=F_t, data0=ones2, data1=fsp,
          initial=Fprev[0][:, 0:1] if Fprev[1] == "Fzero" else Fprev[0][:, 511:512], op0=ALU.mult, op1=ALU.subtract)
        Fprev = (F_t, F_k)
        V("tensor_copy", [F_k], ["Fq"], out=Fq[:, 0, :], in_=F_t)
        V("tensor_tensor", [F_k, "Fq"], ["fr1"], out=fr1, in0=F_t, in1=Fq[:, 0, :], op=ALU.subtract)
        V("tensor_copy", ["fr1", "Fq"], ["Fq"], out=Fq[:, 1, :], in_=fr1)
        V("tensor_tensor", ["fr1", "Fq"], ["fr2"], out=fr2, in0=fr1, in1=Fq[:, 1, :], op=ALU.subtract)
        V("tensor_copy", ["fr2", "Fq"], ["Fq"], out=Fq[:, 2, :], in_=fr2)
        V("tensor_scalar", ["Fq"], ["Fk"], out=Fk, in0=Fq, scalar1=-1.0, scalar2=None, op0=ALU.mult)
        for h in range(2):
            for j in range(3):
                P.dma(QTf[h][64 + j:65 + j, :], Fq[h:h + 1, j, :], reads=["Fq", "QTf%d_%d" % (par, h)], writes=["QTf%d_%d" % (par, h)], q="sp")
                P.dma(KTf[h][67 + j:68 + j, c512], Fk[h:h + 1, j, :], reads=["Fk", "KTf%d_init" % h, ("KTf", h, t)],
                      writes=[("KTf", h, t)], q="sp")
        for g, name in enumerate(("sx", "sB", "sC")):
            g_t, g_k = fm_group(name)
            ck = "cin%d" % g
            A(cin[g][:, 3:515], g_t[:, :], AF.Copy, [g_k, ck], [ck])

        def conv_silu(g):
            ck = "cin%d" % g
            V("tensor_scalar", [ck, "colc"], ["cacc"], out=cacc[:], in0=cin[g][:, 0:512], scalar1=cw[:, 4 * g:4 * g + 1],
              scalar2=cb[:, g:g + 1], op0=ALU.mult, op1=ALU.add)
            for k in range(1, 4):
                V("scalar_tensor_tensor", [ck, "colc", "cacc"], ["cacc"], out=cacc[:], in0=cin[g][:, k:k + 512],
                  scalar=cw[:, 4 * g + k:4 * g + k + 1], in1=cacc[:], op0=ALU.mult, op1=ALU.add)
            V("tensor_copy", [ck], [ck], out=cin[g][:, 0:3], in_=cin[g][:, 512:515])
            silu_to(fT[g][:], cacc[:], stmp[:], ["cacc"], "fT%d" % g, "stmp")

        R = slice(96, 128)
        RO = slice(64, 96)
        P.dma(posi, pos[0:1, c512].partition_broadcast(32), reads=["posi"], writes=["posi"], q="sp")
        V("tensor_copy", ["posi"], ["ang"], out=ang, in_=posi)
        V("tensor_scalar", ["ang", "colc"], ["ang"], out=ang, in0=ang, scalar1=inv[R, :], scalar2=None, op0=ALU.mult)
        for which, tab, tkey in ((0, sn, "sn"), (1, cs, "cs")):
            shift = 0.0 if which == 0 else math.pi / 2
            V("tensor_scalar", ["ang"], ["rtmp"], out=rtmp, in0=ang, scalar1=shift, scalar2=1.0 / TWO_PI, op0=ALU.add, op1=ALU.mult)
            V("tensor_copy", ["rtmp", "posi"], ["posi"], out=rti, in_=rtmp)
            V("tensor_copy", ["posi"], ["rtmp"], out=rtmp, in_=rti)
            V("scalar_tensor_tensor", ["rtmp", "ang"], [tkey], out=tab, in0=rtmp, scalar=-C1, in1=ang, op0=ALU.mult, op1=ALU.add)
            V("scalar_tensor_tensor", ["rtmp", tkey], [tkey], out=tab, in0=rtmp, scalar=-C2, in1=tab, op0=ALU.mult, op1=ALU.add)
            V("tensor_scalar", [tkey], [tkey], out=tab, in0=tab, scalar1=shift, scalar2=3.14159, op0=ALU.add, op1=ALU.min)
            V("tensor_scalar", [tkey], [tkey], out=tab, in0=tab, scalar1=-3.14159, scalar2=None, op0=ALU.max)
            A(tab, tab, AF.Sin, [tkey], [tkey])
        g_t, g_k = fm_group("gq")
        A(gqT[:], g_t[0:64, :], AF.Copy, [g_k], ["gqT"], scale=sc_gq)
        V("tensor_tensor", [g_k, "cs"], ["krA"], out=krA, in0=g_t[R, :], in1=cs, op=ALU.mult)
        g_t, g_k = fm_group("kr2")
        V("tensor_tensor", [g_k, "sn"], ["krB"], out=krB, in0=g_t[R, :], in1=sn, op=ALU.mult)
        for h in range(2):
            V("tensor_tensor", ["krA", "krB"], [("KTm", h, t)], out=KTm[h][RO, c512], in0=krA, in1=krB, op=ALU.add)

        for s in range(4):
            ch = 4 * t + s
            g_t, g_k = tm_group("t1", s)
            for h in range(2):
                V("tensor_copy", [g_k, "VAf_init"], [("VAf", t)], out=VAf[:, ch, h, 0:64], in_=g_t[:, 64 * h:64 * h + 64])
            cc = slice(s * 128, (s + 1) * 128)
            gp_t, gp_k = gen.next()
            MM(gp_t[:, 0:64], gateT[32:48, cc], w2sb[32:48, :], ["gateT", "w2sb"], [gp_k])
            V("tensor_tensor", [gp_k, "rowc"], ["gt"], out=gt[:], in0=gp_t[:, 0:64], in1=b2r, op=ALU.add)
            A(gt[:], gt[:], AF.Exp, ["gt"], ["gt"], scale=-1.0)
            A(gsp[:], gt[:], AF.Ln, ["gt"], ["gsp"], bias=1.0)
            MM(gp_t[:, 128:192], tri, gsp[:], ["cm", "gsp"], [gp_k])
            MM(gp_t[0:64, 256:384], gsp[:], tri, ["cm", "gsp"], [gp_k])
            A(eGT[:], gp_t[0:64, 256:384], AF.Exp, [gp_k], ["eGT"], scale=-1.0 / 16)
            A(enG[:], gp_t[:, 128:192], AF.Exp, [gp_k], ["enG"], scale=1.0 / 16)
            V("tensor_tensor", ["gqT", "eGT"], ["qtl"], out=qtl[:], in0=gqT[:, cc], in1=eGT[:], op=ALU.mult)
            V("tensor_tensor", [g_k, "enG"], ["ktl"], out=ktl[:], in0=g_t[:, 128:192], in1=enG[:], op=ALU.mult)
            A(vb[:], g_t[:, 192:320], AF.Copy, [g_k], ["vb"])
            silu_to(sr[:], g_t[:, 320:448], sr[:], [g_k], "sr", "sr")
            p_t, p_k = ptr.next()
            TR(p_t[0:64, 0, :], ktl[:], ["ktl"], [p_k])
            A(ktlT[:], p_t[0:64, 0, :], AF.Copy, [p_k], ["ktlT"])
            a_t, a_k = gen.next()
            for h in range(2):
                hs = slice(32 * h, 32 * h + 32)
                MM(a_t[:, 128 * h:128 * h + 128], ktlT[hs, :], qtl[hs, :], ["ktlT", "qtl"], [a_k])
            V("tensor_tensor", [a_k, "mask2"], ["Am"], out=Am[:], in0=a_t[:, 0:256].rearrange("p (h n) -> p h n", h=2), in1=mask2[:], op=ALU.mult)
            for h in range(2):
                hs = slice(32 * h, 32 * h + 32)
                vs = slice(64 * h, 64 * h + 64)
                MM(a_t[:, 256 + 64 * h:256 + 64 * h + 64], Am[:, h, :], vb[:, vs], ["Am", "vb"], [a_k], start=True, stop=False)
                MM(a_t[:, 256 + 64 * h:256 + 64 * h + 64], qtl[hs, :], Sbf[hs, vs], ["qtl", "Sbf"], [a_k], start=False, stop=True)
            MM(gp_t[0:64, 384:512], ktl[:], vb[:], ["ktl", "vb"], [gp_k])
            V("tensor_scalar", ["Sst", "eGT"], ["Sst"], out=Sst[:], in0=Sst[:], scalar1=eGT[:, 127:128], scalar2=None, op0=ALU.mult)
            V("scalar_tensor_tensor", [gp_k, "eGT", "Sst"], ["Sst"], out=Sst[:], in0=gp_t[0:64, 384:512], scalar=eGT[:, 127:128],
              in1=Sst[:], op0=ALU.mult, op1=ALU.add)
            V("tensor_copy", ["Sst"], ["Sbf"], out=Sbf[:], in_=Sst[:])
            for h in range(2):
                A(gtmp[:, 64 * h:64 * h + 64], a_t[:, 256 + 64 * h:256 + 64 * h + 64], AF.Square, [a_k], ["gtmp", "st8g"],
                  accum_out=st8[:, 2 + h:3 + h])
            rstd_from(st8[:, 2:4], 2, 64.0, "st8g")
            for h in range(2):
                V("scalar_tensor_tensor", [a_k, "st8g", "rowc"], ["gtmp"], out=gtmp[:, 64 * h:64 * h + 64],
                  in0=a_t[:, 256 + 64 * h:256 + 64 * h + 64], scalar=st8[:, 2 + h:3 + h], in1=gn2[:, 64 * h:64 * h + 64],
                  op0=ALU.mult, op1=ALU.mult)
            V("tensor_tensor", ["gtmp", "sr"], ["yb"], out=yb[:], in0=gtmp[:], in1=sr[:], op=ALU.mult)
            p_t, p_k = ptr.next()
            TR(p_t[:, 0, :], yb[:], ["yb"], [p_k])
            A(mx[:, 1, cc], p_t[:, 0, :], AF.Copy, [p_k], [mxk])

            if s < 3:
                conv_silu(s)

            g_t, g_k = tm_group("t2", s)
            A(junk[:, 0:256], g_t[:, 0:256], AF.Square, [g_k], ["junk", "st8m"], accum_out=st8[:, 4:5])
            A(junk[:, 256:384], g_t[:, 256:384], AF.Square, [g_k], ["junk", "st8m"], accum_out=st8[:, 5:6])
            A(st8[:, 4:5], st8[:, 4:5], AF.Ln, ["st8m"], ["st8m"], scale=1.0 / 256, bias=1e-6)
            A(st8[:, 5:6], st8[:, 5:6], AF.Ln, ["st8m"], ["st8m"], scale=1.0 / 128, bias=1e-6)
            A(st8[:, 4:6], st8[:, 4:6], AF.Exp, ["st8m"], ["st8m"], scale=-0.5)
            V("tensor_scalar", [g_k, "st8m"], ["cn"], out=cn[:, 0:256], in0=g_t[:, 0:256], scalar1=st8[:, 4:5], scalar2=None, op0=ALU.mult)
            V("tensor_scalar", [g_k, "st8m"], ["cn"], out=cn[:, 256:384], in0=g_t[:, 256:384], scalar1=st8[:, 5:6], scalar2=None, op0=ALU.mult)
            p_t, p_k = ptr.next()
            for c in range(3):
                TR(p_t[:, c, :], cn[:, c * 128:(c + 1) * 128], ["cn"], [p_k], inc=(c == 2))
            A(cnT[:, :, cc], p_t[:, 0:3, :], AF.Copy, [p_k], ["cnT"])
            v_t, v_k = gen.next()
            p2_t, p2_k = p_t, p_k
            MM(v_t[:, 0:128], cnT[:, 2, cc], wukvb[:, 128:256], ["cnT", "wukvb"], [v_k])
            for h in range(2):
                V("tensor_copy", [v_k, "VAm_init"], [("VAm", t)], out=VAm[:, ch, h, 0:64], in_=v_t[:, 64 * h:64 * h + 64])

            g_t, g_k = tm_group("t3", s)
            silu_to(szt[:, s, :], g_t[:, 0:128], szt[:, s, :], [g_k], "szt", "szt")
            V("tensor_copy", [g_k], ["dtr"], out=dtr[:, 2 * s:2 * s + 2], in_=g_t[:, 128:130])

        for h in range(2):
            qa_t, qa_k = gen.next()
            for c in range(2):
                MM(qa_t[:, :], wuqb[:, c, 256 * h:256 * h + 128], cnT[:, c, :], ["wuqb", "cnT"], [qa_k], start=(c == 0), stop=(c == 1))
            qb_t, qb_k = gen.next()
            for c in range(2):
                MM(qb_t[:, :], wuqb[:, c, 256 * h + 128:256 * h + 256], cnT[:, c, :], ["wuqb", "cnT"], [qb_k], start=(c == 0), stop=(c == 1))
            qk = "QTm%d_%d" % (par, h)
            A(QTm[h][0:64, :], qa_t[0:64, :], AF.Copy, [qa_k], [qk])
            V("tensor_tensor", [qa_k, "cs"], ["krA"], out=krA, in0=qa_t[R, :], in1=cs, op=ALU.mult)
            V("tensor_tensor", [qb_k, "sn"], ["krB"], out=krB, in0=qb_t[R, :], in1=sn, op=ALU.mult)
            V("tensor_tensor", ["krA", "krB"], [qk], out=QTm[h][RO, :], in0=krA, in1=krB, op=ALU.add)
        kn_t, kn_k = gen.next()
        MM(kn_t[:, :], wukvb[:, 0:128], cnT[:, 2, :], ["wukvb", "cnT"], [kn_k])
        for h in range(2):
            A(KTm[h][0:64, c512], kn_t[64 * h:64 * h + 64, :], AF.Copy, [kn_k], [("KTm", h, t)])

        units = 16 * (t + 1) + 9
        bg["q"].append(tile_attention(t, par, mx, mxk))
        bg["stride"] = max(1, bg.get("nticks", 900) // units)
        bg["burst"] = max(1, -(-units // bg.get("nticks", 900)))
        bg["cnt"] = 0

        V("tensor_tensor", ["dtr", "rowc"], ["dt8"], out=dt8[:].rearrange("p (s h) -> p s h", h=2),
          in0=dtr[:].rearrange("p (s h) -> p s h", h=2), in1=dtb.unsqueeze(1).to_broadcast([128, 4, 2]), op=ALU.add)
        A(dt8[:], dt8[:], AF.Exp, ["dt8"], ["dt8"])
        A(dt8[:], dt8[:], AF.Ln, ["dt8"], ["dt8"], bias=1.0)
        V("tensor_tensor", ["dt8", "arep"], ["a8"], out=a8[:].rearrange("p (s h) -> p s h", h=2),
          in0=dt8[:].rearrange("p (s h) -> p s h", h=2), in1=arep.unsqueeze(1).to_broadcast([128, 4, 2]), op=ALU.mult)
        sc_t, sc_k = gen.next()
        MM(sc_t[:, 0:8], tri, a8[:], ["cm", "a8"], [sc_k])
        MM(sc_t[:, 8:16], ones, a8[:], ["cm", "a8"], [sc_k])
        A(ea8[:], sc_t[:, 0:8], AF.Exp, [sc_k], ["ea8"])
        A(eal8[:], sc_t[:, 8:16], AF.Exp, [sc_k], ["eal8"])
        A(wl8[:], sc_t[:, 0:8], AF.Copy, [sc_k], ["wl8"])
        V("tensor_tensor", [sc_k, "wl8"], ["ds8"], out=ds8[:], in0=sc_t[:, 8:16], in1=wl8[:], op=ALU.subtract)
        A(ds8[:], ds8[:], AF.Exp, ["ds8"], ["ds8"])
        V("tensor_tensor", ["ds8", "dt8"], ["wl8"], out=wl8[:], in0=ds8[:], in1=dt8[:], op=ALU.mult)

        for s in range(4):
            cc = slice(s * 128, (s + 1) * 128)
            p_t, p_k = ptr.next()
            TR(p_t[:, 0, :], fT[0][:, cc], ["fT0"], [p_k])
            TR(p_t[:, 1, :], fT[1][:, cc], ["fT1"], [p_k])
            A(xs_tok[:], p_t[:, 0, :], AF.Copy, [p_k], ["xs_tok"])
            A(B_tok[:], p_t[:, 1, :], AF.Copy, [p_k], ["B_tok"])
            cb_t, cb_k = gen.next()
            MM(cb_t[:, 0:128], fT[1][:, cc], fT[2][:, cc], ["fT1", "fT2"], [cb_k])
            e_t, e_k = gen.next()
            for h in range(2):
                col = 2 * s + h
                V("tensor_scalar", ["cm", "a8"], ["Lh"], out=Lh[:], in0=su, scalar1=a8[:, col:col + 1], scalar2=None, op0=ALU.mult)
                MM(e_t[:, 128 * h:128 * h + 128], Lh[:], tri, ["Lh", "cm"], [e_k])
                A(Dexp[:], e_t[:, 128 * h:128 * h + 128], AF.Exp, [e_k], ["Dexp"])
                V("tensor_tensor", [cb_k, "Dexp"], ["t1s"], out=t1s[:], in0=cb_t[:, 0:128], in1=Dexp[:], op=ALU.mult)
                V("scalar_tensor_tensor", ["t1s", "dt8", "cm"], ["MT"], out=MT[:, h, :], in0=t1s[:], scalar=dt8[:, col:col + 1], in1=tri,
                  op0=ALU.mult, op1=ALU.mult)
                MM(e_t[:, 256 + 64 * h:256 + 64 * h + 64], MT[:, h, :], xs_tok[:, 64 * h:64 * h + 64], ["MT", "xs_tok"], [e_k])
            MM(e_t[:, 384:512], fT[2][:, cc], Hbf[:], ["fT2", "Hbf"], [e_k])
            for h in range(2):
                col = 2 * s + h
                V("tensor_scalar", ["xs_tok", "wl8"], ["Xd"], out=Xd[:, 64 * h:64 * h + 64], in0=xs_tok[:, 64 * h:64 * h + 64],
                  scalar1=wl8[:, col:col + 1], scalar2=None, op0=ALU.mult)
            MM(cb_t[:, 128:256], B_tok[:], Xd[:], ["B_tok", "Xd"], [cb_k])
            for h in range(2):
                col = 2 * s + h
                hs = slice(64 * h, 64 * h + 64)
                V("tensor_scalar", [e_k, "ea8"], ["ytmp"], out=ytmp[:, hs], in0=e_t[:, 384 + 64 * h:384 + 64 * h + 64],
                  scalar1=ea8[:, col:col + 1], scalar2=None, op0=ALU.mult)
                V("tensor_tensor", ["ytmp", e_k], ["ytmp"], out=ytmp[:, hs], in0=ytmp[:, hs], in1=e_t[:, 256 + 64 * h:256 + 64 * h + 64], op=ALU.add)
                V("scalar_tensor_tensor", ["xs_tok", "rowc", "ytmp"], ["ytmp"], out=ytmp[:, hs], in0=xs_tok[:, hs], scalar=dsk[:, h:h + 1],
                  in1=ytmp[:, hs], op0=ALU.mult, op1=ALU.add)
                V("scalar_tensor_tensor", ["Hs", "eal8", cb_k], ["Hs"], out=Hs[:, hs], in0=Hs[:, hs], scalar=eal8[:, col:col + 1],
                  in1=cb_t[:, 128 + 64 * h:128 + 64 * h + 64], op0=ALU.mult, op1=ALU.add)
            V("tensor_copy", ["Hs"], ["Hbf"], out=Hbf[:], in_=Hs[:])
            V("tensor_tensor", ["ytmp", "szt"], ["ytmp"], out=ytmp[:], in0=ytmp[:], in1=szt[:, s, :], op=ALU.mult)
            A(junk[:, 0:128], ytmp[:], AF.Square, ["ytmp"], ["junk", "st8s"], accum_out=st8[:, 6:7])
            rstd_from(st8[:, 6:7], 1, 128.0, "st8s")
            V("scalar_tensor_tensor", ["ytmp", "st8s", "rowc"], ["yd"], out=yd[:], in0=ytmp[:], scalar=st8[:, 6:7], in1=snorm,
              op0=ALU.mult, op1=ALU.mult)
            p_t, p_k = ptr.next()
            TR(p_t[:, 0, :], yd[:], ["yd"], [p_k])
            A(mx[:, 3, cc], p_t[:, 0, :], AF.Copy, [p_k], [mxk])

        bg["side_done"][t] = True
        flush_to(1)
        bg["nticks"] = max(1, bg["ticks"] - ticks0)
    flush_to(0)
    return [out_key]


FH = 2816
NFC = 22


def emit_ffn(nc, st, P, NTOK, final, mixTin, xres, wo, wg, wu, wd, colc2, fnorm, xo, pfx="f", mix_all=None, sel=None,
             xres_keys=(), mix_key=None, out_key="xo_out", tile_hook=None, g2row=None):
    _n = [0]

    def sb(shape, dt, name=None):
        _n[0] += 1
        return st.enter_context(nc.sbuf_tensor("%s%s_%d" % (pfx, name or "t", _n[0]), shape, dt))

    def ps(shape, dt, name=None):
        _n[0] += 1
        return st.enter_context(nc.psum_tensor("%s%s_%d" % (pfx, name or "p", _n[0]), shape, dt))

    def A(out, in_, func, r, w, **kw):
        return P.op("act", lambda e: e.activation(out=out, in_=in_, func=func, **kw), reads=r, writes=w)

    def V(name, r, w, **kw):
        return P.op("dve", lambda e: getattr(e, name)(**kw), reads=r, writes=w)

    def G(name, r, w, **kw):
        return P.op("pool", lambda e: getattr(e, name)(**kw), reads=r, writes=w)

    def MM(out, lhsT, rhs, r, w, start=True, stop=True, inc=None):
        return P.op("pe", lambda e: e.matmul(out=out, lhsT=lhsT, rhs=rhs, start=start, stop=stop), reads=r, writes=w,
                    inc=stop if inc is None else inc)

    c2 = sb([128, 8], F32, "c2"); identf = sb([128, 128], F32, "identf"); identb = sb([128, 128], BF16, "identb")
    P.dma(c2[:], colc2[:, 0:8], writes=["c2"])
    P.dma(identf[:], colc2[:, 8:136], writes=["identf"])
    V("tensor_copy", ["identf"], ["identb"], out=identb[:], in_=identf[:])

    def TR(out, in_, r, w, inc=True):
        return P.op("pe", lambda e: e.transpose(out=out, in_=in_, identity=identb[:]), reads=list(r) + ["identb"], writes=w, inc=inc)

    Wob = sb([128, 8, 1024], BF16, "Wob"); Wgb = sb([128, 8, FH], BF16, "Wgb"); Wub = sb([128, 8, FH], BF16, "Wub")
    Wdb = sb([128, NFC, 1024], BF16, "Wdb")
    g2rep = sb([128, 1024], F32, "g2rep")
    P.dma(g2rep[:], g2row.partition_broadcast(128), writes=["g2rep"], q="sp")

    def load_cast(dst, src, n, scale_ap, wkey):
        P.dma(dst, src, writes=[wkey], q="pool")

    for kc in range(8):
        load_cast(Wob[:, kc, :], wo[kc * 128:(kc + 1) * 128, :], 1024, None, "Wob")

    def load_rest_of_weights():
        for bi, (c0, n) in enumerate(((0, 1024), (1024, 1024), (2048, 768))):
            for kc in range(8):
                load_cast(Wgb[:, kc, c0:c0 + n], wg[kc * 128:(kc + 1) * 128, c0:c0 + n], n, c2[:, kc:kc + 1], ("Wgb", bi))
                load_cast(Wub[:, kc, c0:c0 + n], wu[kc * 128:(kc + 1) * 128, c0:c0 + n], n, c2[:, kc:kc + 1], ("Wub", bi))
        for fc in range(NFC):
            load_cast(Wdb[:, fc, :], wd[fc * 128:(fc + 1) * 128, :], 1024, None, ("Wdb", fc))

    gen = Rot([(ps([128, 512], F32, "g%d" % i), "fpsg%d" % i) for i in range(6)])
    ptr = Rot([(ps([128, 8, 128], BF16, "t%d" % i), "fpst%d" % i) for i in range(2)])
    TT = 512
    NS = TT // 128
    HFC = NFC // 2
    mixin = sb([128, 8, TT], BF16, "mixin"); mik = "mixin"
    x1 = sb([128, NS, 1024], F32, "x1")
    h2b = sb([128, 1024], BF16, "h2b")
    h2T = sb([128, 8, TT], BF16, "h2T"); actT = sb([128, HFC, TT], BF16, "actT")
    st8 = sb([128, 8], F32, "st8")

    def rstd_from(ssap, dim, key):
        A(ssap, ssap, AF.Ln, [key], [key], scale=1.0 / dim, bias=1e-6)
        A(ssap, ssap, AF.Exp, [key], [key], scale=-0.5)

    if mix_all is None:
        mview = mixTin.rearrange("(c p) s -> p c s", p=128)
    else:
        NTH = NTOK // 512
        selt = sb([128, 2], F32, "selt")
        P.dma(selt[:], sel, writes=["selt"], q="sp")
        candB_t = sb([128, 8, 256], BF16, "candB")
        candB = candB_t[:]
        candBk = "candB"
    fn_t = None

    def load_mix(t):
        t0 = t * TT
        if mix_all is None:
            P.dma(mixin[:], mview[:, :, t0:t0 + TT], writes=[mik], q="sp")
        else:
            for hf in range(2):
                c0 = hf * 256
                dsts = ((mixin[:, :, c0:c0 + 256], mik), (candB, candBk))
                for h in range(2):
                    for q, (dst, dk) in enumerate(dsts):
                        T = q * NTH + t
                        P.dma(dst.rearrange("p (m h) s -> p m h s", h=2)[:, :, h, :],
                              mix_all[T, h].rearrange("(m p) s -> p m s", p=128)[:, :, c0:c0 + 256],
                              reads=[(mix_key, T)], writes=[dk], q="sp")
                V("tensor_scalar", [mik, "selt"], [mik], out=mixin[:, :, c0:c0 + 256], in0=mixin[:, :, c0:c0 + 256], scalar1=selt[:, 0:1],
                  scalar2=None, op0=ALU.mult)
                V("scalar_tensor_tensor", [candBk, "selt", mik], [mik], out=mixin[:, :, c0:c0 + 256], in0=candB, scalar=selt[:, 1:2],
                  in1=mixin[:, :, c0:c0 + 256], op0=ALU.mult, op1=ALU.add)

    def load_x(t):
        t0 = t * TT
        for s in range(NS):
            P.dma(x1[:, s, :], xres[t0 + s * 128:t0 + (s + 1) * 128, :], reads=[(k_, t) for k_ in xres_keys] + [("x1", s)],
                  writes=[("x1", s)], q="sp")

    NTILES = NTOK // TT
    load_mix(0)
    load_x(0)
    load_rest_of_weights()
    if final:
        fn_t = sb([128, 1024], F32, "fnrep"); fn_k = "fnrep"
        P.dma(fn_t[:], fnorm.partition_broadcast(128), writes=[fn_k], q="sp")
    for t in range(NTOK // TT):
        t0 = t * TT
        if t > 0:
            load_x(t)
        for s in range(NS):
            for n in range(2):
                g_t, g_k = gen.next()
                for kc in range(8):
                    MM(g_t[:, :], mixin[:, kc, s * 128:(s + 1) * 128], Wob[:, kc, n * 512:(n + 1) * 512], [mik, "Wob"], [g_k],
                       start=(kc == 0), stop=(kc == 7))
                V("tensor_tensor", [g_k, ("x1", s)], [("x1", s)], out=x1[:, s, n * 512:(n + 1) * 512], in0=x1[:, s, n * 512:(n + 1) * 512], in1=g_t[:, :], op=ALU.add)
            A(h2b[:], x1[:, s, :], AF.Square, [("x1", s)], ["h2b", "st8a"], accum_out=st8[:, 0:1])
            rstd_from(st8[:, 0:1], 1024.0, "st8a")
            V("scalar_tensor_tensor", [("x1", s), "st8a", "g2rep"], ["h2b"], out=h2b[:], in0=x1[:, s, :], scalar=st8[:, 0:1], in1=g2rep[:],
              op0=ALU.mult, op1=ALU.mult)
            p_t, p_k = ptr.next()
            for kc in range(8):
                TR(p_t[:, kc, :], h2b[:, kc * 128:(kc + 1) * 128], ["h2b"], [p_k], inc=(kc == 7))
            A(h2T[:, :, s * 128:(s + 1) * 128], p_t[:], AF.Copy, [p_k], ["h2T"])
        if t + 1 < NTILES:
            load_mix(t + 1)
        for fh in range(2):
            for fi in range(HFC):
                fc = fh * HFC + fi
                pg, pgk = gen.next()
                pu, puk = gen.next()
                for kc in range(8):
                    MM(pg[:, :], Wgb[:, kc, fc * 128:(fc + 1) * 128], h2T[:, kc, :], [("Wgb", fc // 8), "h2T"], [pgk], start=(kc == 0), stop=(kc == 7))
                for kc in range(8):
                    MM(pu[:, :], Wub[:, kc, fc * 128:(fc + 1) * 128], h2T[:, kc, :], [("Wub", fc // 8), "h2T"], [puk], start=(kc == 0), stop=(kc == 7))
                A(actT[:, fi, :], pg[:, :], AF.Silu, [pgk], [("actT", fi)])
                V("tensor_tensor", [("actT", fi), puk], [("actT", fi)], out=actT[:, fi, :], in0=actT[:, fi, :], in1=pu[:, :], op=ALU.mult)
            for s in range(NS):
                for n in range(2):
                    pd, pdk = gen.next()
                    for fi in range(HFC):
                        fc = fh * HFC + fi
                        MM(pd[:, :], actT[:, fi, s * 128:(s + 1) * 128], Wdb[:, fc, n * 512:(n + 1) * 512], [("actT", fi), ("Wdb", fc)], [pdk],
                           start=(fi == 0), stop=(fi == HFC - 1))
                    V("tensor_tensor", [pdk, ("x1", s)], [("x1", s)], out=x1[:, s, n * 512:(n + 1) * 512], in0=x1[:, s, n * 512:(n + 1) * 512], in1=pd[:, :], op=ALU.add)
        for s in range(NS):
            if final:
                A(h2b[:], x1[:, s, :], AF.Square, [("x1", s)], ["h2b", "st8b"], accum_out=st8[:, 1:2])
                rstd_from(st8[:, 1:2], 1024.0, "st8b")
                V("scalar_tensor_tensor", [("x1", s), "st8b", fn_k], [("x1", s)], out=x1[:, s, :], in0=x1[:, s, :], scalar=st8[:, 1:2], in1=fn_t[:],
                  op0=ALU.mult, op1=ALU.mult)
            P.dma(xo[t0 + s * 128:t0 + (s + 1) * 128, :], x1[:, s, :], reads=[("x1", s)], writes=[(out_key, t)], q="pool")
        if tile_hook is not None:
            tile_hook(t)
    return [(out_key, j) for j in range(NTOK // TT)]


def consts_cmat():
    ident = np.eye(128, dtype=np.float32)
    tri = np.triu(np.ones((128, 128), np.float32))
    su = np.tril(np.ones((128, 128), np.float32), -1)
    madd = np.where(np.arange(128)[:, None] <= np.arange(128)[None, :], 0.0, -30000.0).astype(np.float32)
    ones = np.ones((128, 128), np.float32)
    return np.ascontiguousarray(np.concatenate([ident, tri, su, madd, ones], axis=1))

def mixer_inputs(inp, l, hh, xb, posb):
    W = inp["w_in"][l]
    hA, hB = 2 * hh, 2 * hh + 1
    r = lambda a, n: np.arange(a, a + n)
    fq = lambda h: r(64 * h, 64); fk = lambda h: r(256 + 64 * h, 64); fv = lambda h: r(512 + 64 * h, 64); ff = lambda h: r(768 + h, 1)
    gq = lambda h: r(772 + 32 * h, 32); gk = lambda h: r(900 + 32 * h, 32); gv = lambda h: r(1028 + 64 * h, 64); gr = lambda h: r(1284 + 64 * h, 64)
    gate = r(1540, 16); mcq = r(1556, 256); mckv = r(1812, 128); mkr = r(1940, 32)
    sz = lambda h: r(1972 + 64 * h, 64); sx = lambda h: r(2228 + 64 * h, 64)
    sB = r(2484 + 128 * hh, 128); sC = r(2740 + 128 * hh, 128); sdt = lambda h: r(2996 + h, 1)
    Z = lambda n: np.zeros((1024, n), np.float32)
    cols = [W[:, fq(hA)], W[:, fq(hB)],
            W[:, fk(hA)], W[:, fk(hB)],
            W[:, ff(hA)], W[:, ff(hB)], Z(30), W[:, gate],
            W[:, sx(hA)], W[:, sx(hB)], W[:, sB], W[:, sC],
            W[:, gq(hA)], W[:, gq(hB)], Z(32), W[:, mkr],
            Z(96), W[:, mkr[16:32]], W[:, mkr[0:16]],
            W[:, fv(hA)], W[:, fv(hB)], W[:, gk(hA)], W[:, gk(hB)], W[:, gv(hA)], W[:, gv(hB)], W[:, gr(hA)], W[:, gr(hB)],
            W[:, mcq], W[:, mckv],
            W[:, sz(hA)], W[:, sz(hB)], W[:, sdt(hA)], W[:, sdt(hB)]]
    w_all = np.ascontiguousarray(np.concatenate(cols, axis=1))
    assert w_all.shape == (1024, 1906), w_all.shape
    colc = np.zeros((128, 28), np.float32)
    colc[:, 0:8] = inp["norm1"][l].reshape(8, 128).T
    colc[:, 8:10] = inp["mla_q_norm"][l].reshape(2, 128).T
    colc[:, 10] = inp["mla_kv_norm"][l]
    cwl = inp["ssm_conv_w"][l]; cbl = inp["ssm_conv_b"][l]
    ccols = [np.concatenate([r(64 * hA, 64), r(64 * hB, 64)]), r(256 + 128 * hh, 128), r(512 + 128 * hh, 128)]
    for g in range(3):
        colc[:, 11 + 4 * g:15 + 4 * g] = cwl[:, ccols[g]].T
        colc[:, 23 + g] = cbl[ccols[g]]
    half = 16
    inv = (10000.0 ** (-np.arange(half, dtype=np.float32) / half)).astype(np.float32)
    colc[96:112, 26] = -inv; colc[112:128, 26] = inv
    colc[0, 27] = inp["fox_f_bias"][l][hA]; colc[1, 27] = inp["fox_f_bias"][l][hB]
    rowc = np.zeros((1, 326), np.float32)
    rowc[0, 0:64] = inp["gla_gate_b"][l][np.concatenate([gq(hA), gq(hB)]) - 772]
    rowc[0, 64:128] = inp["gla_out_norm"][l]; rowc[0, 128:192] = inp["gla_out_norm"][l]
    rowc[0, 192:194] = inp["ssm_dt_bias"][l][[hA, hB]]
    rowc[0, 194:196] = inp["ssm_A_log"][l][[hA, hB]]
    rowc[0, 196:198] = inp["ssm_D"][l][[hA, hB]]
    rowc[0, 198:326] = inp["ssm_norm"][l][128 * hh:128 * hh + 128]
    w2 = np.ascontiguousarray(inp["gla_gate_w2"][l][:, np.concatenate([gq(hA), gq(hB)]) - 772])
    Wq = inp["mla_w_uq"][l]
    qc = []
    for h in (hA, hB):
        base = 96 * h
        z32 = np.zeros((256, 32), np.float32); z96 = np.zeros((256, 96), np.float32)
        qc += [Wq[:, base:base + 64], z32, Wq[:, base + 64:base + 96], z96, Wq[:, base + 80:base + 96], Wq[:, base + 64:base + 80]]
    wuq = np.ascontiguousarray(np.concatenate(qc, axis=1)); assert wuq.shape == (256, 512)
    Wkv = inp["mla_w_ukv"][l]
    wukv = np.ascontiguousarray(np.concatenate([Wkv[:, 128 * hA:128 * hA + 64], Wkv[:, 128 * hB:128 * hB + 64],
                                                Wkv[:, 128 * hA + 64:128 * hA + 128], Wkv[:, 128 * hB + 64:128 * hB + 128]], axis=1))
    d = dict(w_all=w_all, colc=colc, rowc=rowc, w2=w2, wuq=wuq, wukv=wukv)
    if xb is not None:
        d.update(xin=np.ascontiguousarray(xb), pos=np.ascontiguousarray(posb.reshape(1, -1).astype(np.int32)), cmat=consts_cmat())
    return d


_CACHE = {}
PAIRS = [[0, 1], [2, 3], [4, 5], [6, 7]]


def _build_fused_nc(NT):
    S = NT * 512
    H = S // 2
    nc = bass.Bass("TRN2", target_bir_lowering=False, num_devices=8)
    di = lambda name, shape, dt=F32: nc.dram_tensor(name, shape, dt, kind="ExternalInput").ap()
    xin = di("xin", [S, 1024]); xhalf = di("xhalf", [H, 1024]); pos = di("pos", [1, S], I32)
    cmat = di("cmat", [128, 640]); sel = di("sel", [128, 2]); fnorm = di("fnorm", [1, 1024])
    L = []
    for l in range(2):
        L.append(dict(w_all=di("w_all%d" % l, [1024, NW]), colc=di("colc%d" % l, [128, NCOL]), rowc=di("rowc%d" % l, [1, NROW]),
                      w2=di("w2%d" % l, [16, 64]), wuq=di("wuq%d" % l, [256, 512]), wukv=di("wukv%d" % l, [128, 256]),
                      wo=di("wo%d" % l, [1024, 1024]), wg=di("wg%d" % l, [1024, 2816]), wu=di("wu%d" % l, [1024, 2816]),
                      wd=di("wd%d" % l, [2816, 1024]), colc2=di("colc2%d" % l, [128, 136]), g2row=di("g2row%d" % l, [1, 1024])))
    xo = nc.dram_tensor("xo", [H, 1024], F32, kind="ExternalOutput").ap()
    NJ = H // 512
    mx_loc = [nc.dram_tensor("mx_loc%d" % l, [NT, 512, 512], BF16).ap() for l in range(2)]
    mx_all = [nc.dram_tensor("mx_all%d" % l, [NT, 2, 512, 512], BF16).ap() for l in range(2)]
    xn_loc = nc.dram_tensor("xn_loc", [H, 1024], F32).ap()
    xn_all = nc.dram_tensor("xn_all", [NJ, 2, 512, 1024], F32).ap()
    with contextlib.ExitStack() as st0:
        P = Prog(nc, st0)
        P.dma_queues = ["sp"]
        for l in range(2):
            w = L[l]

            def mix_hook(t, l=l):
                P.collective("AllGather", [mx_loc[l][t]], [mx_all[l][t].rearrange("r f s -> (r f) s")], reads=[("mx_loc%d" % l, t)],
                             writes=[("mx_all%d" % l, t)], groups=PAIRS)

            def x_hook(j):
                P.collective("AllGather", [xn_loc[j * 512:(j + 1) * 512, :]], [xn_all[j].rearrange("r t d -> (r t) d")],
                             reads=[("xn_loc", j)], writes=[("xn_all", j)], groups=PAIRS)

            def xin_fn(t, s):
                r, j = t // NJ, t % NJ
                return xn_all[j, r, s * 128:(s + 1) * 128, :], [("xn_all", j)]

            with contextlib.ExitStack() as st:
                emit_mixer(nc, st, P, NT, xin, pos, w["w_all"], w["colc"], w["rowc"], cmat, w["w2"], w["wuq"], w["wukv"],
                           mx_loc[l], pfx="m%d" % l, xin_fn=None if l == 0 else xin_fn, out_key="mx_loc%d" % l, tile_major=True,
                           tile_hook=mix_hook)
                P.emit()
            with contextlib.ExitStack() as st:
                fk = emit_ffn(nc, st, P, H, l == 1, None, xhalf if l == 0 else xn_loc, w["wo"], w["wg"], w["wu"], w["wd"], w["colc2"], fnorm,
                              xn_loc if l == 0 else xo, pfx="f%d" % l, mix_all=mx_all[l], sel=sel,
                              xres_keys=() if l == 0 else ("xn_loc",), mix_key="mx_all%d" % l, out_key="xn_loc" if l == 0 else "xo_out",
                              tile_hook=x_hook if l == 0 else None, g2row=w["g2row"])
                if l == 1:
                    P.finish(fk)
                P.emit()
    return nc


def kernel(**inputs):
    inp = {k: np.asarray(v) for k, v in inputs.items()}
    B, S = 4, 8192
    NT = S // 512
    x = np.ascontiguousarray(inp["x"], dtype=np.float32)
    pos = inp["positions"]
    if "fused" not in _CACHE:
        _CACHE["fused"] = _build_fused_nc(NT)
    cm = consts_cmat()
    fn = np.ascontiguousarray(inp["final_norm"].reshape(1, 1024))
    maps = []
    for c in range(8):
        b, hh = c // 2, c % 2
        m = dict(xin=x[b], xhalf=np.ascontiguousarray(x[b][hh * (S // 2):(hh + 1) * (S // 2)]),
                 pos=np.ascontiguousarray(pos[b].reshape(1, -1).astype(np.int32)), cmat=cm, fnorm=fn)
        sel = np.zeros((128, 2), np.float32); sel[:, hh] = 1.0
        m["sel"] = sel
        for l in range(2):
            mi = mixer_inputs(inp, l, hh, None, None)
            for k in ("w_all", "colc", "rowc", "w2", "wuq", "wukv"):
                m["%s%d" % (k, l)] = mi[k]
            colc2 = np.zeros((128, 136), np.float32)
            colc2[:, 0:8] = inp["norm2"][l].reshape(8, 128).T
            colc2[:, 8:136] = np.eye(128, dtype=np.float32)
            m["wo%d" % l] = inp["w_out"][l]; m["wg%d" % l] = inp["w_gate"][l]; m["wu%d" % l] = inp["w_up"][l]
            m["wd%d" % l] = inp["w_down"][l]; m["colc2%d" % l] = colc2
            m["g2row%d" % l] = np.ascontiguousarray(inp["norm2"][l].reshape(1, 1024))
        maps.append(m)
    res = run_bass_kernel_spmd(_CACHE["fused"], maps, core_ids=list(range(8)))
    out = np.empty((B, S, 1024), np.float32)
    for c in range(8):
        b, hh = c // 2, c % 2
        out[b, hh * (S // 2):(hh + 1) * (S // 2)] = res.results[c]["xo"]
    return out
```

```python
import contextlib, math
import numpy as np
import ml_dtypes
import concourse.bass as bass
import concourse.mybir as mybir
from concourse.bass_utils import run_bass_kernel_spmd


F32 = mybir.dt.float32
BF16 = mybir.dt.bfloat16
I32 = mybir.dt.int32
AF = mybir.ActivationFunctionType
ALU = mybir.AluOpType
AX = mybir.AxisListType


class Prog:
    COMPUTE = ("pe", "act", "dve", "pool")
    NDMASEM = 8

    def __init__(self, nc, stack, dma_queues=("sp", "pool")):
        self.nc = nc
        self._stack = stack
        self.eng = {"pe": nc.tensor, "act": nc.scalar, "dve": nc.vector,
                    "pool": nc.gpsimd, "sp": nc.sync}
        self.streams = {e: [] for e in self.eng}
        self.csem = {e: stack.enter_context(nc.semaphore("c_" + e)) for e in self.COMPUTE}
        self.ccount = {e: 0 for e in self.COMPUTE}
        self.dsem = {q: [stack.enter_context(nc.semaphore("d_%s%d" % (q, i)))
                         for i in range(self.NDMASEM)] for q in dma_queues}
        self.dcount = {q: 0 for q in dma_queues}
        self.waited = {e: {} for e in self.eng}
        self.semobj = {}
        self.last_w = {}
        self.readers = {}
        self.nwaits = 0
        self.nops = 0
        self.pending = {e: False for e in self.COMPUTE}
        self.dma_rr = 0
        self.dma_queues = list(dma_queues)

    def _sid(self, sem):
        i = id(sem)
        self.semobj[i] = sem
        return i

    def _deps(self, reads, writes):
        deps = []
        for k in reads:
            w = self.last_w.get(k)
            if w is not None:
                deps.append(w)
        for k in writes:
            w = self.last_w.get(k)
            if w is not None:
                deps.append(w)
            deps.extend(self.readers.get(k, ()))
        return deps

    def _emit_waits(self, e, deps, own=None):
        wl = []
        wd = self.waited[e]
        best = {}
        own_sid = id(self.csem[e]) if e in self.csem else None
        for (sid, val) in deps:
            if sid == own_sid and val > self.ccount[e]:
                continue
            if wd.get(sid, 0) >= val:
                continue
            if best.get(sid, 0) < val:
                best[sid] = val
        for sid, val in best.items():
            wd[sid] = val
            wl.append((self.semobj[sid], val))
        return wl

    def op(self, e, fn, reads=(), writes=(), inc=True):
        assert e in self.COMPUTE
        pr = [k for k in reads if isinstance(k, str) and (k.startswith("ps") or k.startswith("fps"))]
        if pr:
            writes = list(writes) + pr
        deps = self._deps(reads, writes)
        sem = self.csem[e]
        sid = self._sid(sem)
        wl = self._emit_waits(e, deps)
        if inc:
            self.ccount[e] += 1
            val = self.ccount[e]
        else:
            val = self.ccount[e] + 1
        self.pending[e] = not inc
        self.nwaits += len(wl)
        self.nops += 1

        def run(eng, wl=wl, fn=fn, sem=sem, inc=inc):
            for (s, v) in wl:
                eng.wait_ge(s, v)
            ins = fn(eng)
            if inc:
                ins.then_inc(sem, 1)
        self.streams[e].append(run)
        tok = (sid, val)
        for k in reads:
            self.readers.setdefault(k, []).append(tok)
        for k in writes:
            self.last_w[k] = tok
            self.readers[k] = []
        return tok

    def dma(self, out, in_, reads=(), writes=(), q=None, **kw):
        if q is None:
            q = self.dma_queues[self.dma_rr % len(self.dma_queues)]
            self.dma_rr += 1
        n = self.dcount[q]
        self.dcount[q] += 1
        sem = self.dsem[q][n % self.NDMASEM]
        sid = self._sid(sem)
        val = 16 * (n // self.NDMASEM + 1)
        deps = self._deps(reads, writes)
        if n >= self.NDMASEM:
            deps.append((sid, val - 16))
        wl = self._emit_waits(q, deps)
        self.nwaits += len(wl)
        self.nops += 1

        def run(eng, wl=wl, sem=sem, out=out, in_=in_, kw=kw):
            for (s, v) in wl:
                eng.wait_ge(s, v)
            eng.dma_start(out=out, in_=in_, **kw).then_inc(sem, 16)
        self.streams[q].append(run)
        tok = (sid, val)
        for k in reads:
            self.readers.setdefault(k, []).append(tok)
        for k in writes:
            self.last_w[k] = tok
            self.readers[k] = []
        return tok

    def collective(self, kind, ins, outs, reads=(), writes=(), groups=None, **kw):
        q = "pool"
        if not hasattr(self, "ccsem"):
            self.ccsem = self._stack.enter_context(self.nc.semaphore("cc_sem"))
            self.cccount = 0
        sem = self.ccsem
        sid = self._sid(sem)
        self.cccount += 1
        val = self.cccount
        deps = self._deps(reads, writes)
        if val > 1:
            deps.append((sid, val - 1))
        wl = self._emit_waits(q, deps)
        self.nwaits += len(wl)
        self.nops += 1
        op = ALU.bypass if kind in ("AllGather", "AllToAll") else ALU.add

        def run(eng, wl=wl, sem=sem):
            for (s, v) in wl:
                eng.wait_ge(s, v)
            eng.collective_compute(kind, op, replica_groups=groups, ins=[a.opt() for a in ins], outs=[a.opt() for a in outs], **kw).then_inc(sem, 1)
        self.streams[q].append(run)
        tok = (sid, val)
        for k in reads:
            self.readers.setdefault(k, []).append(tok)
        for k in writes:
            self.last_w[k] = tok
            self.readers[k] = []
        return tok

    def finish(self, final_keys):
        deps = []
        for k in final_keys:
            w = self.last_w.get(k)
            if w is not None:
                deps.append(w)
        wl = self._emit_waits("sp", deps)

        def run(eng, wl=wl):
            for (s, v) in wl:
                eng.wait_ge(s, v)
        self.streams["sp"].append(run)

    def emit(self):
        nc = self.nc
        assert not any(self.pending.values()), self.pending
        streams = self.streams
        self.streams = {e: [] for e in self.eng}
        with nc.Block() as block:
            @block.sync
            def _(eng):
                for r in streams["sp"]:
                    r(eng)

            @block.tensor
            def _(eng):
                for r in streams["pe"]:
                    r(eng)

            @block.scalar
            def _(eng):
                for r in streams["act"]:
                    r(eng)

            @block.vector
            def _(eng):
                for r in streams["dve"]:
                    r(eng)

            @block.gpsimd
            def _(eng):
                for r in streams["pool"]:
                    r(eng)


NFM = 944
NTM = 962
NW = NFM + NTM
FM = {"fq": (0, 128), "fk": (128, 128), "fg": (256, 48), "sx": (304, 128), "sB": (432, 128),
      "sC": (560, 128), "gq": (688, 128), "kr2": (816, 128)}
TM = {"t1": (944, 448), "t2": (1392, 384), "t3": (1776, 130)}
NCOL = 28
NROW = 326
TWO_PI = 2.0 * math.pi
C1 = 6.28125
C2 = TWO_PI - C1


class Rot:
    def __init__(self, items):
        self.items = items
        self.i = 0

    def next(self):
        it = self.items[self.i % len(self.items)]
        self.i += 1
        return it


def emit_mixer(nc, st, P, NT, xin, pos, w_all, colc, rowc, cmat, w2, wuq, wukv, mixT, pfx="", xin_fn=None, out_key="mixT_out",
               tile_major=False, tile_hook=None):
    S = NT * 512
    NCH = NT * 4
    _n = [0]

    def sb(shape, dt, name=None):
        _n[0] += 1
        return st.enter_context(nc.sbuf_tensor("%s%s_%d" % (pfx, name or "t", _n[0]), shape, dt))

    def ps(shape, dt, name=None):
        _n[0] += 1
        return st.enter_context(nc.psum_tensor("%s%s_%d" % (pfx, name or "p", _n[0]), shape, dt))

    bg = {"q": [], "stride": 1, "burst": 1, "cnt": 0, "busy": False, "open": False, "ticks": 0, "done": [], "side_done": {}}

    def pump(n):
        bg["busy"] = True
        for _ in range(n):
            if not bg["q"]:
                break
            try:
                next(bg["q"][0])
            except StopIteration:
                bg["q"].pop(0)
        bg["busy"] = False

    def flush_to(keep):
        while len(bg["q"]) > keep:
            n0 = len(bg["q"])
            while len(bg["q"]) == n0:
                pump(1000)
        if tile_major and tile_hook is not None:
            while bg["done"]:
                tile_hook(bg["done"].pop(0))

    def tick():
        if bg["busy"]:
            return
        bg["ticks"] += 1
        if not bg["q"] or bg["open"]:
            return
        bg["cnt"] += 1
        if bg["cnt"] % bg["stride"] == 0:
            pump(bg["burst"])


    def A(out, in_, func, r, w, **kw):
        tk = P.op("act", lambda e: e.activation(out=out, in_=in_, func=func, **kw), reads=r, writes=w)
        tick()
        return tk

    def V(name, r, w, **kw):
        tk = P.op("dve", lambda e: getattr(e, name)(**kw), reads=r, writes=w)
        tick()
        return tk

    def G(name, r, w, **kw):
        return P.op("pool", lambda e: getattr(e, name)(**kw), reads=r, writes=w)

    def MM(out, lhsT, rhs, r, w, start=True, stop=True, inc=None):
        inc = stop if inc is None else inc
        tk = P.op("pe", lambda e: e.matmul(out=out, lhsT=lhsT, rhs=rhs, start=start, stop=stop), reads=r, writes=w, inc=inc)
        if not bg["busy"]:
            bg["open"] = not inc
            if inc:
                tick()
        return tk

    colc_sb = sb([128, NCOL], F32, "colc"); rowc_sb = sb([128, NROW], F32, "rowc")
    cm = sb([128, 640], F32, "cm"); identb = sb([128, 128], BF16, "identb")
    P.dma(colc_sb[:], colc, writes=["colc"])
    P.dma(rowc_sb[:], rowc.partition_broadcast(128), writes=["rowc"])
    P.dma(cm[:], cmat, writes=["cm"])
    identf = cm[:, 0:128]; tri = cm[:, 128:256]; su = cm[:, 256:384]; madd = cm[:, 384:512]; ones = cm[:, 512:640]
    V("tensor_copy", ["cm"], ["identb"], out=identb[:], in_=identf)
    mask2 = sb([128, 2, 128], F32, "mask2")
    V("tensor_copy", ["cm"], ["mask2"], out=mask2[:, 0, :], in_=tri)
    V("tensor_copy", ["cm", "mask2"], ["mask2"], out=mask2[:, 1, :], in_=tri)

    def TR(out, in_, r, w, inc=True):
        tk = P.op("pe", lambda e: e.transpose(out=out, in_=in_, identity=identb[:]), reads=list(r) + ["identb"], writes=w, inc=inc)
        if not bg["busy"]:
            bg["open"] = not inc
            if inc:
                tick()
        return tk

    g1 = colc_sb[:, 0:8]; qn = colc_sb[:, 8:10]; kvn = colc_sb[:, 10:11]
    cw = colc_sb[:, 11:23]; cb = colc_sb[:, 23:26]; inv = colc_sb[:, 26:27]; fb = colc_sb[:, 27:28]
    b2r = rowc_sb[:, 0:64]; gn2 = rowc_sb[:, 64:192]; dtb = rowc_sb[:, 192:194]; alog = rowc_sb[:, 194:196]
    dsk = rowc_sb[:, 196:198]; snorm = rowc_sb[:, 198:326]

    small = sb([128, 16], F32, "small")
    negfb = small[:, 0:1]; arep = small[:, 2:4]
    V("tensor_scalar", ["colc"], ["negfb"], out=negfb, in0=fb, scalar1=-1.0, scalar2=None, op0=ALU.mult)
    A(arep, alog, AF.Exp, ["rowc"], ["arep"])
    V("tensor_scalar", ["arep"], ["arep"], out=arep, in0=arep, scalar1=-1.0, scalar2=None, op0=ALU.mult)

    xt = Rot([(sb([128, 1024], F32, "xt"), "xt%d" % i) for i in range(2)])
    Wb = sb([128, 8, NW], BF16, "Wb")
    HALF = NW // 2
    for kc in range(8):
        for hf in range(2):
            t_, k_ = xt.next()
            P.dma(t_[:, 0:HALF], w_all[kc * 128:(kc + 1) * 128, hf * HALF:(hf + 1) * HALF], writes=[k_])
            V("tensor_scalar", [k_, "colc"], ["Wb"], out=Wb[:, kc, hf * HALF:(hf + 1) * HALF], in0=t_[:, 0:HALF],
              scalar1=g1[:, kc:kc + 1], scalar2=None, op0=ALU.mult)
    w2sb = sb([48, 64], F32, "w2sb")
    P.dma(w2sb[32:48, :], w2, writes=["w2sb"])
    wuqb = sb([128, 2, 512], BF16, "wuqb")
    t_, k_ = xt.next()
    wuqf = t_[:, 0:1024].rearrange("p (c n) -> p c n", c=2)
    P.dma(wuqf, wuq.rearrange("(c p) n -> p c n", p=128), writes=[k_])
    for c in range(2):
        V("tensor_scalar", [k_, "colc"], ["wuqb"], out=wuqb[:, c, :], in0=wuqf[:, c, :], scalar1=qn[:, c:c + 1],
          scalar2=None, op0=ALU.mult)
    wukvb = sb([128, 256], BF16, "wukvb")
    t_, k_ = xt.next()
    P.dma(t_[:, 0:256], wukv, writes=[k_])
    V("tensor_scalar", [k_, "colc"], ["wukvb"], out=wukvb[:], in0=t_[:, 0:256], scalar1=kvn, scalar2=None, op0=ALU.mult)

    SA = max(S, 8192)
    KTf = [sb([128, SA], BF16, "KTf%d" % h) for h in range(2)]
    KTm = [sb([128, SA], BF16, "KTm%d" % h) for h in range(2)]

    def carve(tile, r0, nr, slot, dt=F32, n=512):
        return tile[r0:r0 + nr, slot * 1024:slot * 1024 + (n * (4 if dt != BF16 else 2)) // 2].bitcast(dt) if dt != BF16 \
            else tile[r0:r0 + nr, slot * 1024:slot * 1024 + n]

    VAf = sb([128, NCH + 1, 2, 65], BF16, "VAf"); VAm = sb([128, NCH + 1, 2, 65], BF16, "VAm")
    for h in range(2):
        G("memset", [], ["KTf%d_init" % h], ap=KTf[h][64:70, :], constant=1.0)
    G("memset", [], ["VAf_init"], ap=VAf[:], constant=1.0)
    G("memset", [], ["VAm_init"], ap=VAm[:], constant=1.0)
    QTf2 = [[sb([70, 512], BF16, "QTf%d_%d" % (p_, h)) for h in range(2)] for p_ in range(2)]
    QTm2 = [[sb([96, 512], BF16, "QTm%d_%d" % (p_, h)) for h in range(2)] for p_ in range(2)]
    for p_ in range(2):
        for h in range(2):
            G("memset", [], ["QTf%d_%d" % (p_, h)], ap=QTf2[p_][h][64:70, :], constant=1.0)

    gen = Rot([(ps([128, 512], F32, "g%d" % i), "psg%d" % i) for i in range(3)])
    sps = Rot([(ps([128, 512], F32, "s%d" % i), "pss%d" % i) for i in range(2)])
    acc = Rot([(ps([128, 512], F32, "a%d" % i), "psa%d" % i) for i in range(1)])
    ptr = Rot([(ps([128, 8, 128], BF16, "t%d" % i), "pst%d" % i) for i in range(2)])

    junk = sb([128, 384], BF16, "junk")
    hb = sb([128, 1024], BF16, "hb")
    hT = sb([128, 8, 512], BF16, "hT")
    st8 = sb([128, 8], F32, "st8")
    PT = Rot([(sb([128, 512], BF16, "PT"), "PT%d" % i) for i in range(3)])
    osb = sb([65, 512], F32, "osb"); rcp = osb
    stmp = sb([128, 512], F32, "stmp")

    def silu_to(out_ap, x_ap, tmp_ap, rkeys, wkey, tmpkey):
        A(tmp_ap, x_ap, AF.Exp, list(rkeys), [tmpkey], scale=-1.0)
        V("tensor_scalar", [tmpkey], [tmpkey], out=tmp_ap, in0=tmp_ap, scalar1=1.0, scalar2=None, op0=ALU.add)
        V("reciprocal", [tmpkey], [tmpkey], out=tmp_ap, in_=tmp_ap)
        V("tensor_tensor", list(rkeys) + [tmpkey], [wkey], out=out_ap, in0=x_ap, in1=tmp_ap, op=ALU.mult)

    mixt = Rot([(sb([128, 4, 512], BF16, "mixt"), "mixt%d" % i) for i in range(2)])
    fe = carve(KTf[0], 96, 2, 0); fsp = carve(KTf[0], 96, 2, 1)
    Fc = Rot([(carve(KTf[0], 96, 2, 2 + i), "Fc%d" % i) for i in range(2)])
    fr1 = carve(KTf[0], 96, 2, 4); fr2 = carve(KTf[0], 96, 2, 5)
    ones2 = carve(KTf[0], 96, 2, 6)
    Fzero = carve(KTf[0], 96, 2, 7)
    Fq = KTf[1][96:98, 0:1536].rearrange("p (j n) -> p j n", j=3)
    Fk = KTf[1][96:98, 1536:3072].rearrange("p (j n) -> p j n", j=3)
    G("memset", [], ["Fzero"], ap=Fzero, constant=0.0)
    G("memset", [], ["ones2"], ap=ones2, constant=1.0)
    cn = sb([128, 384], BF16, "cn"); cnT = sb([128, 3, 512], BF16, "cnT")
    posi = carve(KTm[0], 96, 32, 0, I32); rti = posi
    ang = carve(KTm[0], 96, 32, 1); rtmp = carve(KTm[0], 96, 32, 2)
    cs = carve(KTm[0], 96, 32, 3); sn = carve(KTm[0], 96, 32, 4)
    krA = carve(KTm[1], 96, 32, 0); krB = carve(KTm[1], 96, 32, 1)
    gateT = sb([48, 512], F32, "gateT"); gqT = sb([64, 512], F32, "gqT")
    gt = sb([128, 64], F32, "gt"); gsp = sb([128, 64], F32, "gsp")
    eGT = sb([64, 128], F32, "eGT"); enG = sb([128, 64], F32, "enG")
    qtl = sb([64, 128], BF16, "qtl"); ktl = sb([128, 64], BF16, "ktl"); ktlT = sb([64, 128], BF16, "ktlT")
    Am = sb([128, 2, 128], BF16, "Am"); vb = sb([128, 128], BF16, "vb")
    Sst = sb([64, 128], F32, "Sst"); Sbf = sb([64, 128], BF16, "Sbf")
    sr = sb([128, 128], F32, "sr"); gtmp = sb([128, 128], F32, "gtmp"); yb = sb([128, 128], BF16, "yb")
    G("memset", [], ["Sst"], ap=Sst[:], constant=0.0)
    G("memset", [], ["Sbf"], ap=Sbf[:], constant=0.0)
    cin = [sb([128, 515], F32, "cin%d" % g) for g in range(3)]
    for g in range(3):
        G("memset", [], ["cin%d" % g], ap=cin[g][:, 0:3], constant=0.0)
    cacc = sb([128, 512], F32, "cacc")
    fT = [sb([128, 512], BF16, "fT%d" % g) for g in range(3)]
    xs_tok = sb([128, 128], BF16, "xs_tok"); B_tok = sb([128, 128], BF16, "B_tok")
    dtr = sb([128, 8], F32, "dtr"); dt8 = sb([128, 8], F32, "dt8"); a8 = sb([128, 8], F32, "a8")
    ea8 = sb([128, 8], F32, "ea8"); eal8 = sb([128, 8], F32, "eal8"); ds8 = sb([128, 8], F32, "ds8"); wl8 = sb([128, 8], F32, "wl8")
    Lh = sb([128, 128], F32, "Lh"); Dexp = sb([128, 128], F32, "Dexp"); t1s = sb([128, 128], F32, "t1s")
    MT = sb([128, 2, 128], BF16, "MT"); Xd = sb([128, 128], BF16, "Xd")
    Hs = sb([128, 128], F32, "Hs"); Hbf = sb([128, 128], BF16, "Hbf")
    ytmp = sb([128, 128], F32, "ytmp"); szt = sb([128, 4, 128], F32, "szt"); yd = sb([128, 128], BF16, "yd")
    G("memset", [], ["Hs"], ap=Hs[:], constant=0.0)
    G("memset", [], ["Hbf"], ap=Hbf[:], constant=0.0)

    def rstd_from(ssap, n, dim, key):
        A(ssap, ssap, AF.Ln, [key], [key], scale=1.0 / dim, bias=1e-6)
        A(ssap, ssap, AF.Exp, [key], [key], scale=-0.5)

    def attention(QT, qkey, KT, kkey, VA, vkey, h, t, scale, nrows, out_ap, out_keys):
        nk = 4 * t + 4
        a_t, a_k = acc.next()
        pend = None

        def issue_qk(j):
            i = j - 4 * t
            c0 = 0 if i < 0 else i * 128
            s_t, s_k = sps.next()
            MM(s_t[:, c0:512], KT[0:nrows, j * 128:(j + 1) * 128], QT[0:nrows, c0:512],
               [kkey(j // 4), qkey], [s_k])
            return (j, c0, s_t, s_k)

        nxt = issue_qk(0)
        for j in range(nk):
            cur = nxt
            if j + 1 < nk:
                nxt = issue_qk(j + 1)
            (_, c0, s_t, s_k) = cur
            if j >= 4 * t:
                V("tensor_tensor", [s_k, "cm"], [s_k], out=s_t[:, c0:c0 + 128], in0=s_t[:, c0:c0 + 128], in1=madd, op=ALU.add)
            p_t, p_k = PT.next()
            A(p_t[:, c0:512], s_t[:, c0:512], AF.Exp, [s_k], [p_k], scale=scale)
            vflat = VA[:, j:j + 2, :, :].rearrange("p c h d -> p (c h d)")[:, h * 65:h * 65 + 128]
            MM(a_t[:, c0:512], vflat, p_t[:, c0:512], [vkey(j // 4), vkey(min((j + 1) // 4, t)), p_k], [a_k], start=(j == 0), stop=(j == nk - 1))
            yield
        V("reciprocal", [a_k, "osb"], ["rcp"], out=rcp[64:65, :], in_=a_t[64:65, :])
        A(osb[0:64, :], a_t[0:64, :], AF.Copy, [a_k], ["osb"])
        b_t, b_k = sps.next()
        MM(b_t[0:64, :], ones[64:65, 0:64], rcp[64:65, :], ["cm", "rcp"], [b_k])
        V("tensor_tensor", ["osb", b_k, "rcp"], out_keys + ["rcp"], out=out_ap, in0=osb[0:64, :], in1=b_t[0:64, :], op=ALU.mult)
        yield

    def tile_attention(t, par, mx, mxk):
        for h in range(2):
            yield from attention(QTf2[par][h], "QTf%d_%d" % (par, h), KTf[h], lambda tt, h=h: ("KTf", h, tt), VAf, lambda tt: ("VAf", tt), h, t,
                                 1.0, 70, mx[64 * h:64 * h + 64, 0, :], [mxk])
        for h in range(2):
            yield from attention(QTm2[par][h], "QTm%d_%d" % (par, h), KTm[h], lambda tt, h=h: ("KTm", h, tt), VAm, lambda tt: ("VAm", tt), h, t,
                                 sc_mla, 96, mx[64 * h:64 * h + 64, 2, :], [mxk])
        while not bg["side_done"].get(t):
            yield
        if tile_major:
            P.dma(mixT[t].rearrange("(m p) s -> p m s", p=128), mx[:], reads=[mxk], writes=[(out_key, t)], q="pool")
        else:
            P.dma(mixT.rearrange("(m p) s -> p m s", p=128)[:, :, t * 512:(t + 1) * 512], mx[:], reads=[mxk], writes=[out_key], q="sp")
        bg["done"].append(t)
        yield

    Fprev = (Fzero, "Fzero")
    isq_fox = 0.125
    sc_mla = 96.0 ** -0.5
    sc_gq = 32.0 ** -0.5

    for t in range(NT):
        c512 = slice(t * 512, (t + 1) * 512)
        mx, mxk = mixt.next()
        par = t % 2
        QTf = QTf2[par]; QTm = QTm2[par]
        ticks0 = bg["ticks"]
        for s in range(4):
            x_t, x_k = xt.next()
            r0 = t * 512 + s * 128
            if xin_fn is None:
                P.dma(x_t[:], xin[r0:r0 + 128, :], writes=[x_k], q="sp")
            else:
                xap, xkeys = xin_fn(t, s)
                P.dma(x_t[:], xap, reads=list(xkeys), writes=[x_k], q="sp")
            A(hb[:], x_t[:], AF.Square, [x_k], ["hb", "st8a"], accum_out=st8[:, 0:1])
            rstd_from(st8[:, 0:1], 1, 1024.0, "st8a")
            V("tensor_scalar", [x_k, "st8a"], ["hb"], out=hb[:], in0=x_t[:], scalar1=st8[:, 0:1], scalar2=None, op0=ALU.mult)
            p_t, p_k = ptr.next()
            for kc in range(8):
                TR(p_t[:, kc, :], hb[:, kc * 128:(kc + 1) * 128], ["hb"], [p_k], inc=(kc == 7))
            V("tensor_copy", [p_k], ["hT"], out=hT[:, :, s * 128:(s + 1) * 128], in_=p_t[:])

        def fm_group(name):
            off, M = FM[name]
            g_t, g_k = gen.next()
            for kc in range(8):
                MM(g_t[0:M, :], Wb[:, kc, off:off + M], hT[:, kc, :], ["Wb", "hT"], [g_k], start=(kc == 0), stop=(kc == 7))
            return g_t, g_k

        def tm_group(name, s):
            off, N = TM[name]
            g_t, g_k = gen.next()
            for kc in range(8):
                MM(g_t[:, 0:N], hT[:, kc, s * 128:(s + 1) * 128], Wb[:, kc, off:off + N], ["Wb", "hT"], [g_k], start=(kc == 0), stop=(kc == 7))
            return g_t, g_k

        g_t, g_k = fm_group("fq")
        for h in range(2):
            A(QTf[h][0:64, :], g_t[64 * h:64 * h + 64, :], AF.Copy, [g_k], ["QTf%d_%d" % (par, h)], scale=isq_fox)
        g_t, g_k = fm_group("fk")
        for h in range(2):
            V("tensor_copy", [g_k, "KTf%d_init" % h], [("KTf", h, t)], out=KTf[h][0:64, c512], in_=g_t[64 * h:64 * h + 64, :])
        g_t, g_k = fm_group("fg")
        A(fe, g_t[0:2, :], AF.Exp, [g_k, "negfb"], ["fe"], scale=-1.0, bias=negfb[0:2, :])
        A(gateT[32:48, :], g_t[32:48, :], AF.Copy, [g_k], ["gateT"])
        A(fsp, fe, AF.Ln, ["fe"], ["fsp"], bias=1.0)
        F_t, F_k = Fc.next()
        V("tensor_tensor_scan", ["ones2", "fsp", Fprev[1]], [F_k], out=F_t, data0=ones2, data1=fsp,
          initial=Fprev[0][:, 0:1] if Fprev[1] == "Fzero" else Fprev[0][:, 511:512], op0=ALU.mult, op1=ALU.subtract)
        Fprev = (F_t, F_k)
        V("tensor_copy", [F_k], ["Fq"], out=Fq[:, 0, :], in_=F_t)
        V("tensor_tensor", [F_k, "Fq"], ["fr1"], out=fr1, in0=F_t, in1=Fq[:, 0, :], op=ALU.subtract)
        V("tensor_copy", ["fr1", "Fq"], ["Fq"], out=Fq[:, 1, :], in_=fr1)
        V("tensor_tensor", ["fr1", "Fq"], ["fr2"], out=fr2, in0=fr1, in1=Fq[:, 1, :], op=ALU.subtract)
        V("tensor_copy", ["fr2", "Fq"], ["Fq"], out=Fq[:, 2, :], in_=fr2)
        V("tensor_scalar", ["Fq"], ["Fk"], out=Fk, in0=Fq, scalar1=-1.0, scalar2=None, op0=ALU.mult)
        for h in range(2):
            for j in range(3):
                P.dma(QTf[h][64 + j:65 + j, :], Fq[h:h + 1, j, :], reads=["Fq", "QTf%d_%d" % (par, h)], writes=["QTf%d_%d" % (par, h)], q="sp")
                P.dma(KTf[h][67 + j:68 + j, c512], Fk[h:h + 1, j, :], reads=["Fk", "KTf%d_init" % h, ("KTf", h, t)],
                      writes=[("KTf", h, t)], q="sp")
        for g, name in enumerate(("sx", "sB", "sC")):
            g_t, g_k = fm_group(name)
            ck = "cin%d" % g
            A(cin[g][:, 3:515], g_t[:, :], AF.Copy, [g_k, ck], [ck])

        def conv_silu(g):
            ck = "cin%d" % g
            V("tensor_scalar", [ck, "colc"], ["cacc"], out=cacc[:], in0=cin[g][:, 0:512], scalar1=cw[:, 4 * g:4 * g + 1],
              scalar2=cb[:, g:g + 1], op0=ALU.mult, op1=ALU.add)
            for k in range(1, 4):
                V("scalar_tensor_tensor", [ck, "colc", "cacc"], ["cacc"], out=cacc[:], in0=cin[g][:, k:k + 512],
                  scalar=cw[:, 4 * g + k:4 * g + k + 1], in1=cacc[:], op0=ALU.mult, op1=ALU.add)
            V("tensor_copy", [ck], [ck], out=cin[g][:, 0:3], in_=cin[g][:, 512:515])
            silu_to(fT[g][:], cacc[:], stmp[:], ["cacc"], "fT%d" % g, "stmp")

        R = slice(96, 128)
        RO = slice(64, 96)
        P.dma(posi, pos[0:1, c512].partition_broadcast(32), reads=["posi"], writes=["posi"], q="sp")
        V("tensor_copy", ["posi"], ["ang"], out=ang, in_=posi)
        V("tensor_scalar", ["ang", "colc"], ["ang"], out=ang, in0=ang, scalar1=inv[R, :], scalar2=None, op0=ALU.mult)
        for which, tab, tkey in ((0, sn, "sn"), (1, cs, "cs")):
            shift = 0.0 if which == 0 else math.pi / 2
            V("tensor_scalar", ["ang"], ["rtmp"], out=rtmp, in0=ang, scalar1=shift, scalar2=1.0 / TWO_PI, op0=ALU.add, op1=ALU.mult)
            V("tensor_copy", ["rtmp", "posi"], ["posi"], out=rti, in_=rtmp)
            V("tensor_copy", ["posi"], ["rtmp"], out=rtmp, in_=rti)
            V("scalar_tensor_tensor", ["rtmp", "ang"], [tkey], out=tab, in0=rtmp, scalar=-C1, in1=ang, op0=ALU.mult, op1=ALU.add)
            V("scalar_tensor_tensor", ["rtmp", tkey], [tkey], out=tab, in0=rtmp, scalar=-C2, in1=tab, op0=ALU.mult, op1=ALU.add)
            V("tensor_scalar", [tkey], [tkey], out=tab, in0=tab, scalar1=shift, scalar2=3.14159, op0=ALU.add, op1=ALU.min)
            V("tensor_scalar", [tkey], [tkey], out=tab, in0=tab, scalar1=-3.14159, scalar2=None, op0=ALU.max)
            A(tab, tab, AF.Sin, [tkey], [tkey])
        g_t, g_k = fm_group("gq")
        A(gqT[:], g_t[0:64, :], AF.Copy, [g_k], ["gqT"], scale=sc_gq)
        V("tensor_tensor", [g_k, "cs"], ["krA"], out=krA, in0=g_t[R, :], in1=cs, op=ALU.mult)
        g_t, g_k = fm_group("kr2")
        V("tensor_tensor", [g_k, "sn"], ["krB"], out=krB, in0=g_t[R, :], in1=sn, op=ALU.mult)
        for h in range(2):
            V("tensor_tensor", ["krA", "krB"], [("KTm", h, t)], out=KTm[h][RO, c512], in0=krA, in1=krB, op=ALU.add)

        for s in range(4):
            ch = 4 * t + s
            g_t, g_k = tm_group("t1", s)
            for h in range(2):
                V("tensor_copy", [g_k, "VAf_init"], [("VAf", t)], out=VAf[:, ch, h, 0:64], in_=g_t[:, 64 * h:64 * h + 64])
            cc = slice(s * 128, (s + 1) * 128)
            gp_t, gp_k = gen.next()
            MM(gp_t[:, 0:64], gateT[32:48, cc], w2sb[32:48, :], ["gateT", "w2sb"], [gp_k])
            V("tensor_tensor", [gp_k, "rowc"], ["gt"], out=gt[:], in0=gp_t[:, 0:64], in1=b2r, op=ALU.add)
            A(gt[:], gt[:], AF.Exp, ["gt"], ["gt"], scale=-1.0)
            A(gsp[:], gt[:], AF.Ln, ["gt"], ["gsp"], bias=1.0)
            MM(gp_t[:, 128:192], tri, gsp[:], ["cm", "gsp"], [gp_k])
            MM(gp_t[0:64, 256:384], gsp[:], tri, ["cm", "gsp"], [gp_k])
            A(eGT[:], gp_t[0:64, 256:384], AF.Exp, [gp_k], ["eGT"], scale=-1.0 / 16)
            A(enG[:], gp_t[:, 128:192], AF.Exp, [gp_k], ["enG"], scale=1.0 / 16)
            V("tensor_tensor", ["gqT", "eGT"], ["qtl"], out=qtl[:], in0=gqT[:, cc], in1=eGT[:], op=ALU.mult)
            V("tensor_tensor", [g_k, "enG"], ["ktl"], out=ktl[:], in0=g_t[:, 128:192], in1=enG[:], op=ALU.mult)
            A(vb[:], g_t[:, 192:320], AF.Copy, [g_k], ["vb"])
            silu_to(sr[:], g_t[:, 320:448], sr[:], [g_k], "sr", "sr")
            p_t, p_k = ptr.next()
            TR(p_t[0:64, 0, :], ktl[:], ["ktl"], [p_k])
            A(ktlT[:], p_t[0:64, 0, :], AF.Copy, [p_k], ["ktlT"])
            a_t, a_k = gen.next()
            for h in range(2):
                hs = slice(32 * h, 32 * h + 32)
                MM(a_t[:, 128 * h:128 * h + 128], ktlT[hs, :], qtl[hs, :], ["ktlT", "qtl"], [a_k])
            V("tensor_tensor", [a_k, "mask2"], ["Am"], out=Am[:], in0=a_t[:, 0:256].rearrange("p (h n) -> p h n", h=2), in1=mask2[:], op=ALU.mult)
            for h in range(2):
                hs = slice(32 * h, 32 * h + 32)
                vs = slice(64 * h, 64 * h + 64)
                MM(a_t[:, 256 + 64 * h:256 + 64 * h + 64], Am[:, h, :], vb[:, vs], ["Am", "vb"], [a_k], start=True, stop=False)
                MM(a_t[:, 256 + 64 * h:256 + 64 * h + 64], qtl[hs, :], Sbf[hs, vs], ["qtl", "Sbf"], [a_k], start=False, stop=True)
            MM(gp_t[0:64, 384:512], ktl[:], vb[:], ["ktl", "vb"], [gp_k])
            V("tensor_scalar", ["Sst", "eGT"], ["Sst"], out=Sst[:], in0=Sst[:], scalar1=eGT[:, 127:128], scalar2=None, op0=ALU.mult)
            V("scalar_tensor_tensor", [gp_k, "eGT", "Sst"], ["Sst"], out=Sst[:], in0=gp_t[0:64, 384:512], scalar=eGT[:, 127:128],
              in1=Sst[:], op0=ALU.mult, op1=ALU.add)
            V("tensor_copy", ["Sst"], ["Sbf"], out=Sbf[:], in_=Sst[:])
            for h in range(2):
                A(gtmp[:, 64 * h:64 * h + 64], a_t[:, 256 + 64 * h:256 + 64 * h + 64], AF.Square, [a_k], ["gtmp", "st8g"],
                  accum_out=st8[:, 2 + h:3 + h])
            rstd_from(st8[:, 2:4], 2, 64.0, "st8g")
            for h in range(2):
                V("scalar_tensor_tensor", [a_k, "st8g", "rowc"], ["gtmp"], out=gtmp[:, 64 * h:64 * h + 64],
                  in0=a_t[:, 256 + 64 * h:256 + 64 * h + 64], scalar=st8[:, 2 + h:3 + h], in1=gn2[:, 64 * h:64 * h + 64],
                  op0=ALU.mult, op1=ALU.mult)
            V("tensor_tensor", ["gtmp", "sr"], ["yb"], out=yb[:], in0=gtmp[:], in1=sr[:], op=ALU.mult)
            p_t, p_k = ptr.next()
            TR(p_t[:, 0, :], yb[:], ["yb"], [p_k])
            A(mx[:, 1, cc], p_t[:, 0, :], AF.Copy, [p_k], [mxk])

            if s < 3:
                conv_silu(s)

            g_t, g_k = tm_group("t2", s)
            A(junk[:, 0:256], g_t[:, 0:256], AF.Square, [g_k], ["junk", "st8m"], accum_out=st8[:, 4:5])
            A(junk[:, 256:384], g_t[:, 256:384], AF.Square, [g_k], ["junk", "st8m"], accum_out=st8[:, 5:6])
            A(st8[:, 4:5], st8[:, 4:5], AF.Ln, ["st8m"], ["st8m"], scale=1.0 / 256, bias=1e-6)
            A(st8[:, 5:6], st8[:, 5:6], AF.Ln, ["st8m"], ["st8m"], scale=1.0 / 128, bias=1e-6)
            A(st8[:, 4:6], st8[:, 4:6], AF.Exp, ["st8m"], ["st8m"], scale=-0.5)
            V("tensor_scalar", [g_k, "st8m"], ["cn"], out=cn[:, 0:256], in0=g_t[:, 0:256], scalar1=st8[:, 4:5], scalar2=None, op0=ALU.mult)
            V("tensor_scalar", [g_k, "st8m"], ["cn"], out=cn[:, 256:384], in0=g_t[:, 256:384], scalar1=st8[:, 5:6], scalar2=None, op0=ALU.mult)
            p_t, p_k = ptr.next()
            for c in range(3):
                TR(p_t[:, c, :], cn[:, c * 128:(c + 1) * 128], ["cn"], [p_k], inc=(c == 2))
            A(cnT[:, :, cc], p_t[:, 0:3, :], AF.Copy, [p_k], ["cnT"])
            v_t, v_k = gen.next()
            p2_t, p2_k = p_t, p_k
            MM(v_t[:, 0:128], cnT[:, 2, cc], wukvb[:, 128:256], ["cnT", "wukvb"], [v_k])
            for h in range(2):
                V("tensor_copy", [v_k, "VAm_init"], [("VAm", t)], out=VAm[:, ch, h, 0:64], in_=v_t[:, 64 * h:64 * h + 64])

            g_t, g_k = tm_group("t3", s)
            silu_to(szt[:, s, :], g_t[:, 0:128], szt[:, s, :], [g_k], "szt", "szt")
            V("tensor_copy", [g_k], ["dtr"], out=dtr[:, 2 * s:2 * s + 2], in_=g_t[:, 128:130])

        for h in range(2):
            qa_t, qa_k = gen.next()
            for c in range(2):
                MM(qa_t[:, :], wuqb[:, c, 256 * h:256 * h + 128], cnT[:, c, :], ["wuqb", "cnT"], [qa_k], start=(c == 0), stop=(c == 1))
            qb_t, qb_k = gen.next()
            for c in range(2):
                MM(qb_t[:, :], wuqb[:, c, 256 * h + 128:256 * h + 256], cnT[:, c, :], ["wuqb", "cnT"], [qb_k], start=(c == 0), stop=(c == 1))
            qk = "QTm%d_%d" % (par, h)
            A(QTm[h][0:64, :], qa_t[0:64, :], AF.Copy, [qa_k], [qk])
            V("tensor_tensor", [qa_k, "cs"], ["krA"], out=krA, in0=qa_t[R, :], in1=cs, op=ALU.mult)
            V("tensor_tensor", [qb_k, "sn"], ["krB"], out=krB, in0=qb_t[R, :], in1=sn, op=ALU.mult)
            V("tensor_tensor", ["krA", "krB"], [qk], out=QTm[h][RO, :], in0=krA, in1=krB, op=ALU.add)
        kn_t, kn_k = gen.next()
        MM(kn_t[:, :], wukvb[:, 0:128], cnT[:, 2, :], ["wukvb", "cnT"], [kn_k])
        for h in range(2):
            A(KTm[h][0:64, c512], kn_t[64 * h:64 * h + 64, :], AF.Copy, [kn_k], [("KTm", h, t)])

        units = 16 * (t + 1) + 9
        bg["q"].append(tile_attention(t, par, mx, mxk))
        bg["stride"] = max(1, bg.get("nticks", 900) // units)
        bg["burst"] = max(1, -(-units // bg.get("nticks", 900)))
        bg["cnt"] = 0

        V("tensor_tensor", ["dtr", "rowc"], ["dt8"], out=dt8[:].rearrange("p (s h) -> p s h", h=2),
          in0=dtr[:].rearrange("p (s h) -> p s h", h=2), in1=dtb.unsqueeze(1).to_broadcast([128, 4, 2]), op=ALU.add)
        A(dt8[:], dt8[:], AF.Exp, ["dt8"], ["dt8"])
        A(dt8[:], dt8[:], AF.Ln, ["dt8"], ["dt8"], bias=1.0)
        V("tensor_tensor", ["dt8", "arep"], ["a8"], out=a8[:].rearrange("p (s h) -> p s h", h=2),
          in0=dt8[:].rearrange("p (s h) -> p s h", h=2), in1=arep.unsqueeze(1).to_broadcast([128, 4, 2]), op=ALU.mult)
        sc_t, sc_k = gen.next()
        MM(sc_t[:, 0:8], tri, a8[:], ["cm", "a8"], [sc_k])
        MM(sc_t[:, 8:16], ones, a8[:], ["cm", "a8"], [sc_k])
        A(ea8[:], sc_t[:, 0:8], AF.Exp, [sc_k], ["ea8"])
        A(eal8[:], sc_t[:, 8:16], AF.Exp, [sc_k], ["eal8"])
        A(wl8[:], sc_t[:, 0:8], AF.Copy, [sc_k], ["wl8"])
        V("tensor_tensor", [sc_k, "wl8"], ["ds8"], out=ds8[:], in0=sc_t[:, 8:16], in1=wl8[:], op=ALU.subtract)
        A(ds8[:], ds8[:], AF.Exp, ["ds8"], ["ds8"])
        V("tensor_tensor", ["ds8", "dt8"], ["wl8"], out=wl8[:], in0=ds8[:], in1=dt8[:], op=ALU.mult)

        for s in range(4):
            cc = slice(s * 128, (s + 1) * 128)
            p_t, p_k = ptr.next()
            TR(p_t[:, 0, :], fT[0][:, cc], ["fT0"], [p_k])
            TR(p_t[:, 1, :], fT[1][:, cc], ["fT1"], [p_k])
            A(xs_tok[:], p_t[:, 0, :], AF.Copy, [p_k], ["xs_tok"])
            A(B_tok[:], p_t[:, 1, :], AF.Copy, [p_k], ["B_tok"])
            cb_t, cb_k = gen.next()
            MM(cb_t[:, 0:128], fT[1][:, cc], fT[2][:, cc], ["fT1", "fT2"], [cb_k])
            e_t, e_k = gen.next()
            for h in range(2):
                col = 2 * s + h
                V("tensor_scalar", ["cm", "a8"], ["Lh"], out=Lh[:], in0=su, scalar1=a8[:, col:col + 1], scalar2=None, op0=ALU.mult)
                MM(e_t[:, 128 * h:128 * h + 128], Lh[:], tri, ["Lh", "cm"], [e_k])
                A(Dexp[:], e_t[:, 128 * h:128 * h + 128], AF.Exp, [e_k], ["Dexp"])
                V("tensor_tensor", [cb_k, "Dexp"], ["t1s"], out=t1s[:], in0=cb_t[:, 0:128], in1=Dexp[:], op=ALU.mult)
                V("scalar_tensor_tensor", ["t1s", "dt8", "cm"], ["MT"], out=MT[:, h, :], in0=t1s[:], scalar=dt8[:, col:col + 1], in1=tri,
                  op0=ALU.mult, op1=ALU.mult)
                MM(e_t[:, 256 + 64 * h:256 + 64 * h + 64], MT[:, h, :], xs_tok[:, 64 * h:64 * h + 64], ["MT", "xs_tok"], [e_k])
            MM(e_t[:, 384:512], fT[2][:, cc], Hbf[:], ["fT2", "Hbf"], [e_k])
            for h in range(2):
                col = 2 * s + h
                V("tensor_scalar", ["xs_tok", "wl8"], ["Xd"], out=Xd[:, 64 * h:64 * h + 64], in0=xs_tok[:, 64 * h:64 * h + 64],
                  scalar1=wl8[:, col:col + 1], scalar2=None, op0=ALU.mult)
            MM(cb_t[:, 128:256], B_tok[:], Xd[:], ["B_tok", "Xd"], [cb_k])
            for h in range(2):
                col = 2 * s + h
                hs = slice(64 * h, 64 * h + 64)
                V("tensor_scalar", [e_k, "ea8"], ["ytmp"], out=ytmp[:, hs], in0=e_t[:, 384 + 64 * h:384 + 64 * h + 64],
                  scalar1=ea8[:, col:col + 1], scalar2=None, op0=ALU.mult)
                V("tensor_tensor", ["ytmp", e_k], ["ytmp"], out=ytmp[:, hs], in0=ytmp[:, hs], in1=e_t[:, 256 + 64 * h:256 + 64 * h + 64], op=ALU.add)
                V("scalar_tensor_tensor", ["xs_tok", "rowc", "ytmp"], ["ytmp"], out=ytmp[:, hs], in0=xs_tok[:, hs], scalar=dsk[:, h:h + 1],
                  in1=ytmp[:, hs], op0=ALU.mult, op1=ALU.add)
                V("scalar_tensor_tensor", ["Hs", "eal8", cb_k], ["Hs"], out=Hs[:, hs], in0=Hs[:, hs], scalar=eal8[:, col:col + 1],
                  in1=cb_t[:, 128 + 64 * h:128 + 64 * h + 64], op0=ALU.mult, op1=ALU.add)
            V("tensor_copy", ["Hs"], ["Hbf"], out=Hbf[:], in_=Hs[:])
            V("tensor_tensor", ["ytmp", "szt"], ["ytmp"], out=ytmp[:], in0=ytmp[:], in1=szt[:, s, :], op=ALU.mult)
            A(junk[:, 0:128], ytmp[:], AF.Square, ["ytmp"], ["junk", "st8s"], accum_out=st8[:, 6:7])
            rstd_from(st8[:, 6:7], 1, 128.0, "st8s")
            V("scalar_tensor_tensor", ["ytmp", "st8s", "rowc"], ["yd"], out=yd[:], in0=ytmp[:], scalar=st8[:, 6:7], in1=snorm,
              op0=ALU.mult, op1=ALU.mult)
            p_t, p_k = ptr.next()
            TR(p_t[:, 0, :], yd[:], ["yd"], [p_k])
            A(mx[:, 3, cc], p_t[:, 0, :], AF.Copy, [p_k], [mxk])

        bg["side_done"][t] = True
        flush_to(1)
        bg["nticks"] = max(1, bg["ticks"] - ticks0)
    flush_to(0)
    return [out_key]


FH = 2816
NFC = 22


def emit_ffn(nc, st, P, NTOK, final, mixTin, xres, wo, wg, wu, wd, colc2, fnorm, xo, pfx="f", mix_all=None, sel=None,
             xres_keys=(), mix_key=None, out_key="xo_out", tile_hook=None, g2row=None):
    _n = [0]

    def sb(shape, dt, name=None):
        _n[0] += 1
        return st.enter_context(nc.sbuf_tensor("%s%s_%d" % (pfx, name or "t", _n[0]), shape, dt))

    def ps(shape, dt, name=None):
        _n[0] += 1
        return st.enter_context(nc.psum_tensor("%s%s_%d" % (pfx, name or "p", _n[0]), shape, dt))

    def A(out, in_, func, r, w, **kw):
        return P.op("act", lambda e: e.activation(out=out, in_=in_, func=func, **kw), reads=r, writes=w)

    def V(name, r, w, **kw):
        return P.op("dve", lambda e: getattr(e, name)(**kw), reads=r, writes=w)

    def G(name, r, w, **kw):
        return P.op("pool", lambda e: getattr(e, name)(**kw), reads=r, writes=w)

    def MM(out, lhsT, rhs, r, w, start=True, stop=True, inc=None):
        return P.op("pe", lambda e: e.matmul(out=out, lhsT=lhsT, rhs=rhs, start=start, stop=stop), reads=r, writes=w,
                    inc=stop if inc is None else inc)

    c2 = sb([128, 8], F32, "c2"); identf = sb([128, 128], F32, "identf"); identb = sb([128, 128], BF16, "identb")
    P.dma(c2[:], colc2[:, 0:8], writes=["c2"])
    P.dma(identf[:], colc2[:, 8:136], writes=["identf"])
    V("tensor_copy", ["identf"], ["identb"], out=identb[:], in_=identf[:])

    def TR(out, in_, r, w, inc=True):
        return P.op("pe", lambda e: e.transpose(out=out, in_=in_, identity=identb[:]), reads=list(r) + ["identb"], writes=w, inc=inc)

    Wob = sb([128, 8, 1024], BF16, "Wob"); Wgb = sb([128, 8, FH], BF16, "Wgb"); Wub = sb([128, 8, FH], BF16, "Wub")
    Wdb = sb([128, NFC, 1024], BF16, "Wdb")
    g2rep = sb([128, 1024], F32, "g2rep")
    P.dma(g2rep[:], g2row.partition_broadcast(128), writes=["g2rep"], q="sp")

    def load_cast(dst, src, n, scale_ap, wkey):
        P.dma(dst, src, writes=[wkey], q="pool")

    for kc in range(8):
        load_cast(Wob[:, kc, :], wo[kc * 128:(kc + 1) * 128, :], 1024, None, "Wob")

    def load_rest_of_weights():
        for bi, (c0, n) in enumerate(((0, 1024), (1024, 1024), (2048, 768))):
            for kc in range(8):
                load_cast(Wgb[:, kc, c0:c0 + n], wg[kc * 128:(kc + 1) * 128, c0:c0 + n], n, c2[:, kc:kc + 1], ("Wgb", bi))
                load_cast(Wub[:, kc, c0:c0 + n], wu[kc * 128:(kc + 1) * 128, c0:c0 + n], n, c2[:, kc:kc + 1], ("Wub", bi))
        for fc in range(NFC):
            load_cast(Wdb[:, fc, :], wd[fc * 128:(fc + 1) * 128, :], 1024, None, ("Wdb", fc))

    gen = Rot([(ps([128, 512], F32, "g%d" % i), "fpsg%d" % i) for i in range(6)])
    ptr = Rot([(ps([128, 8, 128], BF16, "t%d" % i), "fpst%d" % i) for i in range(2)])
    TT = 512
    NS = TT // 128
    HFC = NFC // 2
    mixin = sb([128, 8, TT], BF16, "mixin"); mik = "mixin"
    x1 = sb([128, NS, 1024], F32, "x1")
    h2b = sb([128, 1024], BF16, "h2b")
    h2T = sb([128, 8, TT], BF16, "h2T"); actT = sb([128, HFC, TT], BF16, "actT")
    st8 = sb([128, 8], F32, "st8")

    def rstd_from(ssap, dim, key):
        A(ssap, ssap, AF.Ln, [key], [key], scale=1.0 / dim, bias=1e-6)
        A(ssap, ssap, AF.Exp, [key], [key], scale=-0.5)

    if mix_all is None:
        mview = mixTin.rearrange("(c p) s -> p c s", p=128)
    else:
        NTH = NTOK // 512
        selt = sb([128, 2], F32, "selt")
        P.dma(selt[:], sel, writes=["selt"], q="sp")
        candB_t = sb([128, 8, 256], BF16, "candB")
        candB = candB_t[:]
        candBk = "candB"
    fn_t = None

    def load_mix(t):
        t0 = t * TT
        if mix_all is None:
            P.dma(mixin[:], mview[:, :, t0:t0 + TT], writes=[mik], q="sp")
        else:
            for hf in range(2):
                c0 = hf * 256
                dsts = ((mixin[:, :, c0:c0 + 256], mik), (candB, candBk))
                for h in range(2):
                    for q, (dst, dk) in enumerate(dsts):
                        T = q * NTH + t
                        P.dma(dst.rearrange("p (m h) s -> p m h s", h=2)[:, :, h, :],
                              mix_all[T, h].rearrange("(m p) s -> p m s", p=128)[:, :, c0:c0 + 256],
                              reads=[(mix_key, T)], writes=[dk], q="sp")
                V("tensor_scalar", [mik, "selt"], [mik], out=mixin[:, :, c0:c0 + 256], in0=mixin[:, :, c0:c0 + 256], scalar1=selt[:, 0:1],
                  scalar2=None, op0=ALU.mult)
                V("scalar_tensor_tensor", [candBk, "selt", mik], [mik], out=mixin[:, :, c0:c0 + 256], in0=candB, scalar=selt[:, 1:2],
                  in1=mixin[:, :, c0:c0 + 256], op0=ALU.mult, op1=ALU.add)

    def load_x(t):
        t0 = t * TT
        for s in range(NS):
            P.dma(x1[:, s, :], xres[t0 + s * 128:t0 + (s + 1) * 128, :], reads=[(k_, t) for k_ in xres_keys] + [("x1", s)],
                  writes=[("x1", s)], q="sp")

    NTILES = NTOK // TT
    load_mix(0)
    load_x(0)
    load_rest_of_weights()
    if final:
        fn_t = sb([128, 1024], F32, "fnrep"); fn_k = "fnrep"
        P.dma(fn_t[:], fnorm.partition_broadcast(128), writes=[fn_k], q="sp")
    for t in range(NTOK // TT):
        t0 = t * TT
        if t > 0:
            load_x(t)
        def outproj(s):
            for n in range(2):
                g_t, g_k = gen.next()
                for kc in range(8):
                    MM(g_t[:, :], mixin[:, kc, s * 128:(s + 1) * 128], Wob[:, kc, n * 512:(n + 1) * 512], [mik, "Wob"], [g_k],
                       start=(kc == 0), stop=(kc == 7))
                V("tensor_tensor", [g_k, ("x1", s)], [("x1", s)], out=x1[:, s, n * 512:(n + 1) * 512], in0=x1[:, s, n * 512:(n + 1) * 512], in1=g_t[:, :], op=ALU.add)

        def norm_T(s):
            A(h2b[:], x1[:, s, :], AF.Square, [("x1", s)], ["h2b", "st8a"], accum_out=st8[:, 0:1])
            rstd_from(st8[:, 0:1], 1024.0, "st8a")
            V("scalar_tensor_tensor", [("x1", s), "st8a", "g2rep"], ["h2b"], out=h2b[:], in0=x1[:, s, :], scalar=st8[:, 0:1], in1=g2rep[:],
              op0=ALU.mult, op1=ALU.mult)
            p_t, p_k = ptr.next()
            for kc in range(8):
                TR(p_t[:, kc, :], h2b[:, kc * 128:(kc + 1) * 128], ["h2b"], [p_k], inc=(kc == 7))
            A(h2T[:, :, s * 128:(s + 1) * 128], p_t[:], AF.Copy, [p_k], ["h2T"])

        outproj(0)
        for s in range(NS):
            if s + 1 < NS:
                outproj(s + 1)
            norm_T(s)
        if t + 1 < NTILES:
            load_mix(t + 1)
        for fh in range(2):
            for fi in range(HFC):
                fc = fh * HFC + fi
                pg, pgk = gen.next()
                pu, puk = gen.next()
                for kc in range(8):
                    MM(pg[:, :], Wgb[:, kc, fc * 128:(fc + 1) * 128], h2T[:, kc, :], [("Wgb", fc // 8), "h2T"], [pgk], start=(kc == 0), stop=(kc == 7))
                for kc in range(8):
                    MM(pu[:, :], Wub[:, kc, fc * 128:(fc + 1) * 128], h2T[:, kc, :], [("Wub", fc // 8), "h2T"], [puk], start=(kc == 0), stop=(kc == 7))
                A(actT[:, fi, :], pg[:, :], AF.Silu, [pgk], [("actT", fi)])
                V("tensor_tensor", [("actT", fi), puk], [("actT", fi)], out=actT[:, fi, :], in0=actT[:, fi, :], in1=pu[:, :], op=ALU.mult)
            for s in range(NS):
                for n in range(2):
                    pd, pdk = gen.next()
                    for fi in range(HFC):
                        fc = fh * HFC + fi
                        MM(pd[:, :], actT[:, fi, s * 128:(s + 1) * 128], Wdb[:, fc, n * 512:(n + 1) * 512], [("actT", fi), ("Wdb", fc)], [pdk],
                           start=(fi == 0), stop=(fi == HFC - 1))
                    V("tensor_tensor", [pdk, ("x1", s)], [("x1", s)], out=x1[:, s, n * 512:(n + 1) * 512], in0=x1[:, s, n * 512:(n + 1) * 512], in1=pd[:, :], op=ALU.add)
        for s in range(NS):
            if final:
                A(h2b[:], x1[:, s, :], AF.Square, [("x1", s)], ["h2b", "st8b"], accum_out=st8[:, 1:2])
                rstd_from(st8[:, 1:2], 1024.0, "st8b")
                V("scalar_tensor_tensor", [("x1", s), "st8b", fn_k], [("x1", s)], out=x1[:, s, :], in0=x1[:, s, :], scalar=st8[:, 1:2], in1=fn_t[:],
                  op0=ALU.mult, op1=ALU.mult)
            P.dma(xo[t0 + s * 128:t0 + (s + 1) * 128, :], x1[:, s, :], reads=[("x1", s)], writes=[(out_key, t)], q="pool")
        if tile_hook is not None:
            tile_hook(t)
    return [(out_key, j) for j in range(NTOK // TT)]


def consts_cmat():
    ident = np.eye(128, dtype=np.float32)
    tri = np.triu(np.ones((128, 128), np.float32))
    su = np.tril(np.ones((128, 128), np.float32), -1)
    madd = np.where(np.arange(128)[:, None] <= np.arange(128)[None, :], 0.0, -30000.0).astype(np.float32)
    ones = np.ones((128, 128), np.float32)
    return np.ascontiguousarray(np.concatenate([ident, tri, su, madd, ones], axis=1))

def mixer_inputs(inp, l, hh, xb, posb):
    W = inp["w_in"][l]
    hA, hB = 2 * hh, 2 * hh + 1
    r = lambda a, n: np.arange(a, a + n)
    fq = lambda h: r(64 * h, 64); fk = lambda h: r(256 + 64 * h, 64); fv = lambda h: r(512 + 64 * h, 64); ff = lambda h: r(768 + h, 1)
    gq = lambda h: r(772 + 32 * h, 32); gk = lambda h: r(900 + 32 * h, 32); gv = lambda h: r(1028 + 64 * h, 64); gr = lambda h: r(1284 + 64 * h, 64)
    gate = r(1540, 16); mcq = r(1556, 256); mckv = r(1812, 128); mkr = r(1940, 32)
    sz = lambda h: r(1972 + 64 * h, 64); sx = lambda h: r(2228 + 64 * h, 64)
    sB = r(2484 + 128 * hh, 128); sC = r(2740 + 128 * hh, 128); sdt = lambda h: r(2996 + h, 1)
    Z = lambda n: np.zeros((1024, n), np.float32)
    cols = [W[:, fq(hA)], W[:, fq(hB)],
            W[:, fk(hA)], W[:, fk(hB)],
            W[:, ff(hA)], W[:, ff(hB)], Z(30), W[:, gate],
            W[:, sx(hA)], W[:, sx(hB)], W[:, sB], W[:, sC],
            W[:, gq(hA)], W[:, gq(hB)], Z(32), W[:, mkr],
            Z(96), W[:, mkr[16:32]], W[:, mkr[0:16]],
            W[:, fv(hA)], W[:, fv(hB)], W[:, gk(hA)], W[:, gk(hB)], W[:, gv(hA)], W[:, gv(hB)], W[:, gr(hA)], W[:, gr(hB)],
            W[:, mcq], W[:, mckv],
            W[:, sz(hA)], W[:, sz(hB)], W[:, sdt(hA)], W[:, sdt(hB)]]
    w_all = np.ascontiguousarray(np.concatenate(cols, axis=1))
    assert w_all.shape == (1024, 1906), w_all.shape
    colc = np.zeros((128, 28), np.float32)
    colc[:, 0:8] = inp["norm1"][l].reshape(8, 128).T
    colc[:, 8:10] = inp["mla_q_norm"][l].reshape(2, 128).T
    colc[:, 10] = inp["mla_kv_norm"][l]
    cwl = inp["ssm_conv_w"][l]; cbl = inp["ssm_conv_b"][l]
    ccols = [np.concatenate([r(64 * hA, 64), r(64 * hB, 64)]), r(256 + 128 * hh, 128), r(512 + 128 * hh, 128)]
    for g in range(3):
        colc[:, 11 + 4 * g:15 + 4 * g] = cwl[:, ccols[g]].T
        colc[:, 23 + g] = cbl[ccols[g]]
    half = 16
    inv = (10000.0 ** (-np.arange(half, dtype=np.float32) / half)).astype(np.float32)
    colc[96:112, 26] = -inv; colc[112:128, 26] = inv
    colc[0, 27] = inp["fox_f_bias"][l][hA]; colc[1, 27] = inp["fox_f_bias"][l][hB]
    rowc = np.zeros((1, 326), np.float32)
    rowc[0, 0:64] = inp["gla_gate_b"][l][np.concatenate([gq(hA), gq(hB)]) - 772]
    rowc[0, 64:128] = inp["gla_out_norm"][l]; rowc[0, 128:192] = inp["gla_out_norm"][l]
    rowc[0, 192:194] = inp["ssm_dt_bias"][l][[hA, hB]]
    rowc[0, 194:196] = inp["ssm_A_log"][l][[hA, hB]]
    rowc[0, 196:198] = inp["ssm_D"][l][[hA, hB]]
    rowc[0, 198:326] = inp["ssm_norm"][l][128 * hh:128 * hh + 128]
    w2 = np.ascontiguousarray(inp["gla_gate_w2"][l][:, np.concatenate([gq(hA), gq(hB)]) - 772])
    Wq = inp["mla_w_uq"][l]
    qc = []
    for h in (hA, hB):
        base = 96 * h
        z32 = np.zeros((256, 32), np.float32); z96 = np.zeros((256, 96), np.float32)
        qc += [Wq[:, base:base + 64], z32, Wq[:, base + 64:base + 96], z96, Wq[:, base + 80:base + 96], Wq[:, base + 64:base + 80]]
    wuq = np.ascontiguousarray(np.concatenate(qc, axis=1)); assert wuq.shape == (256, 512)
    Wkv = inp["mla_w_ukv"][l]
    wukv = np.ascontiguousarray(np.concatenate([Wkv[:, 128 * hA:128 * hA + 64], Wkv[:, 128 * hB:128 * hB + 64],
                                                Wkv[:, 128 * hA + 64:128 * hA + 128], Wkv[:, 128 * hB + 64:128 * hB + 128]], axis=1))
    d = dict(w_all=w_all, colc=colc, rowc=rowc, w2=w2, wuq=wuq, wukv=wukv)
    if xb is not None:
        d.update(xin=np.ascontiguousarray(xb), pos=np.ascontiguousarray(posb.reshape(1, -1).astype(np.int32)), cmat=consts_cmat())
    return d


_CACHE = {}
PAIRS = [[0, 1], [2, 3], [4, 5], [6, 7]]


def _build_fused_nc(NT):
    S = NT * 512
    H = S // 2
    nc = bass.Bass("TRN2", target_bir_lowering=False, num_devices=8)
    di = lambda name, shape, dt=F32: nc.dram_tensor(name, shape, dt, kind="ExternalInput").ap()
    xin = di("xin", [S, 1024]); xhalf = di("xhalf", [H, 1024]); pos = di("pos", [1, S], I32)
    cmat = di("cmat", [128, 640]); sel = di("sel", [128, 2]); fnorm = di("fnorm", [1, 1024])
    L = []
    for l in range(2):
        L.append(dict(w_all=di("w_all%d" % l, [1024, NW]), colc=di("colc%d" % l, [128, NCOL]), rowc=di("rowc%d" % l, [1, NROW]),
                      w2=di("w2%d" % l, [16, 64]), wuq=di("wuq%d" % l, [256, 512]), wukv=di("wukv%d" % l, [128, 256]),
                      wo=di("wo%d" % l, [1024, 1024]), wg=di("wg%d" % l, [1024, 2816]), wu=di("wu%d" % l, [1024, 2816]),
                      wd=di("wd%d" % l, [2816, 1024]), colc2=di("colc2%d" % l, [128, 136]), g2row=di("g2row%d" % l, [1, 1024])))
    xo = nc.dram_tensor("xo", [H, 1024], F32, kind="ExternalOutput").ap()
    NJ = H // 512
    mx_loc = [nc.dram_tensor("mx_loc%d" % l, [NT, 512, 512], BF16).ap() for l in range(2)]
    mx_all = [nc.dram_tensor("mx_all%d" % l, [NT, 2, 512, 512], BF16).ap() for l in range(2)]
    xn_loc = nc.dram_tensor("xn_loc", [H, 1024], F32).ap()
    xn_all = nc.dram_tensor("xn_all", [NJ, 2, 512, 1024], F32).ap()
    with contextlib.ExitStack() as st0:
        P = Prog(nc, st0)
        P.dma_queues = ["sp"]
        for l in range(2):
            w = L[l]

            def mix_hook(t, l=l):
                P.collective("AllGather", [mx_loc[l][t]], [mx_all[l][t].rearrange("r f s -> (r f) s")], reads=[("mx_loc%d" % l, t)],
                             writes=[("mx_all%d" % l, t)], groups=PAIRS)

            def x_hook(j):
                P.collective("AllGather", [xn_loc[j * 512:(j + 1) * 512, :]], [xn_all[j].rearrange("r t d -> (r t) d")],
                             reads=[("xn_loc", j)], writes=[("xn_all", j)], groups=PAIRS)

            def xin_fn(t, s):
                r, j = t // NJ, t % NJ
                return xn_all[j, r, s * 128:(s + 1) * 128, :], [("xn_all", j)]

            with contextlib.ExitStack() as st:
                emit_mixer(nc, st, P, NT, xin, pos, w["w_all"], w["colc"], w["rowc"], cmat, w["w2"], w["wuq"], w["wukv"],
                           mx_loc[l], pfx="m%d" % l, xin_fn=None if l == 0 else xin_fn, out_key="mx_loc%d" % l, tile_major=True,
                           tile_hook=mix_hook)
                P.emit()
            with contextlib.ExitStack() as st:
                fk = emit_ffn(nc, st, P, H, l == 1, None, xhalf if l == 0 else xn_loc, w["wo"], w["wg"], w["wu"], w["wd"], w["colc2"], fnorm,
                              xn_loc if l == 0 else xo, pfx="f%d" % l, mix_all=mx_all[l], sel=sel,
                              xres_keys=() if l == 0 else ("xn_loc",), mix_key="mx_all%d" % l, out_key="xn_loc" if l == 0 else "xo_out",
                              tile_hook=x_hook if l == 0 else None, g2row=w["g2row"])
                if l == 1:
                    P.finish(fk)
                P.emit()
    return nc


def kernel(**inputs):
    inp = {k: np.asarray(v) for k, v in inputs.items()}
    B, S = 4, 8192
    NT = S // 512
    x = np.ascontiguousarray(inp["x"], dtype=np.float32)
    pos = inp["positions"]
    if "fused" not in _CACHE:
        _CACHE["fused"] = _build_fused_nc(NT)
    cm = consts_cmat()
    fn = np.ascontiguousarray(inp["final_norm"].reshape(1, 1024))
    maps = []
    for c in range(8):
        b, hh = c // 2, c % 2
        m = dict(xin=x[b], xhalf=np.ascontiguousarray(x[b][hh * (S // 2):(hh + 1) * (S // 2)]),
                 pos=np.ascontiguousarray(pos[b].reshape(1, -1).astype(np.int32)), cmat=cm, fnorm=fn)
        sel = np.zeros((128, 2), np.float32); sel[:, hh] = 1.0
        m["sel"] = sel
        for l in range(2):
            mi = mixer_inputs(inp, l, hh, None, None)
            for k in ("w_all", "colc", "rowc", "w2", "wuq", "wukv"):
                m["%s%d" % (k, l)] = mi[k]
            colc2 = np.zeros((128, 136), np.float32)
            colc2[:, 0:8] = inp["norm2"][l].reshape(8, 128).T
            colc2[:, 8:136] = np.eye(128, dtype=np.float32)
            m["wo%d" % l] = inp["w_out"][l]; m["wg%d" % l] = inp["w_gate"][l]; m["wu%d" % l] = inp["w_up"][l]
            m["wd%d" % l] = inp["w_down"][l]; m["colc2%d" % l] = colc2
            m["g2row%d" % l] = np.ascontiguousarray(inp["norm2"][l].reshape(1, 1024))
        maps.append(m)
    res = run_bass_kernel_spmd(_CACHE["fused"], maps, core_ids=list(range(8)))
    out = np.empty((B, S, 1024), np.float32)
    for c in range(8):
        b, hh = c // 2, c % 2
        out[b, hh * (S // 2):(hh + 1) * (S // 2)] = res.results[c]["xo"]
    return out
```
